# Optimizing a Trainium2 kernel written in Bass

```python
import math
import jax
import jax.numpy as jnp
from jax import lax
import numpy as np

D_MODEL = 1024
BATCH = 4
SEQ = 4096
DEPTH = 1
DEC_BATCH = 32
DEC_SEQ = 1
PAST_LEN = 16384
PAGE_SIZE = 128

D_MIX = D_MODEL
HEAD_DIM = 64
D_NSA = D_MIX // 2
N_HEADS = D_NSA // HEAD_DIM
N_KV = N_HEADS // 4
HPG = N_HEADS // N_KV
CMP_BLOCK = 32
CMP_STRIDE = 16
CMP_HIDDEN = HEAD_DIM
SEL_BLOCK = 64
TOP_N = 16
WINDOW = 512
Q_BLOCK = 128
D_RET = D_MIX - D_NSA
N_RET_HEADS = 4
DV_RET = D_RET // N_RET_HEADS
DK_RET = DV_RET // 2
RET_CHUNK = 128
ROPE_BASE = 10000.0
EPS = 1e-6
NEG = -1e30
FORCE = 1e4
KV_W = N_KV * HEAD_DIM
COLS = (D_NSA, KV_W, KV_W, KV_W, KV_W, KV_W, KV_W, 3 * N_HEADS, D_NSA,
        N_RET_HEADS * DK_RET, N_RET_HEADS * DK_RET, D_RET, D_RET)
D_IN = sum(COLS)
SPLITS = tuple(int(v) for v in np.cumsum(COLS)[:-1])

kernel_name = 'nsa_retention_hybrid_step'


def rmsnorm(x, g):
    xf = x.astype(jnp.float32)
    y = xf * lax.rsqrt(jnp.mean(xf * xf, axis=-1, keepdims=True) + EPS)
    return (y * g.astype(jnp.float32)).astype(x.dtype)


def masked_softmax(s, mask):
    p = jax.nn.softmax(jnp.where(mask, s, NEG), axis=-1)
    return jnp.where(mask, p, 0.0)


def rotary(x, pos):
    xf = x.astype(jnp.float32)
    half = xf.shape[-1] // 2
    freqs = ROPE_BASE ** (-jnp.arange(half, dtype=jnp.float32) / half)
    ang = pos.astype(jnp.float32)[:, None] * freqs[None, :]
    cos = jnp.cos(ang)[None, :, None, :]
    sin = jnp.sin(ang)[None, :, None, :]
    x1, x2 = xf[..., :half], xf[..., half:]
    return jnp.concatenate([x1 * cos - x2 * sin, x1 * sin + x2 * cos], axis=-1)


def pad_rows(x, mult):
    extra = (-x.shape[1]) % mult
    return jnp.pad(x, ((0, 0), (0, extra)) + ((0, 0),) * (x.ndim - 2))


def gather_pages(cache, page_table):
    rows = cache[page_table]
    return rows.reshape(page_table.shape[0], -1, *cache.shape[2:])


def compress(x, pe, w1, w2):
    b, l = x.shape[:2]
    n_chunk = l // CMP_STRIDE
    span = CMP_BLOCK // CMP_STRIDE
    n_cmp = n_chunk - span + 1
    ch = x.reshape(b, n_chunk, CMP_STRIDE, N_KV, HEAD_DIM)
    blocks = jnp.concatenate([ch[:, o:o + n_cmp] for o in range(span)], axis=2)
    h = jax.nn.silu(jnp.einsum('bnlgd,ldf->bngf', blocks + pe[:, None, :], w1))
    return jnp.einsum('bngf,fd->bngd', h, w2)


def nsa_block(q, q_pos, kc, vc, ks_blk, vs_blk, kw, vw, kw_pos, gates):
    f32 = jnp.float32
    scale = HEAD_DIM ** -0.5
    q = q.astype(f32)
    b, nq = q.shape[:2]
    n_cmp, n_sel = kc.shape[1], ks_blk.shape[1]
    t = q_pos[:, None]
    c_end = jnp.arange(n_cmp, dtype=jnp.int32) * CMP_STRIDE + CMP_BLOCK - 1
    c_mask = (c_end[None, :] <= t)[None, :, None, None, :]
    s_c = jnp.einsum('bqghd,bngd->bqghn', q, kc.astype(f32)) * scale
    p_c = masked_softmax(s_c, c_mask)
    o_c = jnp.einsum('bqghn,bngd->bqghd', p_c, vc.astype(f32))
    ratio = SEL_BLOCK // CMP_STRIDE
    span = CMP_BLOCK // CMP_STRIDE
    imp = jnp.pad(p_c.sum(axis=3), ((0, 0), (0, 0), (0, 0), (span - 1, ratio * n_sel - n_cmp)))
    imp_t = imp[..., span - 1:span - 1 + ratio * n_sel]
    for n in range(1, span):
        imp_t = imp_t + imp[..., span - 1 - n:span - 1 - n + ratio * n_sel]
    imp_s = imp_t.reshape(b, nq, N_KV, n_sel, ratio).sum(axis=-1)
    blk = jnp.arange(n_sel, dtype=jnp.int32)[None, :]
    jt = (q_pos // SEL_BLOCK)[:, None]
    s_valid = blk * SEL_BLOCK <= t
    forced = (blk == 0) | (blk == jt) | (blk == jt - 1)
    score = jnp.where(s_valid[None, :, None, :], jnp.where(forced[None, :, None, :], FORCE, imp_s), NEG)
    k_eff = min(TOP_N, n_sel)
    _, idx = lax.top_k(score, k_eff)
    b_i = jnp.arange(b)[:, None, None, None]
    g_i = jnp.arange(N_KV)[None, None, :, None]
    ks_sel = ks_blk.transpose(0, 3, 1, 2, 4)[b_i, g_i, idx]
    vs_sel = vs_blk.transpose(0, 3, 1, 2, 4)[b_i, g_i, idx]
    kpos = idx[..., None] * SEL_BLOCK + jnp.arange(SEL_BLOCK, dtype=jnp.int32)
    s_mask = (kpos <= q_pos[None, :, None, None, None]).reshape(b, nq, N_KV, 1, k_eff * SEL_BLOCK)
    s_s = jnp.einsum('bqghd,bqgksd->bqghks', q, ks_sel.astype(f32)) * scale
    s_s = s_s.reshape(b, nq, N_KV, HPG, k_eff * SEL_BLOCK)
    p_s = masked_softmax(s_s, s_mask).reshape(b, nq, N_KV, HPG, k_eff, SEL_BLOCK)
    o_s = jnp.einsum('bqghks,bqgksd->bqghd', p_s, vs_sel.astype(f32))
    kp = kw_pos[None, :]
    w_mask = ((kp <= t) & (kp > t - WINDOW) & (kp >= 0))[None, :, None, None, :]
    s_w = jnp.einsum('bqghd,bwgd->bqghw', q, kw.astype(f32)) * scale
    p_w = masked_softmax(s_w, w_mask)
    o_w = jnp.einsum('bqghw,bwgd->bqghd', p_w, vw.astype(f32))
    g = gates.astype(f32)
    return g[..., 0:1] * o_c + g[..., 1:2] * o_s + g[..., 2:3] * o_w


def retention(q, k, v, s0):
    b, l = q.shape[:2]
    cl = math.gcd(l, RET_CHUNK)
    n = l // cl
    log_g = jnp.log(1.0 - 2.0 ** (-5.0 - jnp.arange(N_RET_HEADS, dtype=jnp.float32)))
    i = jnp.arange(cl, dtype=jnp.float32)
    diff = i[:, None] - i[None, :]
    d_in = jnp.where(diff >= 0, jnp.exp(log_g[:, None, None] * jnp.maximum(diff, 0.0)), 0.0)
    xi = jnp.exp(log_g[None, :] * (i[:, None] + 1.0))
    zeta = jnp.exp(log_g[None, :] * (cl - 1.0 - i[:, None]))
    g_c = jnp.exp(log_g * cl)

    def chunks(a):
        return a.reshape(b, n, cl, *a.shape[2:]).swapaxes(0, 1)

    def step(s, inp):
        qc, kc, vc = inp
        a = jnp.einsum('bihd,bjhd->bhij', qc, kc) * d_in
        o = jnp.einsum('bhij,bjhe->bihe', a, vc) + jnp.einsum('bihd,bhde->bihe', qc, s) * xi[None, :, :, None]
        s = s * g_c[None, :, None, None] + jnp.einsum('bjhd,bjhe->bhde', kc * zeta[None, :, :, None], vc)
        return s, o

    s, o = lax.scan(step, s0, (chunks(q), chunks(k), chunks(v)))
    return o.swapaxes(0, 1).reshape(b, l, N_RET_HEADS, DV_RET), s


def mix_inputs(x, c, g_norm, w_ada, b_ada, w_in, g_q, g_ks, g_kw):
    b, l = x.shape[:2]
    shift, scale, gate = jnp.split(jax.nn.silu(c) @ w_ada + b_ada, 3, axis=-1)
    h = rmsnorm(x, g_norm) * (1.0 + scale[:, None, :]) + shift[:, None, :]
    q, kc, vc, ks, vs, kw, vw, br, g_a, rq, rk, rv, g_r = jnp.split(h @ w_in, SPLITS, axis=-1)

    def kv(a):
        return a.reshape(b, l, N_KV, HEAD_DIM)

    q = rmsnorm(q.reshape(b, l, N_HEADS, HEAD_DIM), g_q).reshape(b, l, N_KV, HPG, HEAD_DIM)
    br = jax.nn.sigmoid(br).reshape(b, l, N_KV, HPG, 3)
    return (gate, q, kv(kc), kv(vc), rmsnorm(kv(ks), g_ks), kv(vs), rmsnorm(kv(kw), g_kw), kv(vw), br, g_a,
            rq.reshape(b, l, N_RET_HEADS, DK_RET), rk.reshape(b, l, N_RET_HEADS, DK_RET),
            rv.reshape(b, l, N_RET_HEADS, DV_RET), g_r)


def mix_outputs(x, gate, o_nsa, g_a, o_ret, g_r, g_ret, w_out):
    b, l = x.shape[:2]
    y_a = o_nsa.reshape(b, l, D_NSA).astype(x.dtype) * jax.nn.silu(g_a)
    y_r = rmsnorm(o_ret, g_ret).reshape(b, l, D_RET).astype(x.dtype) * jax.nn.silu(g_r)
    return x + gate[:, None, :] * (jnp.concatenate([y_a, y_r], axis=-1) @ w_out)


def prompt_layer(x, c, W):
    (g_norm, w_ada, b_ada, w_in, g_q, g_kc, g_ks, g_kw, pe_ck, w_ck1, w_ck2,
     pe_cv, w_cv1, w_cv2, g_ret, w_out) = W
    b, l = x.shape[:2]
    pos = jnp.arange(l, dtype=jnp.int32)
    (gate, q, kc_raw, vc_raw, ks, vs, kw, vw, br, g_a, rq, rk, rv, g_r) = mix_inputs(
        x, c, g_norm, w_ada, b_ada, w_in, g_q, g_ks, g_kw)
    kc = rmsnorm(compress(kc_raw, pe_ck, w_ck1, w_ck2), g_kc)
    vc = compress(vc_raw, pe_cv, w_cv1, w_cv2)
    ks_blk = ks.reshape(b, l // SEL_BLOCK, SEL_BLOCK, N_KV, HEAD_DIM)
    vs_blk = vs.reshape(b, l // SEL_BLOCK, SEL_BLOCK, N_KV, HEAD_DIM)
    pad_w = ((0, 0), (WINDOW, 0), (0, 0), (0, 0))
    kw_pad = jnp.pad(kw, pad_w)
    vw_pad = jnp.pad(vw, pad_w)
    qb = min(Q_BLOCK, l)

    def q_block(i):
        s0 = i * qb
        return nsa_block(lax.dynamic_slice_in_dim(q, s0, qb, axis=1),
                         s0 + jnp.arange(qb, dtype=jnp.int32), kc, vc, ks_blk, vs_blk,
                         lax.dynamic_slice_in_dim(kw_pad, s0, WINDOW + qb, axis=1),
                         lax.dynamic_slice_in_dim(vw_pad, s0, WINDOW + qb, axis=1),
                         s0 - WINDOW + jnp.arange(WINDOW + qb, dtype=jnp.int32),
                         lax.dynamic_slice_in_dim(br, s0, qb, axis=1))

    o_nsa = lax.map(q_block, jnp.arange(l // qb, dtype=jnp.int32))
    o_nsa = o_nsa.swapaxes(0, 1).reshape(b, l, N_KV, HPG, HEAD_DIM)
    s_init = jnp.zeros((b, N_RET_HEADS, DK_RET, DV_RET), jnp.float32)
    o_ret, s_ret = retention(rotary(rq, pos), rotary(rk, pos) * DK_RET ** -0.5, rv.astype(jnp.float32), s_init)
    y = mix_outputs(x, gate, o_nsa, g_a, o_ret, g_r, g_ret, w_out)
    keep = min(WINDOW, l)
    return y, (jnp.stack([kc_raw, vc_raw], axis=2), jnp.stack([ks, vs], axis=2),
               jnp.stack([kw[:, l - keep:], vw[:, l - keep:]], axis=2), s_ret.astype(x.dtype))


def sample_layer(x, c, cache_cmp, cache_slc, state_win, state_ret, page_table, W):
    (g_norm, w_ada, b_ada, w_in, g_q, g_kc, g_ks, g_kw, pe_ck, w_ck1, w_ck2,
     pe_cv, w_cv1, w_cv2, g_ret, w_out) = W
    b, l = x.shape[:2]
    pos = PAST_LEN + jnp.arange(l, dtype=jnp.int32)
    (gate, q, kc_raw, vc_raw, ks, vs, kw, vw, br, g_a, rq, rk, rv, g_r) = mix_inputs(
        x, c, g_norm, w_ada, b_ada, w_in, g_q, g_ks, g_kw)
    new_cmp = jnp.stack([kc_raw, vc_raw], axis=2)
    new_slc = jnp.stack([ks, vs], axis=2)
    full_cmp = pad_rows(jnp.concatenate(
        [gather_pages(cache_cmp, page_table).astype(x.dtype), new_cmp], axis=1), CMP_STRIDE)
    kc = rmsnorm(compress(full_cmp[:, :, 0], pe_ck, w_ck1, w_ck2), g_kc)
    vc = compress(full_cmp[:, :, 1], pe_cv, w_cv1, w_cv2)
    full_slc = pad_rows(jnp.concatenate(
        [gather_pages(cache_slc, page_table).astype(x.dtype), new_slc], axis=1), SEL_BLOCK)
    slc_blk = full_slc.reshape(b, -1, SEL_BLOCK, 2, N_KV, HEAD_DIM)
    wb = state_win.shape[1]
    win = jnp.concatenate([state_win.astype(x.dtype), jnp.stack([kw, vw], axis=2)], axis=1)
    win_pos = PAST_LEN - wb + jnp.arange(wb + l, dtype=jnp.int32)
    o_nsa = nsa_block(q, pos, kc, vc, slc_blk[:, :, :, 0], slc_blk[:, :, :, 1],
                      win[:, :, 0], win[:, :, 1], win_pos, br)
    o_ret, s_ret = retention(rotary(rq, pos), rotary(rk, pos) * DK_RET ** -0.5,
                             rv.astype(jnp.float32), state_ret.astype(jnp.float32))
    y = mix_outputs(x, gate, o_nsa, g_a, o_ret, g_r, g_ret, w_out)
    keep = min(WINDOW, wb + l)
    return y, (new_cmp, new_slc, win[:, wb + l - keep:], s_ret.astype(x.dtype))


def setup_inputs(seed: int = 0) -> dict:
    key = jax.random.key(seed)
    k = jax.random.split(key, 25)
    n_pages = PAST_LEN // PAGE_SIZE
    n_used = DEC_BATCH * n_pages
    n_phys = n_used + n_used // 4
    win_len = min(WINDOW, PAST_LEN)

    def nrm(kk, shape, s):
        return s * jax.random.normal(kk, shape, jnp.float32)

    def gain(kk, n):
        return 1.0 + nrm(kk, (DEPTH, n), 0.01)

    kv_page = (DEPTH, n_phys, PAGE_SIZE, 2, N_KV, HEAD_DIM)
    page_table = jax.random.permutation(k[8], n_phys)[:n_used].reshape(DEC_BATCH, n_pages).astype(jnp.int32)
    return {
        'x_prompt': nrm(k[0], (BATCH, SEQ, D_MODEL), 1.0),
        'x_sample': nrm(k[1], (DEC_BATCH, DEC_SEQ, D_MODEL), 1.0),
        'c_prompt': nrm(k[2], (BATCH, D_MODEL), 1.0),
        'c_sample': nrm(k[3], (DEC_BATCH, D_MODEL), 1.0),
        'cache_cmp': nrm(k[4], kv_page, 1.0),
        'cache_slc': nrm(k[5], kv_page, 1.0),
        'state_win': nrm(k[6], (DEPTH, DEC_BATCH, win_len, 2, N_KV, HEAD_DIM), 1.0),
        'state_ret': nrm(k[7], (DEPTH, DEC_BATCH, N_RET_HEADS, DK_RET, DV_RET), 0.5),
        'page_table': page_table,
        'g_norm': gain(k[9], D_MODEL),
        'w_ada': nrm(k[10], (DEPTH, D_MODEL, 3 * D_MODEL), 0.5 * D_MODEL ** -0.5),
        'b_ada': nrm(k[11], (DEPTH, 3 * D_MODEL), 0.02),
        'w_in': nrm(k[12], (DEPTH, D_MODEL, D_IN), D_MODEL ** -0.5),
        'g_q': gain(k[13], HEAD_DIM),
        'g_kc': gain(k[14], HEAD_DIM),
        'g_ks': gain(k[15], HEAD_DIM),
        'g_kw': gain(k[16], HEAD_DIM),
        'pe_ck': nrm(k[17], (DEPTH, CMP_BLOCK, HEAD_DIM), 0.1),
        'w_ck1': nrm(k[18], (DEPTH, CMP_BLOCK, HEAD_DIM, CMP_HIDDEN), (CMP_BLOCK * HEAD_DIM) ** -0.5),
        'w_ck2': nrm(k[19], (DEPTH, CMP_HIDDEN, HEAD_DIM), CMP_HIDDEN ** -0.5),
        'pe_cv': nrm(k[20], (DEPTH, CMP_BLOCK, HEAD_DIM), 0.1),
        'w_cv1': nrm(k[21], (DEPTH, CMP_BLOCK, HEAD_DIM, CMP_HIDDEN), (CMP_BLOCK * HEAD_DIM) ** -0.5),
        'w_cv2': nrm(k[22], (DEPTH, CMP_HIDDEN, HEAD_DIM), CMP_HIDDEN ** -0.5),
        'g_ret': gain(k[23], DV_RET),
        'w_out': nrm(k[24], (DEPTH, D_MIX, D_MODEL), D_MIX ** -0.5),
    }


def reference(x_prompt, x_sample, c_prompt, c_sample, cache_cmp, cache_slc, state_win, state_ret, page_table,
              g_norm, w_ada, b_ada, w_in, g_q, g_kc, g_ks, g_kw, pe_ck, w_ck1, w_ck2, pe_cv, w_cv1, w_cv2,
              g_ret, w_out):
    xp, xs = x_prompt, x_sample
    new_p, new_s = [], []
    for li in range(DEPTH):
        W = (g_norm[li], w_ada[li], b_ada[li], w_in[li], g_q[li], g_kc[li], g_ks[li], g_kw[li],
             pe_ck[li], w_ck1[li], w_ck2[li], pe_cv[li], w_cv1[li], w_cv2[li], g_ret[li], w_out[li])
        xp, st_p = prompt_layer(xp, c_prompt, W)
        xs, st_s = sample_layer(xs, c_sample, cache_cmp[li], cache_slc[li], state_win[li], state_ret[li],
                                page_table, W)
        new_p.append(st_p)
        new_s.append(st_s)

    def stack(group, j):
        return jnp.stack([st[j] for st in group])

    return (xp, xs, stack(new_p, 0), stack(new_p, 1), stack(new_p, 2), stack(new_p, 3),
            stack(new_s, 0), stack(new_s, 1), stack(new_s, 2), stack(new_s, 3))
```

```python
import os
import numpy as np
import ml_dtypes
from contextlib import ExitStack
import concourse.bass as bass
import concourse.mybir as mybir
from concourse.bass_utils import run_bass_kernel_spmd

F32 = mybir.dt.float32
BF16 = mybir.dt.bfloat16
I32 = mybir.dt.int32
AF = mybir.ActivationFunctionType
ALU = mybir.AluOpType
AX = mybir.AxisListType

D = 1024
NT = 32
COLS = (512, 128, 128, 128, 128, 128, 128, 24, 512, 256, 256, 512, 512)
OFF = np.concatenate([[0], np.cumsum(COLS)]).astype(int)
D_IN = int(OFF[-1])
(O_Q, O_KC, O_VC, O_KS, O_VS, O_KW, O_VW, O_BR, O_GA, O_RQ, O_RK, O_RV, O_GR) = [int(v) for v in OFF[:-1]]
NEGB = -30000.0
GC = [float((1.0 - 2.0 ** (-5.0 - h)) ** 128) for h in range(4)]
EPS = 1e-6
SAME_ENGINE_SYNC = True
DO_SAMPLE = True


class TW:
    def __init__(self, t, name):
        self.t = t
        self.name = name
        self.w = None
        self.r = []

    def __getitem__(self, k):
        return self.t[k]


class TWView:
    def __init__(self, base, ap):
        self.base = base
        self.ap = ap
        self.name = base.name

    def __getitem__(self, k):
        return self.ap[k]

    @property
    def w(self):
        return self.base.w

    @w.setter
    def w(self, v):
        self.base.w = v

    @property
    def r(self):
        return self.base.r

    @r.setter
    def r(self, v):
        self.base.r = v


class Prog:
    def __init__(self, nc, es):
        self.nc = nc
        self.es = es
        self.scope = None
        self.tiles = {}
        self.ops = []
        self.eng = {"pe": nc.tensor, "act": nc.scalar, "dve": nc.vector, "pool": nc.gpsimd, "sp": nc.sync}
        self.cnt = {}
        self.sems = {}
        self.known = {e: {} for e in self.eng}
        self.emitted = 0
        self.all_waits = []

    def sem(self, name):
        return self.es.enter_context(self.nc.semaphore(name))

    def scope_push(self):
        self.scope = ExitStack()

    def scope_pop(self):
        self.scope.close()
        self.scope = None

    def sb(self, name, shape, dt):
        if name in self.tiles:
            return self.tiles[name]
        st = self.scope if self.scope is not None else self.es
        t = TW(st.enter_context(self.nc.sbuf_tensor(name, list(shape), dt)), name)
        if self.scope is None:
            self.tiles[name] = t
        return t

    def ps(self, name, shape, dt):
        return TW(self.es.enter_context(self.nc.psum_tensor(name, list(shape), dt)), name)

    def op(self, eng, fn, reads=(), writes=(), dma=None):
        idx = len(self.ops)
        deps = set()
        for t in reads:
            if t.w is not None:
                deps.add(t.w)
        for t in writes:
            if t.w is not None:
                deps.add(t.w)
            for e in t.r:
                deps.add(e)
        deps.discard(idx)
        self.ops.append(dict(eng=eng, fn=fn, deps=sorted(deps), dma=dma, sig=False))
        for t in reads:
            t.r.append(idx)
        for t in writes:
            t.w = idx
            t.r = []
        return idx

    def emit(self, final=True):
        ops = self.ops
        s0 = self.emitted
        seg = range(s0, len(ops))
        cnt, sems = self.cnt, self.sems
        for i in seg:
            o = ops[i]
            for d in o["deps"]:
                if d < s0:
                    continue
                p = ops[d]
                if p["dma"] is not None or p["eng"] != o["eng"] or (SAME_ENGINE_SYNC and o["eng"] != "pe"):
                    p["sig"] = True
        last = {}
        for i in seg:
            if ops[i]["dma"] is None:
                last[ops[i]["eng"]] = i
        for i in last.values():
            ops[i]["sig"] = True
        barrier = dict(cnt) if s0 > 0 else {}
        for i in seg:
            o = ops[i]
            key = ("dma", o["dma"]) if o["dma"] is not None else ("eng", o["eng"])
            if o["dma"] is not None:
                o["sig"] = True
            if o["sig"]:
                if key not in sems:
                    sems[key] = self.sem("s_%s_%s" % key)
                    cnt[key] = 0
                cnt[key] += 16 if o["dma"] is not None else 1
                o["ev"] = (key, cnt[key])
        did_barrier = set()
        for i in seg:
            o = ops[i]
            e = self.eng[o["eng"]]
            kn = self.known[o["eng"]]
            need = {}
            if o["eng"] not in did_barrier:
                did_barrier.add(o["eng"])
                for key, val in barrier.items():
                    need[key] = val
            for d in o["deps"]:
                if d < s0:
                    continue
                p = ops[d]
                if not p["sig"]:
                    continue
                if p["dma"] is None and p["eng"] == o["eng"] and (not SAME_ENGINE_SYNC or o["eng"] == "pe"):
                    continue
                pc = p["dma"] is not None and p["dma"].startswith("const")
                if pc and o["dma"] == p["dma"]:
                    continue
                key, val = p["ev"]
                if pc:
                    val = cnt[key]
                need[key] = max(need.get(key, 0), val)
            o["waits"] = []
            for key, val in need.items():
                if kn.get(key, 0) >= val:
                    continue
                e.wait_ge(sems[key], val)
                o["waits"].append((key, val))
                kn[key] = val
            ins = o["fn"]()
            if o["sig"]:
                key, val = o["ev"]
                ins.then_inc(sems[key], 16 if o["dma"] is not None else 1)
        self.emitted = len(ops)
        self.stats = dict(n_ops=len(ops), cnt={str(k): v for k, v in cnt.items()})
        if final:
            for key, s in sems.items():
                if key[0] == "dma":
                    self.nc.sync.wait_ge(s, cnt[key])


def build_program(sample=True):
    nc = bass.Bass("TRN2", target_bir_lowering=False)
    es = ExitStack()
    P = Prog(nc, es)

    def din(name, shape, dt=F32):
        return nc.dram_tensor(name, list(shape), dt, kind="ExternalInput").ap()

    def dout(name, shape, dt=F32):
        return nc.dram_tensor(name, list(shape), dt, kind="ExternalOutput").ap()

    xloc = din("xloc", [NT * 128, D])
    cvec = din("cvec", [5, D])
    w_ada = din("w_ada", [D, 3 * D])
    b_ada = din("b_ada", [1, 3 * D])
    w_in = din("w_in", [D, D_IN])
    w_out = din("w_out", [D, D])
    g_norm = din("g_norm", [1, D])
    g_q = din("g_q", [1, 64]); g_kc = din("g_kc", [1, 64]); g_ks = din("g_ks", [1, 64]); g_kw = din("g_kw", [1, 64])
    g_ret = din("g_ret", [1, 128])
    pe_ck = din("pe_ck", [32, 64]); pe_cv = din("pe_cv", [32, 64])
    w_ck1 = din("w_ck1", [32, 64, 64]); w_cv1 = din("w_cv1", [32, 64, 64])
    w_ck2 = din("w_ck2", [64, 64]); w_cv2 = din("w_cv2", [64, 64])
    c_ident = din("c_ident", [128, 128])
    c_expand = din("c_expand", [64, NT * 128])
    c_mask3 = din("c_mask3", [128, 3, 128])
    c_maskc = din("c_maskc", [128, 16, 2, 128])
    c_adds = din("c_adds", [128, 16, 64])
    c_rope = din("c_rope", [128, NT, 4, 32])
    c_wimp = din("c_wimp", [128, 2, 64])
    c_dmask = din("c_dmask", [128, 4, 128])
    c_rscal = din("c_rscal", [128, NT, 8])
    y_out = dout("y_out", [16 * 128, D])
    pcmp_out = dout("pcmp_out", [16 * 128, 256])
    pslc_out = dout("pslc_out", [16 * 128, 256])
    pwin_out = dout("pwin_out", [2 * 128, 256])
    pret_out = dout("pret_out", [64, 4, 128])
    DBGO = os.environ.get('KS_DBGOUT', '0') == '1'
    dbg_out = dout("dbg_out", [16 * 128, D]) if DBGO else None

    xs_in = din("xs_in", [4, D])
    ptc = din("ptc", [4, 128], I32)
    NPHYS = int(os.environ.get('KS_NPHYS', '5120'))
    cache_cmp = din("cache_cmp", [NPHYS * 128, 256])
    cache_slc = din("cache_slc", [NPHYS * 128, 256])
    dbg2_out = dout("dbg2_out", [128, 64]) if DBGO else None
    dbg3_out = dout("dbg3_out", [128, 2048]) if DBGO else None
    swin_in = din("swin_in", [4, 512, 256])
    sret_in = din("sret_in", [4, 4, 64, 128])
    c_rope_s = din("c_rope_s", [4, 4, 32])
    c_wfull = din("c_wfull", [128, 8, 256])
    c_top = din("c_top", [8, 2, 256])
    c_pcol = din("c_pcol", [128, 6])
    c_oh4 = din("c_oh4", [4, 4, 128])
    c_ohq = din("c_ohq", [64, 4, 4])
    ys_out = dout("ys_out", [4, D])
    scmp_out = dout("scmp_out", [4, 256])
    sslc_out = dout("sslc_out", [4, 256])
    swin_out = dout("swin_out", [4, 512, 256])
    sret_out = dout("sret_out", [4, 4, 64, 128])
    scr1 = nc.dram_tensor("scr1", [128, 1], F32, kind="Internal").ap()
    scr2 = nc.dram_tensor("scr2", [128, 1], F32, kind="Internal").ap()
    sb, ps = P.sb, P.ps
    V, A, T, G, S = "dve", "act", "pe", "pool", "sp"

    def dma(eng, out_t, out_ap, in_t, in_ap, grp, **kw):
        reads = [in_t] if in_t is not None else []
        writes = [out_t] if out_t is not None else []
        e = P.eng[eng]
        if not grp.startswith("const"):
            grp = (out_t or in_t).name
        return P.op(eng, lambda: e.dma_start(out=out_ap, in_=in_ap, **kw), reads, writes, dma=grp)

    def vop(name, reads, writes, eng=V, **kw):
        e = P.eng[eng]
        f = getattr(e, name)
        return P.op(eng, lambda: f(**kw), reads, writes)

    def act(out_t, out_ap, in_t, in_ap, func, extra_reads=(), extra_writes=(), **kw):
        return P.op(A, lambda: nc.scalar.activation(out=out_ap, in_=in_ap, func=func, **kw), [in_t] + list(extra_reads), [out_t] + list(extra_writes))

    def acopy(out_t, out_ap, in_t, in_ap):
        return P.op(A, lambda: nc.scalar.copy(out=out_ap, in_=in_ap), [in_t], [out_t])

    def mm(out_t, out_ap, l_t, l_ap, r_t, r_ap, start, stop, sgc=False):
        return P.op(T, lambda: nc.tensor.matmul(out_ap, lhsT=l_ap, rhs=r_ap, start=start, stop=stop, skip_group_check=sgc), [l_t, r_t], [out_t])

    def tr(out_t, out_ap, in_t, in_ap, id_t, id_ap):
        return P.op(T, lambda: nc.tensor.transpose(out=out_ap, in_=in_ap, identity=id_ap), [in_t, id_t], [out_t])

    sb("w_in_bf", [128, 8, D_IN], BF16)
    sb("w_out_bf", [128, 8, D], BF16)
    sb("ident", [128, 128], BF16)
    ident_f = sb("ident_f", [128, 128], F32)
    ones_b = sb("ones_b", [128, 128], BF16)
    for i in range(2):
        sb("w1blk%d" % i, [128, 32, 128], BF16)
        sb("w2blk%d" % i, [128, 128], BF16)
        sb("cbias%d" % i, [128, 1], F32)
    sb("modA", [128, 8, 5], F32)
    sb("modS", [128, 8, 5], F32)
    sb("gvec", [128, 64 * 4 + 128], F32)
    sb("modT", [128, 24, 5], F32)
    P.scope_push()
    w_in_bf = sb("w_in_bf", [128, 8, D_IN], BF16)
    w_out_bf = sb("w_out_bf", [128, 8, D], BF16)
    wst = [sb("wst%d" % i, [128, 8, 128], F32) for i in range(2)]
    ident = sb("ident", [128, 128], BF16)
    mask3 = sb("mask3", [128, 3, 128], BF16)
    maskc = sb("maskc", [128, 16, 2, 128], BF16)
    adds = sb("adds", [128, 16, 64], F32)
    wimp = sb("wimp", [128, 2, 64], BF16)
    dmask = sb("dmask", [128, 4, 128], F32)
    rscal = sb("rscal", [128, NT, 8], F32)
    ksT = [sb("ksT%d" % g, [128, NT * 128], BF16) for g in range(2)]
    vsa = [sb("vsa%d" % g, [128, NT, 65], BF16) for g in range(2)]
    kwT = [sb("kwT%d" % g, [64, 6, 128], BF16) for g in range(2)]
    vwa = [sb("vwa%d" % g, [128, 6, 65], BF16) for g in range(2)]
    cT = [sb("cT%d" % i, [128, 144], BF16) for i in range(2)]
    hTall = [sb("hTall%d" % i, [128, 256], BF16) for i in range(2)]
    kcT = [sb("kcT%d" % g, [64, 256], BF16) for g in range(2)]
    vca = [sb("vca%d" % g, [128, 2, 129], BF16) for g in range(2)]
    w1blk = [sb("w1blk%d" % i, [128, 32, 128], BF16) for i in range(2)]
    w2blk = [sb("w2blk%d" % i, [128, 128], BF16) for i in range(2)]
    cbias = [sb("cbias%d" % i, [128, 1], F32) for i in range(2)]
    S_f = sb("S_f", [64, 4, 128], F32)
    S_b = sb("S_b", [64, 4, 128], BF16)
    modA = sb("modA", [128, 8, 5], F32)
    modS = sb("modS", [128, 8, 5], F32)
    gate_bc = sb("gate_bc", [128, D], F32)
    gvec = sb("gvec", [128, 64 * 4 + 128], F32)
    gn_col = sb("gn_col", [128, 8], F32)

    ptr = ps("ptr", [128, 1024], BF16)
    pin = [ps("pin%d" % i, [128, 512], F32) for i in range(2)]
    pst = [ps("pst%d" % i, [128, 512], F32) for i in range(2)]
    pacc = [ps("pacc%d" % i, [128, 512], F32) for i in range(2)]
    paccw = ps("paccw", [128, 512], F32)

    CG = "const"
    for g in range(2):
        vop("memset", [], [vsa[g]], eng=G, ap=vsa[g][:, :, 64:65], constant=1.0)
        vop("memset", [], [vwa[g]], eng=G, ap=vwa[g][:, :, 64:65], constant=1.0)
        vop("memset", [], [vca[g]], eng=G, ap=vca[g][:], constant=0.0)
    for i in range(2):
        vop("memset", [], [hTall[i]], eng=G, ap=hTall[i][:], constant=0.0)
        vop("memset", [], [cT[i]], eng=G, ap=cT[i][:], constant=0.0)
        vop("memset", [], [w1blk[i]], eng=G, ap=w1blk[i][:], constant=0.0)
        vop("memset", [], [w2blk[i]], eng=G, ap=w2blk[i][:], constant=0.0)
    vop("memset", [], [ones_b], eng=G, ap=ones_b[:], constant=1.0)
    vop("memset", [], [S_f], eng=G, ap=S_f[:], constant=0.0)
    vop("memset", [], [S_b], eng=G, ap=S_b[:], constant=0.0)
    dma(G, ident, ident[:], None, c_ident, CG)
    dma(S, ident_f, ident_f[:], None, c_ident, CG)
    dma(G, mask3, mask3[:], None, c_mask3, CG)
    dma(S, adds, adds[:], None, c_adds, CG)
    dma(S, dmask, dmask[:], None, c_dmask, CG)
    dma(S, rscal, rscal[:], None, c_rscal, CG)
    dma(G, maskc, maskc[:], None, c_maskc, CG)
    dma(G, wimp, wimp[:], None, c_wimp, CG)
    for g in range(2):
        dma(G, ksT[g], ksT[g][64:128, :], None, c_expand, CG)
    dma(S, gvec, gvec[:, 0:64], None, g_q.to_broadcast([128, 64]), CG)
    dma(S, gvec, gvec[:, 64:128], None, g_kc.to_broadcast([128, 64]), CG)
    dma(S, gvec, gvec[:, 128:192], None, g_ks.to_broadcast([128, 64]), CG)
    dma(S, gvec, gvec[:, 192:256], None, g_kw.to_broadcast([128, 64]), CG)
    dma(S, gvec, gvec[:, 256:384], None, g_ret.to_broadcast([128, 128]), CG)
    dma(S, gn_col, gn_col[:], None, g_norm.rearrange("o (k p) -> p (o k)", p=128), CG, allow_slow_non_contiguous=True)
    peTb = [sb("peTb%d" % i, [128, 32], BF16) for i in range(2)]
    for i, (w1, w2, pe) in enumerate(((w_ck1, w_ck2, pe_ck), (w_cv1, w_cv2, pe_cv))):
        src = w1.rearrange("l d f -> d l f")
        dma(G, w1blk[i], w1blk[i][0:64, :, 0:64], None, src, CG)
        dma(G, w1blk[i], w1blk[i][64:128, :, 64:128], None, src, CG)
        dma(G, w2blk[i], w2blk[i][0:64, 0:64], None, w2, CG)
        dma(G, w2blk[i], w2blk[i][64:128, 64:128], None, w2, CG)
        srcp = pe.rearrange("l d -> d l")
        dma(G, peTb[i], peTb[i][0:64, :], None, srcp, CG, allow_slow_non_contiguous=True)
        dma(G, peTb[i], peTb[i][64:128, :], None, srcp, CG, allow_slow_non_contiguous=True)
    cT_f = sb("cT_f", [128, 8, 5], F32)
    cT_b = sb("cT_b", [128, 8, 5], BF16)
    xs = sb("xs", [128, D], BF16)
    hT = sb("hT", [128, 8, 128], BF16)
    crep = hT
    for r in range(5):
        dma(S, cT_f, cT_f[:, :, r], None, cvec[r:r + 1, :].rearrange("o (k p) -> p (o k)", p=128), CG, allow_slow_non_contiguous=True)
    bad = sb("bad", [128, 24], F32)
    dma(S, bad, bad[:], None, b_ada.rearrange("o (j p) -> p (o j)", p=128), CG, allow_slow_non_contiguous=True)
    yst = [sb("yst0", [128, D], F32)]
    badg = yst[0]
    dma(S, badg, badg[:], None, b_ada[:, 2 * D:3 * D].to_broadcast([128, D]), CG)
    STAGE = int(os.environ.get('KS_STAGE', '9'))
    RUN_NT = int(os.environ.get('KS_NT', str(NT)))
    if STAGE < 2:
        P.emit(); es.close(); return nc
    for g in range(2):
        vop("memset", [], [vca[g]], eng=G, ap=vca[g][:, :, 64:65], constant=1.0)
        vop("tensor_copy", [wimp], [vca[g]], eng=G, out=vca[g][:, :, 65:129], in_=wimp[:])
    vop("tensor_scalar", [gvec], [gvec], out=gvec[:, 64:256], in0=gvec[:, 64:256], scalar1=8.0, scalar2=None, op0=ALU.mult)
    vop("tensor_scalar", [gvec], [gvec], out=gvec[:, 256:384], in0=gvec[:, 256:384], scalar1=float(np.sqrt(128.0)), scalar2=None, op0=ALU.mult)
    for i in range(2):
        for l in range(32):
            mm(pin[0], pin[0][:, 0:1], w1blk[i], w1blk[i][:, l, :], peTb[i], peTb[i][:, l:l + 1], l == 0, l == 31)
        vop("tensor_copy", [pin[0]], [cbias[i]], out=cbias[i][:], in_=pin[0][:, 0:1])

    def load_cast(src, ncols, dst):
        k = 0
        for c0 in range(0, ncols, 128):
            cw = min(128, ncols - c0)
            st = wst[k % 2]
            k += 1
            dma(S if k % 2 else A, st, st[:, :, 0:cw], None, src[:, c0:c0 + cw].rearrange("(k p) c -> p k c", p=128), "x")
            vop("tensor_copy", [st], [dst], eng=(V if k % 2 else G), out=dst[:, :, c0:c0 + cw], in_=st[:, :, 0:cw])

    act(cT_b, cT_b[:], cT_f, cT_f[:], AF.Silu)
    for kt in range(8):
        vop("tensor_copy", [cT_b], [crep], out=crep[:, kt, :], in_=cT_b[:, kt, 0:1].to_broadcast([128, 128]))
    modT = sb("modT", [128, 24, 5], F32)
    wab = [sb("wab0", [128, 8, 128], BF16), TWView(xs, xs[:].rearrange("p (k c) -> p k c", k=8))]
    k = 0
    for c0 in range(0, 3 * D, 128):
        st = wst[k % 2]
        wb = wab[k % 2]
        k += 1
        dma(S if k % 2 else A, st, st[:], None, w_ada[:, c0:c0 + 128].rearrange("(k p) c -> p k c", p=128), "x")
        vop("tensor_copy", [st], [wb], out=wb[:], in_=st[:])
        j = c0 // 128
        po = pin[j % 2]
        for kt in range(8):
            mm(po, po[:, 0:5], wb, wb[:, kt, :], cT_b, cT_b[:, kt, :], kt == 0, kt == 7)
        vop("tensor_scalar", [po, bad], [modT], out=modT[:, j, :], in0=po[:, 0:5], scalar1=bad[:, j:j + 1], scalar2=None, op0=ALU.add)
        if c0 >= 2 * D:
            po = pst[j % 2]
            for kt in range(8):
                mm(po, po[:, 0:128], crep, crep[:, kt, :], wb, wb[:, kt, :], kt == 0, kt == 7)
            vop("tensor_tensor", [po, badg], [gate_bc], out=gate_bc[:, c0 - 2 * D:c0 - 2 * D + 128], in0=po[:, 0:128], in1=badg[:, c0 - 2 * D:c0 - 2 * D + 128], op=ALU.add)
    vop("tensor_scalar", [modT], [modA], out=modA[:], in0=modT[:, 8:16, :], scalar1=1.0, scalar2=None, op0=ALU.add)
    vop("tensor_tensor", [modA, gn_col], [modA], out=modA[:], in0=modA[:], in1=gn_col[:].unsqueeze(2).to_broadcast([128, 8, 5]), op=ALU.mult)
    vop("tensor_copy", [modT], [modS], out=modS[:], in_=modT[:, 0:8, :])

    load_cast(w_in, D_IN, w_in_bf)
    load_cast(w_out, D, w_out_bf)

    xt = [sb("xt0", [128, D], F32)] * 2
    rope = [sb("rope%d" % i, [128, 4, 32], F32) for i in range(2)]
    ss = sb("ss", [128, 1], F32)
    rstd = sb("rstd", [128, 1], F32)
    sq = sb("sq", [128, 512], F32)
    ssq = sb("ssq", [128, 16], F32)
    rq8 = sb("rq8", [128, 16], F32)
    qn = sb("qn", [128, 512], BF16)
    qaug = [sb("qaug%d" % g, [128, 4, 128], BF16) for g in range(2)]
    kvf = sb("kvf", [128, 512], F32)
    kvb = sb("kvb", [128, 512], BF16)
    kwf = sb("kwf", [128, 280], F32)
    kwb = sb("kwb", [128, 128], BF16)
    gates = sb("gates", [128, 24], F32)
    ga_s = sb("ga_s", [128, 512], BF16)
    gr_s = sb("gr_s", [128, 512], BF16)
    rot = sb("rot", [128, 2, 4, 64], F32)
    tmp1 = sb("tmp1", [128, 2, 4, 32], F32)
    tmp2 = sb("tmp2", [128, 2, 4, 32], F32)
    rqb = sb("rqb", [128, 3, 4, 64], BF16)
    ktl = sb("ktl", [128, 4, 64], BF16)
    rT = sb("rT", [64, 3, 4, 128], BF16)
    vb = sb("vb", [128, 4, 128], BF16)
    atm = sb("atm", [128, 4, 128], BF16)
    oret = sb("oret", [128, 4, 128], F32)
    et = [sb("et%d" % i, [128, 4, 128], BF16) for i in range(2)]
    obr = [sb("obr%d" % i, [128, 4, 65], F32) for i in range(3)]
    aimp = sb("aimp", [128, 4, 64], F32)
    rden = sb("rden", [128, 3, 4], F32)
    coef = sb("coef", [128, 3, 4], F32)
    score = sb("score", [128, 64], F32)
    swork = sb("swork", [128, 64], F32)
    mx8 = sb("mx8", [128, 16], F32)
    thr = sb("thr", [128, 1], F32)
    selb = sb("selb", [128, 128], BF16)
    onsa = sb("onsa", [128, 8, 64], F32)
    otmp = sb("otmp", [128, 4, 64], F32)
    ymix = xs
    yT = hT
    kcn = sb("kcn", [128, 128], F32)
    kcnb = sb("kcnb", [128, 128], BF16)
    vop("memset", [], [selb], eng=G, ap=selb[:], constant=0.0)

    def rms_heads(src_t, src_ap, nh, hd, gain_ap, out_t, out_ap, eps_sum, np_=128, scr=None):
        sq_, ssq_, rq_ = scr if scr is not None else (sq, ssq, rq8)
        if src_t.name.startswith("pin") or src_t.name.startswith("pst") or src_t.name.startswith("pacc"):
            act(sq_, sq_[0:np_, 0:nh * hd], src_t, src_ap, AF.Square)
        else:
            vop("tensor_tensor", [src_t], [sq_], out=sq_[0:np_, 0:nh * hd], in0=src_ap, in1=src_ap, op=ALU.mult)
        vop("tensor_reduce", [sq_], [ssq_], out=ssq_[0:np_, 0:nh], in_=sq_[0:np_, 0:nh * hd].rearrange("p (h d) -> p h d", h=nh), axis=AX.X, op=ALU.add)
        vop("tensor_scalar", [ssq_], [rq_], out=rq_[0:np_, 0:nh], in0=ssq_[0:np_, 0:nh], scalar1=eps_sum, scalar2=None, op0=ALU.add)
        act(rq_, rq_[0:np_, 0:nh], rq_, rq_[0:np_, 0:nh], AF.Sqrt)
        vop("reciprocal", [rq_], [rq_], out=rq_[0:np_, 0:nh], in_=rq_[0:np_, 0:nh])
        vop("tensor_tensor", [src_t, rq_], [sq_], out=sq_[0:np_, 0:nh * hd].rearrange("p (h d) -> p h d", h=nh),
            in0=src_ap.rearrange("p (h d) -> p h d", h=nh), in1=rq_[0:np_, 0:nh].unsqueeze(2).to_broadcast([np_, nh, hd]), op=ALU.mult)
        vop("tensor_tensor", [sq_, gvec], [out_t], out=out_ap.rearrange("p (h d) -> p h d", h=nh),
            in0=sq_[0:np_, 0:nh * hd].rearrange("p (h d) -> p h d", h=nh), in1=gain_ap.unsqueeze(1).to_broadcast([np_, nh, hd]), op=ALU.mult)

    def inproj(c0, cw, po):
        for kt in range(8):
            mm(po, po[:, 0:cw], hT, hT[:, kt, :], w_in_bf, w_in_bf[:, kt, c0:c0 + cw], kt == 0, kt == 7)

    n_et = [0]

    def attn_tile(g, po, l_t, l_ap, r_ap, masks, v_t, v_ap, nv, acc, first, last, accw=None, vw_ap=None):
        nmm = 1 + len(masks)
        mm(po, po[:], l_t, l_ap, qaug[g], r_ap, True, nmm == 1)
        for mi, (m_t, m_ap) in enumerate(masks):
            for h in range(4):
                mm(po, po[:, h * 128:(h + 1) * 128], ident, ident[:], m_t, m_ap, False, (mi == len(masks) - 1) and h == 3, sgc=True)
        e = et[n_et[0] % 2]
        n_et[0] += 1
        act(e, e[:].rearrange("p h q -> p (h q)"), po, po[:], AF.Exp)
        for h in range(4):
            mm(acc, acc[:, h * 128:h * 128 + nv], e, e[:, h, :], v_t, v_ap, first and h == 0, last and h == 3, sgc=True)
            if accw is not None:
                mm(accw, accw[:, h * 64:(h + 1) * 64], e, e[:, h, :], v_t, vw_ap, first and h == 0, last and h == 3, sgc=True)

    n_st = [0]

    if STAGE < 3:
        P.emit(); es.close(); return nc
    SUB = int(os.environ.get('KS_SUB', '99'))

    class _Stop(Exception):
        pass

    def ck(k):
        if SUB < k:
            raise _Stop()

    def _tile(L):
            own = (L % 2 == 1)
            j = L // 2
            x_t = xt[L % 2]
            rp = rope[L % 2]
            dma(S, x_t, x_t[:], None, xloc[L * 128:(L + 1) * 128, :], "x%d" % (L % 2))
            dma(A, rp, rp[:], None, c_rope[:, L, :, :], "x%d" % (L % 2))
            vop("memset", [], [ss], eng=G, ap=ss[:], constant=0.0)
            act(xs, xs[:], x_t, x_t[:], AF.Square, accum_out=ss[:], extra_writes=[ss])
            vop("tensor_scalar", [ss], [rstd], out=rstd[:], in0=ss[:], scalar1=1.0 / D, scalar2=EPS, op0=ALU.mult, op1=ALU.add)
            act(rstd, rstd[:], rstd, rstd[:], AF.Sqrt)
            vop("reciprocal", [rstd], [rstd], out=rstd[:], in_=rstd[:])
            P.op(A, lambda x_t=x_t: nc.scalar.mul(out=xs[:], in_=x_t[:], mul=rstd[:, 0:1]), [x_t, rstd], [xs])
            for kt in range(8):
                tr(ptr, ptr[:, kt * 128:(kt + 1) * 128], xs, xs[:, kt * 128:(kt + 1) * 128], ident, ident[:])
            for kt in range(8):
                vop("tensor_scalar", [ptr, modA, modS], [hT], out=hT[:, kt, :], in0=ptr[:, kt * 128:(kt + 1) * 128],
                    scalar1=modA[:, kt, 0:1], scalar2=modS[:, kt, 0:1], op0=ALU.mult, op1=ALU.add)

            ck(1)
            po = pin[0]
            inproj(O_KC, 512, po)
            DBG = int(os.environ.get('KS_DBG', '3'))
            if DBG & 1:
                vop("tensor_copy", [po], [kvf], out=kvf[:], in_=po[:])
            if DBG & 2:
                if os.environ.get('KS_ACTV', '0') == '1':
                    act(kvb, kvb[:], po, po[:], AF.Identity)
                elif os.environ.get('KS_ACTV', '0') == '2':
                    act(kvb, kvb[:], po, po[:], AF.Silu)
                else:
                    vop("tensor_copy", [kvf], [kvb], eng=G, out=kvb[:], in_=kvf[:])
            ck(2)
            for i in range(2):
                tr(ptr, ptr[:, i * 128:(i + 1) * 128], kvb, kvb[:, i * 128:(i + 1) * 128], ident, ident[:])
            for i in range(2):
                vop("tensor_copy", [ptr], [cT[i]], out=cT[i][:, 16:144], in_=ptr[:, i * 128:(i + 1) * 128])
            ck(3)
            rms_heads(kvf, kvf[:, 256:384], 2, 64, gvec[:, 128:192], kvf, kvf[:, 256:384], 64 * EPS)
            vop("tensor_copy", [kvf], [kvb], out=kvb[:, 256:384], in_=kvf[:, 256:384])
            for g in range(2):
                tr(ptr, ptr[0:64, (2 + g) * 128:(3 + g) * 128], kvb, kvb[:, 256 + g * 64:256 + (g + 1) * 64], ident, ident[:])
            for g in range(2):
                vop("tensor_copy", [ptr], [ksT[g]], out=ksT[g][0:64, L * 128:(L + 1) * 128], in_=ptr[0:64, (2 + g) * 128:(3 + g) * 128])
                vop("tensor_copy", [kvb], [vsa[g]], eng=G, out=vsa[g][:, L, 0:64], in_=kvb[:, 384 + g * 64:384 + (g + 1) * 64])
            if own:
                dma(S, None, pcmp_out[j * 128:(j + 1) * 128, :], kvf, kvf[:, 0:256], "x")
                dma(S, None, pslc_out[j * 128:(j + 1) * 128, :], kvf, kvf[:, 256:512], "x")
            ck(4)
            po = pin[1]
            cw = 280 if own else 256
            inproj(O_KW, cw, po)
            vop("tensor_copy", [po], [kwf], out=kwf[:, 0:cw], in_=po[:, 0:cw])
            rms_heads(kwf, kwf[:, 0:128], 2, 64, gvec[:, 192:256], kwf, kwf[:, 0:128], 64 * EPS)
            vop("tensor_copy", [kwf], [kwb], out=kwb[:], in_=kwf[:, 0:128])
            for g in range(2):
                tr(ptr, ptr[0:64, (4 + g) * 128:(5 + g) * 128], kwb, kwb[:, g * 64:(g + 1) * 64], ident, ident[:])
            for g in range(2):
                vop("tensor_copy", [ptr], [kwT[g]], out=kwT[g][:, L % 6, :], in_=ptr[0:64, (4 + g) * 128:(5 + g) * 128])
                vop("tensor_copy", [kwf], [vwa[g]], eng=G, out=vwa[g][:, L % 6, 0:64], in_=kwf[:, 128 + g * 64:128 + (g + 1) * 64])
            if L in (29, 31):
                dma(S, None, pwin_out[((L - 29) // 2) * 128:((L - 29) // 2 + 1) * 128, :], kwf, kwf[:, 0:256], "x")
            if own:
                act(gates, gates[:], kwf, kwf[:, 256:280], AF.Sigmoid)

            ck(5)
            nb0 = 8 * L - 1
            c_lo = 1 if L == 0 else 0
            for i in range(2):
                po = pin[i]
                for l in range(32):
                    mm(po, po[:, 0:8 - c_lo], w1blk[i], w1blk[i][:, l, :], cT[i], cT[i][:, 16 * c_lo + l:16 * c_lo + l + 16 * (7 - c_lo) + 1:16], l == 0, l == 31)
                act(hTall[i], hTall[i][:, nb0 + c_lo:nb0 + 8], po, po[:, 0:8 - c_lo], AF.Silu, extra_reads=[cbias[i]], bias=cbias[i][:, 0:1])
                vop("tensor_copy", [cT[i]], [cT[i]], out=cT[i][:, 0:16], in_=cT[i][:, 128:144])

            ck(6)
            po = pin[0]
            rqk = po
            if own:
                inproj(O_RQ, 512, po)
                qk0 = 0
                rqo = 0
            else:
                inproj(O_RK, 256, po)
                qk0 = 1
                rqo = -256
            po = pin[1]
            inproj(O_RV, 512, po)
            vop("tensor_copy", [po], [vb], out=vb[:].rearrange("p h d -> p (h d)"), in_=po[:])
            ck(7)
            for s in range(qk0, 2):
                xv = rqk[:, s * 256 + rqo:(s + 1) * 256 + rqo].rearrange("p (h d) -> p h d", h=4)
                cs = rp[:, 2 * s, :].unsqueeze(1).to_broadcast([128, 4, 32])
                sn = rp[:, 2 * s + 1, :].unsqueeze(1).to_broadcast([128, 4, 32])
                vop("tensor_tensor", [rqk, rp], [tmp1], out=tmp1[:, s], in0=xv[:, :, 0:32], in1=cs, op=ALU.mult)
                vop("tensor_tensor", [rqk, rp], [tmp2], out=tmp2[:, s], in0=xv[:, :, 32:64], in1=sn, op=ALU.mult)
                vop("tensor_tensor", [tmp1, tmp2], [rot], out=rot[:, s, :, 0:32], in0=tmp1[:, s], in1=tmp2[:, s], op=ALU.subtract)
                vop("tensor_tensor", [rqk, rp], [tmp1], out=tmp1[:, s], in0=xv[:, :, 0:32], in1=sn, op=ALU.mult)
                vop("tensor_tensor", [rqk, rp], [tmp2], out=tmp2[:, s], in0=xv[:, :, 32:64], in1=cs, op=ALU.mult)
                vop("tensor_tensor", [tmp1, tmp2], [rot], out=rot[:, s, :, 32:64], in0=tmp1[:, s], in1=tmp2[:, s], op=ALU.add)
            vop("tensor_tensor", [rot, rscal], [ktl], out=ktl[:], in0=rot[:, 1], in1=rscal[:, L, 0:4].unsqueeze(2).to_broadcast([128, 4, 64]), op=ALU.mult)
            if own:
                vop("tensor_copy", [rot], [rqb], out=rqb[:, 0:2], in_=rot[:])
                vop("tensor_tensor", [rot, rscal], [rqb], out=rqb[:, 2], in0=rot[:, 0], in1=rscal[:, L, 4:8].unsqueeze(2).to_broadcast([128, 4, 64]), op=ALU.mult)
                for s in range(2):
                    for h in range(4):
                        c = s * 4 + h
                        tr(ptr, ptr[0:64, c * 128:(c + 1) * 128], rqb, rqb[:, s, h, :], ident, ident[:])
                vop("tensor_copy", [ptr], [rT], out=rT[:, 0:2].rearrange("p s h q -> p (s h q)"), in_=ptr[0:64, :])
                for h in range(4):
                    tr(ptr, ptr[0:64, h * 128:(h + 1) * 128], rqb, rqb[:, 2, h, :], ident, ident[:])
                vop("tensor_copy", [ptr], [rT], out=rT[:, 2].rearrange("p h q -> p (h q)"), in_=ptr[0:64, 0:512])
                po = pst[0]
                for h in range(4):
                    mm(po, po[:, h * 128:(h + 1) * 128], rT, rT[:, 1, h, :], rT, rT[:, 0, h, :], True, True)
                vop("tensor_tensor", [po, dmask], [atm], out=atm[:].rearrange("p h q -> p (h q)"), in0=po[:], in1=dmask[:].rearrange("p h q -> p (h q)"), op=ALU.mult)
                po = pst[1]
                for h in range(4):
                    mm(po, po[:, h * 128:(h + 1) * 128], atm, atm[:, h, :], vb, vb[:, h, :], True, False)
                    mm(po, po[:, h * 128:(h + 1) * 128], rT, rT[:, 2, h, :], S_b, S_b[:, h, :], False, True)
                vop("tensor_copy", [po], [oret], out=oret[:].rearrange("p h d -> p (h d)"), in_=po[:])
            ck(8)
            po = pacc[0]
            for h in range(4):
                mm(po, po[0:64, h * 128:(h + 1) * 128], ktl, ktl[:, h, :], vb, vb[:, h, :], True, True)
            for h in range(4):
                vop("tensor_scalar", [S_f], [S_f], out=S_f[:, h, :], in0=S_f[:, h, :], scalar1=GC[h], scalar2=None, op0=ALU.mult)
            vop("tensor_tensor", [S_f, po], [S_f], out=S_f[:].rearrange("p h d -> p (h d)"), in0=S_f[:].rearrange("p h d -> p (h d)"), in1=po[0:64, :], op=ALU.add)
            vop("tensor_copy", [S_f], [S_b], out=S_b[:], in_=S_f[:])
            if L == NT - 1:
                dma(S, None, pret_out, S_f, S_f[:], "oret")
            if not own:
                return

            po = pin[0]
            inproj(O_Q, 512, po)
            rms_heads(po, po[:], 8, 64, gvec[:, 0:64], qn, qn[:], 64 * EPS)
            for hh in range(8):
                tr(ptr, ptr[0:64, hh * 128:(hh + 1) * 128], qn, qn[:, hh * 64:(hh + 1) * 64], ident, ident[:])
            for g in range(2):
                vop("tensor_copy", [ptr], [qaug[g]], out=qaug[g][0:64, :, :].rearrange("p h q -> p (h q)"), in_=ptr[0:64, g * 512:(g + 1) * 512])
            po = pin[1]
            inproj(O_GA, 512, po)
            act(ga_s, ga_s[:], po, po[:], AF.Silu)
            po = pin[0]
            inproj(O_GR, 512, po)
            act(gr_s, gr_s[:], po, po[:], AF.Silu)

            nmax = 8 * L + 6
            ntiles = nmax // 128 + 1
            for nt in range(ntiles):
                po = pin[nt % 2]
                mm(po, po[:, 0:128], hTall[0], hTall[0][:, nt * 128:(nt + 1) * 128], w2blk[0], w2blk[0][:], True, True)
                mm(po, po[:, 128:256], hTall[1], hTall[1][:, nt * 128:(nt + 1) * 128], w2blk[1], w2blk[1][:], True, True)
                vop("tensor_copy", [po], [kcn], out=kcn[:], in_=po[:, 0:128])
                rms_heads(kcn, kcn[:], 2, 64, gvec[:, 64:128], kcnb, kcnb[:], 64 * EPS)
                for g in range(2):
                    tr(ptr, ptr[0:64, g * 128:(g + 1) * 128], kcnb, kcnb[:, g * 64:(g + 1) * 64], ident, ident[:])
                    vop("tensor_copy", [ptr], [kcT[g]], out=kcT[g][:, nt * 128:(nt + 1) * 128], in_=ptr[0:64, g * 128:(g + 1) * 128])
                    vop("tensor_copy", [po], [vca[g]], out=vca[g][:, nt, 0:64], in_=po[:, 128 + g * 64:128 + (g + 1) * 64])

            for g in range(2):
                acc = pacc[n_st[0] % 2]; n_st[0] += 1
                for nt in range(ntiles):
                    po = pst[nt % 2]
                    attn_tile(g, po, kcT[g], kcT[g][:, nt * 128:(nt + 1) * 128], qaug[g][0:64, :, :].rearrange("p h q -> p (h q)"),
                              [(maskc, maskc[:, j, nt, :])], vca[g], vca[g][:, nt, 0:65], 65, acc, nt == 0, nt == ntiles - 1,
                              accw=paccw, vw_ap=vca[g][:, nt, 65:129])
                ob = obr[0]
                vop("tensor_copy", [acc], [ob], out=ob[:], in_=acc[:].rearrange("p (h d) -> p h d", h=4)[:, :, 0:65])
                vop("tensor_scalar", [ob], [rden], out=rden[:, 0, :], in0=ob[:, :, 64], scalar1=1e-30, scalar2=None, op0=ALU.max)
                vop("reciprocal", [rden], [rden], out=rden[:, 0, :], in_=rden[:, 0, :])
                vop("tensor_tensor", [paccw, rden], [aimp], out=aimp[:], in0=paccw[:, 0:256].rearrange("p (h s) -> p h s", h=4),
                    in1=rden[:, 0, :].unsqueeze(2).to_broadcast([128, 4, 64]), op=ALU.mult)
                vop("tensor_reduce", [aimp], [score], out=score[:], in_=aimp[:].rearrange("p h s -> p s h"), axis=AX.X, op=ALU.add)
                vop("tensor_tensor", [score, adds], [score], out=score[:], in0=score[:], in1=adds[:, j, :], op=ALU.add)
                vop("max", [score], [mx8], out=mx8[:, 0:8], in_=score[:])
                vop("match_replace", [mx8, score], [swork], out=swork[:], in_to_replace=mx8[:, 0:8], in_values=score[:], imm_value=-3e30)
                vop("max", [swork], [mx8], out=mx8[:, 8:16], in_=swork[:])
                vop("tensor_scalar", [mx8], [thr], out=thr[:], in0=mx8[:, 15:16], scalar1=-1e29, scalar2=None, op0=ALU.max)
                vop("tensor_scalar", [score, thr], [swork], out=swork[:], in0=score[:], scalar1=thr[:, 0:1], scalar2=None, op0=ALU.is_lt)
                vop("tensor_scalar", [swork], [selb], out=selb[:, 64:128], in0=swork[:], scalar1=NEGB, scalar2=None, op0=ALU.mult)
                tr(ptr, ptr[:, 0:128], selb, selb[:], ident, ident[:])
                vop("tensor_copy", [ptr], [qaug[g]], out=qaug[g][64:128, :, :], in_=ptr[64:128, 0:128].unsqueeze(1).to_broadcast([64, 4, 128]))
                acc = pacc[n_st[0] % 2]; n_st[0] += 1
                for kt in range(L + 1):
                    po = pst[kt % 2]
                    masks = [(mask3, mask3[:, 0, :])] if kt == L else []
                    attn_tile(g, po, ksT[g], ksT[g][:, kt * 128:(kt + 1) * 128], qaug[g][:].rearrange("p h q -> p (h q)"),
                              masks, vsa[g], vsa[g][:, kt, :], 65, acc, kt == 0, kt == L)
                ob = obr[1]
                vop("tensor_copy", [acc], [ob], out=ob[:], in_=acc[:].rearrange("p (h d) -> p h d", h=4)[:, :, 0:65])
                acc = pacc[n_st[0] % 2]; n_st[0] += 1
                k0 = max(0, L - 4)
                for kt in range(k0, L + 1):
                    po = pst[kt % 2]
                    masks = []
                    if kt == L - 4:
                        masks.append((mask3, mask3[:, 1, :]))
                    if kt == L:
                        masks.append((mask3, mask3[:, 0, :]))
                    if kt == 0:
                        masks.append((mask3, mask3[:, 2, :]))
                    attn_tile(g, po, kwT[g], kwT[g][:, kt % 6, :], qaug[g][0:64, :, :].rearrange("p h q -> p (h q)"),
                              masks, vwa[g], vwa[g][:, kt % 6, :], 65, acc, kt == k0, kt == L)
                ob = obr[2]
                vop("tensor_copy", [acc], [ob], out=ob[:], in_=acc[:].rearrange("p (h d) -> p h d", h=4)[:, :, 0:65])
                for b_ in (1, 2):
                    vop("reciprocal", [obr[b_]], [rden], out=rden[:, b_, :], in_=obr[b_][:, :, 64])
                gv = gates[:, g * 12:(g + 1) * 12].rearrange("p (h t) -> p t h", h=4)
                vop("tensor_tensor", [rden, gates], [coef], out=coef[:], in0=rden[:], in1=gv, op=ALU.mult)
                og = onsa[:, g * 4:(g + 1) * 4, :]
                vop("tensor_tensor", [obr[0], coef], [onsa], out=og, in0=obr[0][:, :, 0:64], in1=coef[:, 0, :].unsqueeze(2).to_broadcast([128, 4, 64]), op=ALU.mult)
                for b_ in (1, 2):
                    vop("tensor_tensor", [obr[b_], coef], [otmp], out=otmp[:], in0=obr[b_][:, :, 0:64], in1=coef[:, b_, :].unsqueeze(2).to_broadcast([128, 4, 64]), op=ALU.mult)
                    vop("tensor_tensor", [onsa, otmp], [onsa], out=og, in0=og, in1=otmp[:], op=ALU.add)
            vop("tensor_tensor", [onsa, ga_s], [ymix], out=ymix[:, 0:512], in0=onsa[:].rearrange("p h d -> p (h d)"), in1=ga_s[:], op=ALU.mult)
            rms_heads(oret, oret[:].rearrange("p h d -> p (h d)"), 4, 128, gvec[:, 256:384], oret, oret[:].rearrange("p h d -> p (h d)"), 128 * EPS)
            vop("tensor_tensor", [oret, gr_s], [ymix], out=ymix[:, 512:1024], in0=oret[:].rearrange("p h d -> p (h d)"), in1=gr_s[:], op=ALU.mult)
            if os.environ.get('KS_DBGOUT', '0') == '1':
                dma(G, None, dbg_out[j * 128:(j + 1) * 128, :], ymix, ymix[:], "x")
            for kt in range(8):
                tr(ptr, ptr[:, kt * 128:(kt + 1) * 128], ymix, ymix[:, kt * 128:(kt + 1) * 128], ident, ident[:])
            vop("tensor_copy", [ptr], [yT], out=yT[:].rearrange("p k t -> p (k t)"), in_=ptr[:])
            ys = yst[0]
            for n in range(2):
                po = pin[n]
                for kt in range(8):
                    mm(po, po[:], yT, yT[:, kt, :], w_out_bf, w_out_bf[:, kt, n * 512:(n + 1) * 512], kt == 0, kt == 7)
                vop("tensor_tensor", [po, gate_bc], [ys], out=ys[:, n * 512:(n + 1) * 512], in0=po[:], in1=gate_bc[:, n * 512:(n + 1) * 512], op=ALU.mult)
                vop("tensor_tensor", [ys, x_t], [ys], out=ys[:, n * 512:(n + 1) * 512], in0=ys[:, n * 512:(n + 1) * 512], in1=x_t[:, n * 512:(n + 1) * 512], op=ALU.add)
            dma(S, None, y_out[j * 128:(j + 1) * 128, :], ys, ys[:], "oy%d" % (j % 2))


    try:
        for L in range(RUN_NT):
            _tile(L)
    except _Stop:
        pass

    P.emit(final=False)
    P.scope_pop()
    if sample and os.environ.get('KS_NOSAMPLE', '0') != '1':
        _sample_phase(locals())
    P.emit(final=True)
    _NC_CACHE['stats'] = P.stats
    _NC_CACHE['ops'] = [(o['eng'], o.get('waits', []), o.get('ev') if o['sig'] else None, o['dma']) for o in P.ops]
    es.close()
    return nc


def _sample_phase(E):
    nc, P = E["nc"], E["P"]
    sb, dma, vop, act, mm, tr, rms_heads = E["sb"], E["dma"], E["vop"], E["act"], E["mm"], E["tr"], E["rms_heads"]
    V, A, T, G, S = "dve", "act", "pe", "pool", "sp"
    w_in_bf, w_out_bf, ident, ident_f, ones_b = E["w_in_bf"], E["w_out_bf"], E["ident"], E["ident_f"], E["ones_b"]
    w1blk, w2blk, cbias, modA, modS, gvec, modT = E["w1blk"], E["w2blk"], E["cbias"], E["modA"], E["modS"], E["gvec"], E["modT"]
    ptr, pin, pst, pacc, sacc = E["ptr"], E["pin"], E["pst"], E["pacc"], E["paccw"]
    CG2 = "const2"
    f4 = lambda t: t[0:4]
    xs4 = sb("xs4", [4, D], F32); xs4b = sb("xs4b", [4, D], BF16)
    ss4 = sb("ss4", [4, 1], F32); rs4 = sb("rs4", [4, 1], F32)
    hTf = sb("hTf", [128, 8, 4], F32); hTs = sb("hTs", [128, 8, 4], BF16)
    sq4 = sb("sq4", [4, 512], F32); ssq4 = sb("ssq4", [4, 16], F32); rq4 = sb("rq4", [4, 16], F32)
    scr4 = (sq4, ssq4, rq4)
    qn_s = sb("qn_s", [4, 512], BF16)
    kvf_s = sb("kvf_s", [4, 512], F32); kwf_s = sb("kwf_s", [4, 280], F32)
    gates_s = sb("gates_s", [4, 24], BF16)
    ga4 = sb("ga4", [4, 512], BF16); gr4 = sb("gr4", [4, 512], BF16)
    rope_s = sb("rope_s", [4, 4, 32], F32)
    rot_s = sb("rot_s", [4, 2, 4, 64], F32); t1 = sb("t1s", [4, 2, 4, 32], F32); t2 = sb("t2s", [4, 2, 4, 32], F32)
    rot_b = sb("rot_b", [4, 4, 64], BF16)
    v_s = sb("v_s", [4, 4, 128], F32)
    qblk = sb("qblk", [128, 4, 8], BF16)
    gaT = sb("gaT", [128, 4, 4], F32)
    gbc = sb("gbc", [128, 4, 24], F32)
    oh4 = sb("oh4", [4, 4, 128], BF16); oh4f = sb("oh4f", [4, 4, 128], F32)
    ohq = sb("ohq", [64, 4, 4], F32)
    pcol = sb("pcol", [128, 6], F32)
    wfull = sb("wfull", [128, 8, 256], BF16)
    topc = sb("topc", [8, 2, 256], F32)
    Sst = sb("Sst", [64, 4, 4, 128], F32); Sbf = sb("Sbf", [64, 4, 4, 128], BF16)
    kT_s = sb("kT_s", [64, 4, 4], F32); qT2 = sb("qT2", [64, 4, 4], F32); qTm = sb("qTm", [64, 4, 4, 4], BF16)
    qk = sb("qk", [4, 4], F32); prod = sb("prod", [4, 4, 64], F32)
    ors = sb("ors", [4, 4, 128], F32); yr4 = sb("yr4", [4, 512], BF16)
    ymT = sb("ymT", [128, 8, 4], BF16)
    ys4 = sb("ys4", [4, D], F32)
    xT = [sb("xTc%d" % i, [128, 16, 1 + 31 * 8], BF16) for i in range(2)]
    hS = [sb("hS%d" % i, [128, 1024], BF16) for i in range(2)]
    pgt = [sb("pgt%d" % i, [128, 256], F32) for i in range(4)]
    pgb = [sb("pgb%d" % i, [128, 256], BF16) for i in range(2)]
    ptb = sb("ptb", [128, 128], I32); ptf = sb("ptf", [128, 128], F32); idxb = sb("idxb", [128, 128], I32)
    kcn_s = sb("kcn_s", [128, 128], F32); kcb_s = sb("kcb_s", [128, 128], BF16); kcT_s = sb("kcT_s", [128, 128], BF16)
    sqs = sb("sqs", [128, 128], F32); ssqs = sb("ssqs", [128, 4], F32); rqs = sb("rqs", [128, 4], F32)
    Vd = sb("Vd", [128, 2, 2, 64], BF16)
    E8 = sb("E8", [128, 8], BF16)
    wt = sb("wt", [128, 4, 256], F32); wtb = sb("wtb", [128, 4, 256], BF16)
    res = sb("res", [128, 256], F32)
    rdn = sb("rdn", [128, 3, 32], F32); cf = sb("cf", [128, 3, 32], F32); comb = sb("comb", [128, 32], F32); ctmp = sb("ctmp", [128, 32], F32)
    impT = sb("impT", [128, 2, 8], F32); atn = sb("atn", [128, 2, 32], F32)
    sc8 = sb("sc8", [8, 256], F32); sw8 = sb("sw8", [8, 256], F32); mx = sb("mx16", [8, 16], F32); th8 = sb("th8", [8, 1], F32)
    sv = sb("sv", [8, 16], F32)
    s128 = sb("s128", [128, 1], F32); h128 = sb("h128", [128, 1], F32); d128 = sb("d128", [128, 1], F32); i128 = sb("i128", [128, 1], I32)
    pg128 = sb("pg128", [128, 1], I32); pf128 = sb("pf128", [128, 1], F32)
    rbb = sb("rbb", [128, 8, 8], F32); gidx = sb("gidx", [128, 8, 8], I32)
    T_scr1 = TW(None, "scr1"); T_scr2 = TW(None, "scr2")
    scr1, scr2 = E["scr1"], E["scr2"]
    cache_rows = [E["cache_cmp"], E["cache_slc"]]

    dma(S, xs4, xs4[:], None, E["xs_in"], CG2)
    dma(S, rope_s, rope_s[:], None, E["c_rope_s"], CG2)
    dma(S, oh4f, oh4f[:], None, E["c_oh4"], CG2)
    dma(G, oh4, oh4[:], None, E["c_oh4"], CG2)
    dma(S, ohq, ohq[:], None, E["c_ohq"], CG2)
    dma(S, pcol, pcol[:], None, E["c_pcol"], CG2)
    dma(G, wfull, wfull[:], None, E["c_wfull"], CG2)
    dma(S, topc, topc[:], None, E["c_top"], CG2)
    for b in range(4):
        dma(S, Sst, Sst[:, b], None, E["sret_in"][b].rearrange("h k v -> k h v"), CG2)
    for b in range(4):
        P.op(S, (lambda b=b: nc.sync.dma_start(out=E["swin_out"][b, 0:511, :], in_=E["swin_in"][b, 1:512, :])), [], [], dma="d2d")

    gate_s = sb("gate_s", [4, D], F32)
    for n in range(2):
        po = pacc[n]
        for k4 in range(4):
            kt = n * 4 + k4
            P.op(T, (lambda po=po, kt=kt, k4=k4: nc.tensor.transpose(out=po[0:4, k4 * 128:(k4 + 1) * 128], in_=modT[:, 16 + kt, 1:5], identity=ident_f[:])), [modT, ident_f], [po])
        vop("tensor_copy", [po], [gate_s], out=gate_s[:, n * 512:(n + 1) * 512], in_=po[0:4, :])
    vop("memset", [], [ss4], eng=G, ap=ss4[:], constant=0.0)
    act(xs4b, xs4b[:], xs4, xs4[:], AF.Square, accum_out=ss4[:], extra_writes=[ss4])
    vop("tensor_scalar", [ss4], [rs4], out=rs4[:], in0=ss4[:], scalar1=1.0 / D, scalar2=EPS, op0=ALU.mult, op1=ALU.add)
    act(rs4, rs4[:], rs4, rs4[:], AF.Sqrt)
    vop("reciprocal", [rs4], [rs4], out=rs4[:], in_=rs4[:])
    P.op(A, lambda: nc.scalar.mul(out=xs4b[:], in_=xs4[:], mul=rs4[:, 0:1]), [xs4, rs4], [xs4b])
    for kt in range(8):
        tr(ptr, ptr[:, kt * 4:(kt + 1) * 4], xs4b, xs4b[:, kt * 128:(kt + 1) * 128], ident, ident[0:4, 0:4])
    vop("tensor_tensor", [ptr, modA], [hTf], out=hTf[:], in0=ptr[:, 0:32].rearrange("p (k t) -> p k t", k=8), in1=modA[:, :, 1:5], op=ALU.mult)
    vop("tensor_tensor", [hTf, modS], [hTs], out=hTs[:], in0=hTf[:], in1=modS[:, :, 1:5], op=ALU.add)

    def inproj(c0, cw, po):
        for kt in range(8):
            mm(po, po[0:4, 0:cw], hTs, hTs[:, kt, :], w_in_bf, w_in_bf[:, kt, c0:c0 + cw], kt == 0, kt == 7)

    po = pin[0]; inproj(O_Q, 512, po)
    rms_heads(po, po[0:4, :], 8, 64, gvec[0:4, 0:64], qn_s, qn_s[:], 64 * EPS, np_=4, scr=scr4)
    po = pin[1]; inproj(O_KC, 512, po)
    vop("tensor_copy", [po], [kvf_s], out=kvf_s[:], in_=po[0:4, :])
    rms_heads(kvf_s, kvf_s[:, 256:384], 2, 64, gvec[0:4, 128:192], kvf_s, kvf_s[:, 256:384], 64 * EPS, np_=4, scr=scr4)
    dma(S, None, E["scmp_out"], kvf_s, kvf_s[:, 0:256], "x")
    dma(S, None, E["sslc_out"], kvf_s, kvf_s[:, 256:512], "x")
    po = pin[0]; inproj(O_KW, 280, po)
    vop("tensor_copy", [po], [kwf_s], out=kwf_s[:], in_=po[0:4, 0:280])
    rms_heads(kwf_s, kwf_s[:, 0:128], 2, 64, gvec[0:4, 192:256], kwf_s, kwf_s[:, 0:128], 64 * EPS, np_=4, scr=scr4)
    act(gates_s, gates_s[:], kwf_s, kwf_s[:, 256:280], AF.Sigmoid)
    for b in range(4):
        dma(S, None, E["swin_out"][b, 511:512, :], kwf_s, kwf_s[b:b + 1, 0:256], "x")
    po = pin[1]; inproj(O_GA, 512, po)
    act(ga4, ga4[:], po, po[0:4, :], AF.Silu)
    po = pin[0]; inproj(O_GR, 512, po)
    act(gr4, gr4[:], po, po[0:4, :], AF.Silu)
    po = pin[1]; inproj(O_RV, 512, po)
    vop("tensor_copy", [po], [v_s], out=v_s[:].rearrange("p h d -> p (h d)"), in_=po[0:4, :])
    po = pin[0]; inproj(O_RQ, 512, po)
    for s_ in range(2):
        xv = po[0:4, s_ * 256:(s_ + 1) * 256].rearrange("p (h d) -> p h d", h=4)
        cs = rope_s[:, 2 * s_, :].unsqueeze(1).to_broadcast([4, 4, 32])
        sn = rope_s[:, 2 * s_ + 1, :].unsqueeze(1).to_broadcast([4, 4, 32])
        vop("tensor_tensor", [po, rope_s], [t1], out=t1[:, s_], in0=xv[:, :, 0:32], in1=cs, op=ALU.mult)
        vop("tensor_tensor", [po, rope_s], [t2], out=t2[:, s_], in0=xv[:, :, 32:64], in1=sn, op=ALU.mult)
        vop("tensor_tensor", [t1, t2], [rot_s], out=rot_s[:, s_, :, 0:32], in0=t1[:, s_], in1=t2[:, s_], op=ALU.subtract)
        vop("tensor_tensor", [po, rope_s], [t1], out=t1[:, s_], in0=xv[:, :, 0:32], in1=sn, op=ALU.mult)
        vop("tensor_tensor", [po, rope_s], [t2], out=t2[:, s_], in0=xv[:, :, 32:64], in1=cs, op=ALU.mult)
        vop("tensor_tensor", [t1, t2], [rot_s], out=rot_s[:, s_, :, 32:64], in0=t1[:, s_], in1=t2[:, s_], op=ALU.add)

    vop("memset", [], [qblk], eng=G, ap=qblk[:], constant=0.0)
    for i in range(4):
        for g in range(2):
            c0 = g * 256 + i * 64
            tr(ptr, ptr[g * 64:(g + 1) * 64, 64 + i * 4:64 + (i + 1) * 4], qn_s, qn_s[:, c0:c0 + 64], ident, ident[0:4, 0:4])
    qv = ptr[:, 64:80].rearrange("p (i b) -> p b i", i=4)
    vop("tensor_copy", [ptr], [qblk], out=qblk[0:64, :, 0:4], in_=qv[0:64])
    vop("tensor_copy", [ptr], [qblk], out=qblk[64:128, :, 4:8], in_=qv[64:128])
    for kt in range(4):
        tr(ptr, ptr[:, 96 + kt * 4:96 + (kt + 1) * 4], ga4, ga4[:, kt * 128:(kt + 1) * 128], ident, ident[0:4, 0:4])
    vop("tensor_copy", [ptr], [gaT], out=gaT[:].rearrange("p k b -> p (k b)"), in_=ptr[:, 96:112])
    po = pacc[0]
    for b in range(4):
        mm(po, po[:, b * 24:(b + 1) * 24], oh4, oh4[:, b, :], gates_s, gates_s[:], True, True)
    vop("tensor_copy", [po], [gbc], out=gbc[:].rearrange("p b t -> p (b t)"), in_=po[:, 0:96])

    vop("tensor_copy", [Sst], [Sbf], out=Sbf[:], in_=Sst[:])
    DBG3 = os.environ.get('KS_DBGOUT', '0') == '1'
    if DBG3:
        dma(S, None, E["dbg3_out"][0:64, 0:512], Sst, Sst[:, 0].rearrange("p h v -> p (h v)"), "x")
        dma(S, None, E["dbg3_out"][0:4, 512:1024], rot_s, rot_s[:].rearrange("p s h d -> p (s h d)"), "x")
        dma(S, None, E["dbg3_out"][0:4, 1024:1536], v_s, v_s[:].rearrange("p h d -> p (h d)"), "x")
    rot_b2 = sb("rot_b2", [4, 2, 4, 64], BF16)
    vop("tensor_copy", [rot_s], [rot_b2], out=rot_b2[:], in_=rot_s[:])
    for h in range(4):
        tr(ptr, ptr[0:64, 160 + h * 4:160 + (h + 1) * 4], rot_b2, rot_b2[:, 1, h, :], ident, ident[0:4, 0:4])
        tr(ptr, ptr[0:64, 176 + h * 4:176 + (h + 1) * 4], rot_b2, rot_b2[:, 0, h, :], ident, ident[0:4, 0:4])
    vop("tensor_copy", [ptr], [kT_s], out=kT_s[:].rearrange("p h b -> p (h b)"), in_=ptr[0:64, 160:176])
    vop("tensor_copy", [ptr], [qT2], out=qT2[:].rearrange("p h b -> p (h b)"), in_=ptr[0:64, 176:192])
    vop("tensor_tensor", [qT2, ohq], [qTm], out=qTm[:], in0=qT2[:].unsqueeze(1).to_broadcast([64, 4, 4, 4]),
        in1=ohq[:].unsqueeze(2).to_broadcast([64, 4, 4, 4]), op=ALU.mult)
    po = pst[0]
    first = True
    for h in range(4):
        for b in range(4):
            mm(po, po[0:4, h * 128:(h + 1) * 128], qTm, qTm[:, b, h, :], Sbf, Sbf[:, b, h, :], first, (h == 3 and b == 3), sgc=True)
            first = False
    if DBG3:
        dma(S, None, E["dbg3_out"][0:64, 1536:1552], kT_s, kT_s[:].rearrange("p h b -> p (h b)"), "x")
        dma(S, None, E["dbg3_out"][0:64, 1552:1568], qT2, qT2[:].rearrange("p h b -> p (h b)"), "x")
    vop("tensor_tensor", [rot_s], [prod], out=prod[:], in0=rot_s[:, 0], in1=rot_s[:, 1], op=ALU.mult)
    vop("tensor_reduce", [prod], [qk], out=qk[:], in_=prod[:], axis=AX.X, op=ALU.add)
    vop("tensor_tensor", [v_s, qk], [ors], out=ors[:], in0=v_s[:], in1=qk[:].unsqueeze(2).to_broadcast([4, 4, 128]), op=ALU.mult)
    for h in range(4):
        gam = float(1.0 - 2.0 ** (-5.0 - h))
        vop("scalar_tensor_tensor", [po, ors], [ors], out=ors[:, h, :], in0=po[0:4, h * 128:(h + 1) * 128], scalar=gam, in1=ors[:, h, :], op0=ALU.mult, op1=ALU.add)
    for b in range(4):
        pv = pst[1]
        P.op(T, (lambda b=b, pv=pv: nc.tensor.matmul(pv[0:64, :], lhsT=oh4f[:, b, 0:64], rhs=v_s[:].rearrange("p h d -> p (h d)"), start=True, stop=True)), [oh4f, v_s], [pv])
        for h in range(4):
            gam = float(1.0 - 2.0 ** (-5.0 - h))
            vop("tensor_scalar", [Sst], [Sst], out=Sst[:, b, h, :], in0=Sst[:, b, h, :], scalar1=gam, scalar2=None, op0=ALU.mult)
            vop("scalar_tensor_tensor", [pv, kT_s, Sst], [Sst], out=Sst[:, b, h, :], in0=pv[0:64, h * 128:(h + 1) * 128], scalar=kT_s[:, h, b:b + 1], in1=Sst[:, b, h, :], op0=ALU.mult, op1=ALU.add)
    for b in range(4):
        dma(S, None, E["sret_out"][b].rearrange("h k v -> k h v"), Sst, Sst[:, b], "x")
    rms_heads(ors, ors[:].rearrange("p h d -> p (h d)"), 4, 128, gvec[0:4, 256:384], ors, ors[:].rearrange("p h d -> p (h d)"), 128 * EPS, np_=4, scr=scr4)
    vop("tensor_tensor", [ors, gr4], [yr4], out=yr4[:], in0=ors[:].rearrange("p h d -> p (h d)"), in1=gr4[:], op=ALU.mult)
    for kt in range(4):
        tr(ptr, ptr[:, 128 + kt * 4:128 + (kt + 1) * 4], yr4, yr4[:, kt * 128:(kt + 1) * 128], ident, ident[0:4, 0:4])
    vop("tensor_copy", [ptr], [ymT], out=ymT[:, 4:8, :].rearrange("p k b -> p (k b)"), in_=ptr[:, 128:144])

    sfirst = [True]

    def smm(col, n, l_t, l_ap, r_t, r_ap):
        mm(sacc, sacc[:, col:col + n], l_t, l_ap, r_t, r_ap, sfirst[0], False, sgc=True)
        sfirst[0] = False

    OC = lambda br: br * 64

    def key_tile(br, b, kT_t, kT_ap, vsrc_t, vsrc_ap, cols, bias_ap, po):
        mm(po, po[:, 0:8], kT_t, kT_ap, qblk, qblk[:, b, :], True, True)
        if bias_ap is None:
            act(E8, E8[:], po, po[:, 0:8], AF.Exp)
        else:
            act(E8, E8[:], po, po[:, 0:8], AF.Exp, extra_reads=[pcol], bias=bias_ap)
        vop("tensor_copy", [vsrc_t], [Vd], out=Vd[:], in_=vsrc_ap.rearrange("p (g d) -> p g d", g=2).unsqueeze(2).to_broadcast([128, 2, 2, 64]))
        for g in cols:
            smm(OC(br) + b * 8 + g * 4, 4, Vd, Vd[:, g].rearrange("p r d -> p (r d)"), E8, E8[:, g * 4:(g + 1) * 4])
            smm(OC(br) + 32 + b * 8 + g * 4, 4, ones_b, ones_b[:], E8, E8[:, g * 4:(g + 1) * 4])

    for b in range(4):
        dma(S, wt, wt[:], None, E["swin_in"][b].rearrange("(t p) c -> p t c", p=128), "x")
        dma(S, wt, wt[0:1, 0, :], kwf_s, kwf_s[b:b + 1, 0:256], "x")
        for t in range(4):
            pp = pst[t % 2]
            P.op(T, (lambda pp=pp, t=t: nc.tensor.transpose(out=pp[:, 0:128], in_=wt[:, t, 0:128], identity=ident_f[:])), [wt, ident_f], [pp])
            vop("tensor_copy", [pp], [kcT_s], out=kcT_s[:], in_=pp[:, 0:128])
            key_tile(2, b, kcT_s, kcT_s[:], wt, wt[:, t, 128:256], (0, 1), None, pin[t % 2])

    CH = [31, 31, 31, 31, 4]
    npg = [0]
    idx_all = sb("idx_all", [128, 4, 128], I32)
    for b in range(4):
        dma(S, ptb, ptb[:], None, E["ptc"][b:b + 1, :].to_broadcast([128, 128]), "x")
        vop("tensor_copy", [ptb], [ptf], out=ptf[:], in_=ptb[:])
        vop("tensor_scalar", [ptf], [ptf], out=ptf[:], in0=ptf[:], scalar1=128.0, scalar2=None, op0=ALU.mult)
        vop("tensor_tensor", [ptf, pcol], [ptf], out=ptf[:], in0=ptf[:], in1=pcol[:, 0:1].to_broadcast([128, 128]), op=ALU.add)
        vop("tensor_copy", [ptf], [idx_all], out=idx_all[:, b, :], in_=ptf[:])
    PF = 3
    pages = [(b, j) for b in range(4) for j in range(128)]
    issued = []

    def issue_next():
        k = len(issued)
        if k >= len(pages):
            return
        b_, j_ = pages[k]
        pt_ = pgt[k % 4]
        P.op(G, (lambda pt_=pt_, b_=b_, j_=j_: nc.gpsimd.indirect_dma_start(out=pt_[:], out_offset=None, in_=cache_rows[0],
             in_offset=bass.IndirectOffsetOnAxis(ap=idx_all[:, b_, j_:j_ + 1], axis=0))), [idx_all], [pt_], dma=pt_.name)
        issued.append(pt_)

    for _ in range(PF):
        issue_next()
    kpage = 0
    for b in range(4):
        for i in range(2):
            vop("memset", [], [xT[i]], eng=G, ap=xT[i][:, :, 0:1], constant=0.0)
        j0 = 0
        for c, npages in enumerate(CH):
            for jj in range(npages):
                pt_ = issued[kpage]
                issue_next()
                pp = pst[kpage % 2]
                kpage += 1
                for i in range(2):
                    P.op(T, (lambda pp=pp, pt_=pt_, i=i: nc.tensor.transpose(out=pp[:, i * 128:(i + 1) * 128], in_=pt_[:, i * 128:(i + 1) * 128], identity=ident_f[:])), [pt_, ident_f], [pp])
                for i in range(2):
                    vop("tensor_copy", [pp], [xT[i]], out=xT[i][:, :, 1 + jj * 8:1 + (jj + 1) * 8], in_=pp[:, i * 128:(i + 1) * 128].rearrange("p (m r) -> p r m", r=16))
            R0 = j0 * 128
            nblk = npages * 8
            nb0 = R0 // 16 - 1
            c_lo = 1 if c == 0 else 0
            for i in range(2):
                po = pin[i]
                nn = nblk - c_lo
                for l in range(32):
                    a_, r_ = l // 16, l % 16
                    mm(po, po[:, 0:nn], w1blk[i], w1blk[i][:, l, :], xT[i], xT[i][:, r_, c_lo + a_:c_lo + a_ + nn], l == 0, l == 31)
                act(hS[i], hS[i][:, nb0 + c_lo:nb0 + nblk], po, po[:, 0:nn], AF.Silu, extra_reads=[cbias[i]], bias=cbias[i][:, 0:1])
                vop("tensor_copy", [xT[i]], [xT[i]], out=xT[i][:, :, 0:1], in_=xT[i][:, :, npages * 8:npages * 8 + 1])
            j0 += npages
        for i in range(2):
            vop("memset", [], [hS[i]], eng=G, ap=hS[i][:, 1023:1024], constant=0.0)
        for nt in range(8):
            po = pin[nt % 2]
            mm(po, po[:, 0:128], hS[0], hS[0][:, nt * 128:(nt + 1) * 128], w2blk[0], w2blk[0][:], True, True)
            mm(po, po[:, 128:256], hS[1], hS[1][:, nt * 128:(nt + 1) * 128], w2blk[1], w2blk[1][:], True, True)
            vop("tensor_copy", [po], [kcn_s], out=kcn_s[:], in_=po[:, 0:128])
            rms_heads(kcn_s, kcn_s[:], 2, 64, gvec[:, 64:128], kcb_s, kcb_s[:], 64 * EPS, np_=128, scr=(sqs, ssqs, rqs))
            tr(ptr, ptr[:, 768:896], kcb_s, kcb_s[:], ident, ident[:])
            vop("tensor_copy", [ptr], [kcT_s], out=kcT_s[:], in_=ptr[:, 768:896])
            ps2 = pst[nt % 2]
            mm(ps2, ps2[:, 0:8], kcT_s, kcT_s[:], qblk, qblk[:, b, :], True, True)
            if nt == 7:
                act(E8, E8[:], ps2, ps2[:, 0:8], AF.Exp, extra_reads=[pcol], bias=pcol[:, 4:5])
            else:
                act(E8, E8[:], ps2, ps2[:, 0:8], AF.Exp)
            vop("tensor_copy", [po], [Vd], out=Vd[:], in_=po[:, 128:256].rearrange("p (g d) -> p g d", g=2).unsqueeze(2).to_broadcast([128, 2, 2, 64]))
            for g in range(2):
                smm(OC(0) + b * 8 + g * 4, 4, Vd, Vd[:, g].rearrange("p r d -> p (r d)"), E8, E8[:, g * 4:(g + 1) * 4])
            smm(OC(0) + 32 + b * 8, 8, ones_b, ones_b[:], E8, E8[:])
            for st in range(2):
                smm(192 + st * 32 + b * 8, 8, wfull, wfull[:, nt, st * 128:(st + 1) * 128], E8, E8[:])

    vop("tensor_copy", [sacc], [res], out=res[:], in_=sacc[:, 0:256])
    vop("reciprocal", [res], [rdn], out=rdn[:, 0, :], in_=res[:, 32:64])
    vop("tensor_tensor", [res, rdn], [atn], out=atn[:], in0=res[:, 192:256].rearrange("p (s c) -> p s c", s=2),
        in1=rdn[:, 0, :].unsqueeze(1).to_broadcast([128, 2, 32]), op=ALU.mult)
    vop("tensor_reduce", [atn], [impT], out=impT[:].rearrange("p s c -> p (s c)"), in_=atn[:].rearrange("p s (c i) -> p (s c) i", i=4), axis=AX.X, op=ALU.add)
    po = pacc[0]
    for st in range(2):
        P.op(T, (lambda st=st, po=po: nc.tensor.transpose(out=po[0:8, st * 128:(st + 1) * 128], in_=impT[:, st, :], identity=ident_f[:])), [impT, ident_f], [po])
    vop("tensor_tensor", [po, topc], [sc8], out=sc8[:], in0=po[0:8, 0:256], in1=topc[:, 0, :], op=ALU.add)
    vop("max", [sc8], [mx], out=mx[:, 0:8], in_=sc8[:])
    vop("match_replace", [mx, sc8], [sw8], out=sw8[:], in_to_replace=mx[:, 0:8], in_values=sc8[:], imm_value=-3e30)
    vop("max", [sw8], [mx], out=mx[:, 8:16], in_=sw8[:])
    vop("tensor_scalar", [sc8, mx], [sw8], out=sw8[:], in0=sc8[:], scalar1=mx[:, 14:15], scalar2=None, op0=ALU.is_ge)
    vop("tensor_tensor", [sw8, topc], [sw8], out=sw8[:], in0=sw8[:], in1=topc[:, 1, :], op=ALU.mult)
    vop("max", [sw8], [sv], out=sv[:, 0:8], in_=sw8[:])
    vop("match_replace", [sv, sw8], [sc8], out=sc8[:], in_to_replace=sv[:, 0:8], in_values=sw8[:], imm_value=0.0)
    vop("max", [sc8], [sv], out=sv[:, 8:16], in_=sc8[:])
    vop("tensor_scalar", [sv], [sv], out=sv[:], in0=sv[:], scalar1=-1.0, scalar2=0.0, op0=ALU.add, op1=ALU.max)
    P.op(S, lambda: nc.sync.dma_start(out=scr1.rearrange("(a k) o -> a (k o)", a=8), in_=sv[:]), [sv], [T_scr1], dma="scr1")
    P.op(S, lambda: nc.sync.dma_start(out=s128[:], in_=scr1), [T_scr1], [s128], dma="s128")
    vop("tensor_scalar", [s128], [d128], out=d128[:], in0=s128[:], scalar1=0.5, scalar2=-0.25, op0=ALU.mult, op1=ALU.add)
    vop("tensor_scalar", [d128], [d128], out=d128[:], in0=d128[:], scalar1=8388608.0, scalar2=None, op0=ALU.add)
    vop("tensor_scalar", [d128], [d128], out=d128[:], in0=d128[:], scalar1=-8388608.0, scalar2=None, op0=ALU.add)
    vop("tensor_scalar", [d128], [h128], out=h128[:], in0=d128[:], scalar1=-2.0, scalar2=None, op0=ALU.mult)
    vop("tensor_tensor", [s128, h128], [h128], out=h128[:], in0=s128[:], in1=h128[:], op=ALU.add)
    vop("tensor_tensor", [d128, pcol], [d128], out=d128[:], in0=d128[:], in1=pcol[:, 2:3], op=ALU.add)
    vop("tensor_copy", [d128], [i128], out=i128[:], in_=d128[:])
    P.op(G, lambda: nc.gpsimd.indirect_dma_start(out=pg128[:], out_offset=None, in_=E["ptc"].rearrange("b (j o) -> (b j) o", o=1),
         in_offset=bass.IndirectOffsetOnAxis(ap=i128[:, 0:1], axis=0)), [i128], [pg128], dma="pg128")
    vop("tensor_copy", [pg128], [pf128], out=pf128[:], in_=pg128[:])
    vop("tensor_scalar", [pf128], [pf128], out=pf128[:], in0=pf128[:], scalar1=128.0, scalar2=None, op0=ALU.mult)
    vop("tensor_scalar", [h128], [h128], out=h128[:], in0=h128[:], scalar1=64.0, scalar2=None, op0=ALU.mult)
    vop("tensor_tensor", [pf128, h128], [pf128], out=pf128[:], in0=pf128[:], in1=h128[:], op=ALU.add)
    P.op(S, lambda: nc.sync.dma_start(out=scr2, in_=pf128[:]), [pf128], [T_scr2], dma="scr2")
    s2v = scr2.rearrange("(a t h) o -> h a (t o)", a=8, t=8, h=2)
    for hh in range(2):
        P.op(S, (lambda hh=hh: nc.sync.dma_start(out=rbb[hh * 64:(hh + 1) * 64], in_=s2v[hh:hh + 1].to_broadcast([64, 8, 8]), allow_slow_non_contiguous=True)), [T_scr2], [rbb], dma="rbb%d" % hh)
    vop("tensor_tensor", [rbb, pcol], [rbb], out=rbb[:], in0=rbb[:], in1=pcol[:, 1:2].unsqueeze(2).to_broadcast([128, 8, 8]), op=ALU.add)
    vop("tensor_copy", [rbb], [gidx], out=gidx[:], in_=rbb[:])

    tiles = [(b, g, t) for b in range(4) for g in range(2) for t in range(8)]
    sissued = []

    def sissue_next():
        k = len(sissued)
        if k >= len(tiles):
            return
        b_, g_, t_ = tiles[k]
        pt_ = pgt[k % 4]
        P.op(G, (lambda pt_=pt_, bg=b_ * 2 + g_, t_=t_: nc.gpsimd.indirect_dma_start(out=pt_[:], out_offset=None, in_=cache_rows[1],
             in_offset=bass.IndirectOffsetOnAxis(ap=gidx[:, bg, t_:t_ + 1], axis=0))), [gidx], [pt_], dma=pt_.name)
        if t_ == 7:
            dma(S, pt_, pt_[64:65, :], kvf_s, kvf_s[b_:b_ + 1, 256:512], "x")
        sissued.append(pt_)

    for _ in range(PF):
        sissue_next()
    for k, (b, g, t) in enumerate(tiles):
        pt_ = sissued[k]
        sissue_next()
        pp = pst[k % 2]
        P.op(T, (lambda pp=pp, pt_=pt_: nc.tensor.transpose(out=pp[:, 0:128], in_=pt_[:, 0:128], identity=ident_f[:])), [pt_, ident_f], [pp])
        vop("tensor_copy", [pp], [kcT_s], out=kcT_s[:], in_=pp[:, 0:128])
        key_tile(1, b, kcT_s, kcT_s[:], pt_, pt_[:, 128:256], (g,), (pcol[:, 3:4] if t == 7 else None), pin[k % 2])

    vop("tensor_copy", [sacc], [res], out=res[:, 0:192], in_=sacc[:, 0:192])
    for br in range(3):
        vop("reciprocal", [res], [rdn], out=rdn[:, br, :], in_=res[:, br * 64 + 32:br * 64 + 64])
    vop("tensor_tensor", [rdn, gbc], [cf], out=cf[:].rearrange("p r (b c) -> p r b c", b=4), in0=rdn[:].rearrange("p r (b c) -> p r b c", b=4),
        in1=gbc[:].rearrange("p b (c r) -> p r b c", r=3), op=ALU.mult)
    vop("tensor_tensor", [res, cf], [comb], out=comb[:], in0=res[:, 0:32], in1=cf[:, 0, :], op=ALU.mult)
    for br in (1, 2):
        vop("tensor_tensor", [res, cf], [ctmp], out=ctmp[:], in0=res[:, br * 64:br * 64 + 32], in1=cf[:, br, :], op=ALU.mult)
        vop("tensor_tensor", [comb, ctmp], [comb], out=comb[:], in0=comb[:], in1=ctmp[:], op=ALU.add)
    cv = comb[:].rearrange("p (b g i r) -> p r g i b", b=4, g=2, i=2, r=2)
    for hf in range(2):
        ps_ = slice(hf * 64, (hf + 1) * 64)
        vop("tensor_tensor", [comb, gaT], [ymT], out=ymT[ps_, 0:4, :].rearrange("p (g i) b -> p g i b", g=2), in0=cv[ps_, hf],
            in1=gaT[ps_, :, :].rearrange("p (g i) b -> p g i b", g=2), op=ALU.mult)
    if os.environ.get('KS_DBGOUT', '0') == '1':
        dma(G, None, E["dbg2_out"][:, 0:32], ymT, ymT[:].rearrange("p k b -> p (k b)"), "x")
        dma(S, None, E["dbg2_out"][:, 32:64], comb, comb[:], "x")
    for n in range(2):
        po = pin[n]
        for kt in range(8):
            mm(po, po[0:4, :], ymT, ymT[:, kt, :], w_out_bf, w_out_bf[:, kt, n * 512:(n + 1) * 512], kt == 0, kt == 7)
        vop("tensor_tensor", [po, gate_s], [ys4], out=ys4[:, n * 512:(n + 1) * 512], in0=po[0:4, :], in1=gate_s[:, n * 512:(n + 1) * 512], op=ALU.mult)
        vop("tensor_tensor", [ys4, xs4], [ys4], out=ys4[:, n * 512:(n + 1) * 512], in0=ys4[:, n * 512:(n + 1) * 512], in1=xs4[:, n * 512:(n + 1) * 512], op=ALU.add)
    dma(S, None, E["ys_out"], ys4, ys4[:], "x")


def _consts(p):
    sh = 1 - p
    c = {}
    c["c_ident"] = np.eye(128, dtype=np.float32)
    n = np.arange(NT * 128)
    c["c_expand"] = (n[None, :] // 64 == np.arange(64)[:, None]).astype(np.float32)
    kk = np.arange(128)[:, None]
    qq = np.arange(128)[None, :]
    m3 = np.zeros((128, 3, 128), np.float32)
    m3[:, 0, :] = np.where(kk <= qq, 0.0, NEGB)
    m3[:, 1, :] = np.where(kk > qq, 0.0, NEGB)
    m3[:, 2, :] = NEGB if p == 0 else 0.0
    c["c_mask3"] = m3
    mc = np.zeros((128, 16, 2, 128), np.float32)
    for j in range(16):
        L = 2 * j + 1
        t = L * 128 + np.arange(128)[None, :]
        for nt in range(2):
            nn = nt * 128 + np.arange(128)[:, None]
            ok = (16 * nn + 31 <= t) & (nn >= 8 * sh)
            mc[:, j, nt, :] = np.where(ok, 0.0, NEGB)
    c["c_maskc"] = mc
    ad = np.zeros((128, 16, 64), np.float32)
    for j in range(16):
        L = 2 * j + 1
        t = L * 128 + np.arange(128)[:, None]
        s = np.arange(64)[None, :]
        jt = t // 64
        valid = (s * 64 <= t) & (s >= 2 * sh)
        forced = (s == 2 * sh) | (s == jt) | (s == jt - 1)
        ad[:, j, :] = np.where(valid, np.where(forced, 1e4, 0.0), -1e30)
    c["c_adds"] = ad
    half = 32
    freqs = (10000.0 ** (-np.arange(half, dtype=np.float32) / half)).astype(np.float32)
    rp = np.zeros((128, NT, 4, 32), np.float32)
    rs = np.zeros((128, NT, 8), np.float32)
    lg = np.log(1.0 - 2.0 ** (-5.0 - np.arange(4, dtype=np.float32))).astype(np.float32)
    i = np.arange(128, dtype=np.float32)
    for L in range(NT):
        gt = L - sh
        if gt < 0:
            continue
        pos = (gt * 128 + np.arange(128)).astype(np.float32)
        ang = pos[:, None] * freqs[None, :]
        rp[:, L, 0] = np.cos(ang); rp[:, L, 1] = np.sin(ang)
        rp[:, L, 2] = np.cos(ang) / 8.0; rp[:, L, 3] = np.sin(ang) / 8.0
        rs[:, L, 0:4] = np.exp(lg[None, :] * (127.0 - i[:, None]))
        rs[:, L, 4:8] = np.exp(lg[None, :] * (i[:, None] + 1.0))
    c["c_rope"] = rp
    c["c_rscal"] = rs
    w = np.zeros((255 + 1, 64), np.float32)
    for s in range(64):
        for nn, wt in ((4 * s - 1, 1), (4 * s, 2), (4 * s + 1, 2), (4 * s + 2, 2), (4 * s + 3, 1)):
            if 0 <= nn < 255:
                w[nn, s] += wt
    wl = np.zeros((256, 64), np.float32)
    for nn in range(255):
        for s in range(64):
            if w[nn, s] and nn + 8 * sh < 256 and s + 2 * sh < 64:
                wl[nn + 8 * sh, s + 2 * sh] = w[nn, s]
    c["c_wimp"] = wl.reshape(2, 128, 64).transpose(1, 0, 2).copy()
    jj = np.arange(128)[:, None]; ii = np.arange(128)[None, :]
    dm = np.zeros((128, 4, 128), np.float32)
    for h in range(4):
        dm[:, h, :] = np.where(ii >= jj, np.exp(lg[h] * np.maximum(ii - jj, 0)), 0.0)
    c["c_dmask"] = dm
    return {k: np.ascontiguousarray(v, dtype=np.float32) for k, v in c.items()}


def _sconsts():
    c = {}
    half = 32
    freqs = (10000.0 ** (-np.arange(half, dtype=np.float32) / half)).astype(np.float32)
    ang = np.float32(16384.0) * freqs
    r = np.stack([np.cos(ang), np.sin(ang), np.cos(ang) / 8.0, np.sin(ang) / 8.0]).astype(np.float32)
    c["c_rope_s"] = np.broadcast_to(r[None], (4, 4, 32)).copy()
    w = np.zeros((1024, 256), np.float32)
    for s_ in range(256):
        for nn, wt in ((4 * s_ - 1, 1), (4 * s_, 2), (4 * s_ + 1, 2), (4 * s_ + 2, 2), (4 * s_ + 3, 1)):
            if 0 <= nn < 1023:
                w[nn, s_] += wt
    c["c_wfull"] = w.reshape(8, 128, 256).transpose(1, 0, 2).copy()
    top = np.zeros((8, 2, 256), np.float32)
    top[:, 0, 0] = 1e4
    top[:, 0, 255] = 1e4
    top[:, 1, :] = np.arange(256, dtype=np.float32)[None, :] + 1.0
    c["c_top"] = top
    p = np.arange(128)
    pc = np.zeros((128, 6), np.float32)
    pc[:, 0] = p
    pc[:, 1] = p % 64
    pc[:, 2] = (p // 32) * 128
    pc[:, 3] = np.where(p <= 64, 0.0, NEGB)
    pc[:, 4] = np.where(p == 127, NEGB, 0.0)
    c["c_pcol"] = pc
    oh = np.zeros((4, 4, 128), np.float32)
    for b in range(4):
        oh[b, b, :] = 1.0
    c["c_oh4"] = oh
    c["c_ohq"] = np.broadcast_to(np.eye(4, dtype=np.float32)[None], (64, 4, 4)).copy()
    return {k: np.ascontiguousarray(v, dtype=np.float32) for k, v in c.items()}


_NC_CACHE = {}


def kernel(x_prompt, x_sample, c_prompt, c_sample, cache_cmp, cache_slc, state_win, state_ret, page_table,
           g_norm, w_ada, b_ada, w_in, g_q, g_kc, g_ks, g_kw, pe_ck, w_ck1, w_ck2, pe_cv, w_cv1, w_cv2,
           g_ret, w_out):
    f = lambda a: np.ascontiguousarray(np.asarray(a), dtype=np.float32)
    x_prompt = f(x_prompt)
    if "nc" not in _NC_CACHE:
        _NC_CACHE["nc"] = build_program()
    nc = _NC_CACHE["nc"]
    shared = dict(w_ada=f(w_ada)[0], b_ada=f(b_ada), w_in=f(w_in)[0], w_out=f(w_out)[0], g_norm=f(g_norm),
                  g_q=f(g_q), g_kc=f(g_kc), g_ks=f(g_ks), g_kw=f(g_kw), g_ret=f(g_ret),
                  pe_ck=f(pe_ck)[0], pe_cv=f(pe_cv)[0], w_ck1=f(w_ck1)[0], w_cv1=f(w_cv1)[0],
                  w_ck2=f(w_ck2)[0], w_cv2=f(w_cv2)[0])
    consts = [_consts(0), _consts(1)]
    sconst = _sconsts()
    cc = f(cache_cmp).reshape(-1, 256)
    cs_ = f(cache_slc).reshape(-1, 256)
    in_maps = []
    for c in range(8):
        b, p = c // 2, c % 2
        if p == 0:
            xl = np.concatenate([np.zeros((128, D), np.float32), x_prompt[b, :31 * 128]], axis=0)
        else:
            xl = x_prompt[b]
        cv = np.concatenate([f(c_prompt)[b:b + 1], f(c_sample)[4 * c:4 * c + 4]], axis=0)
        m = dict(shared)
        m.update(consts[p])
        m["xloc"] = np.ascontiguousarray(xl)
        m["cvec"] = np.ascontiguousarray(cv)
        m.update(sconst)
        m["xs_in"] = np.ascontiguousarray(f(x_sample)[4 * c:4 * c + 4, 0])
        m["ptc"] = np.ascontiguousarray(np.asarray(page_table)[4 * c:4 * c + 4].astype(np.int32))
        m["cache_cmp"] = cc
        m["cache_slc"] = cs_
        m["swin_in"] = np.ascontiguousarray(f(state_win)[0, 4 * c:4 * c + 4].reshape(4, 512, 256))
        m["sret_in"] = np.ascontiguousarray(f(state_ret)[0, 4 * c:4 * c + 4])
        in_maps.append(m)
    res = run_bass_kernel_spmd(nc, in_maps, core_ids=list(range(8)))
    R = res.results
    y_prompt = np.zeros((4, 4096, D), np.float32)
    p_cmp = np.zeros((1, 4, 4096, 256), np.float32)
    p_slc = np.zeros((1, 4, 4096, 256), np.float32)
    p_win = np.zeros((1, 4, 512, 256), np.float32)
    p_ret = np.zeros((1, 4, 4, 64, 128), np.float32)
    for c in range(8):
        b, p = c // 2, c % 2
        yo = R[c]["y_out"].reshape(16, 128, D)
        pc = R[c]["pcmp_out"].reshape(16, 128, 256)
        pl = R[c]["pslc_out"].reshape(16, 128, 256)
        pw = R[c]["pwin_out"].reshape(2, 128, 256)
        for j in range(16):
            gt = 2 * j + p
            y_prompt[b, gt * 128:(gt + 1) * 128] = yo[j]
            p_cmp[0, b, gt * 128:(gt + 1) * 128] = pc[j]
            p_slc[0, b, gt * 128:(gt + 1) * 128] = pl[j]
        for k in range(2):
            gt = 28 + 2 * k + p
            p_win[0, b, (gt - 28) * 128:(gt - 27) * 128] = pw[k]
        if p == 1:
            p_ret[0, b] = R[c]["pret_out"].transpose(1, 0, 2)
    y_sample = np.concatenate([R[c]["ys_out"] for c in range(8)], axis=0).reshape(32, 1, D)
    s_cmp = np.concatenate([R[c]["scmp_out"] for c in range(8)], axis=0).reshape(1, 32, 1, 2, 2, 64)
    s_slc = np.concatenate([R[c]["sslc_out"] for c in range(8)], axis=0).reshape(1, 32, 1, 2, 2, 64)
    s_win = np.concatenate([R[c]["swin_out"] for c in range(8)], axis=0).reshape(1, 32, 512, 2, 2, 64)
    s_ret = np.concatenate([R[c]["sret_out"] for c in range(8)], axis=0).reshape(1, 32, 4, 64, 128)
    outs = (y_prompt, y_sample, p_cmp.reshape(1, 4, 4096, 2, 2, 64), p_slc.reshape(1, 4, 4096, 2, 2, 64),
            p_win.reshape(1, 4, 512, 2, 2, 64), p_ret, s_cmp, s_slc, s_win, s_ret)
    return outs
```

```python
import os
import numpy as np
import ml_dtypes
from contextlib import ExitStack
import concourse.bass as bass
import concourse.mybir as mybir
from concourse.bass_utils import run_bass_kernel_spmd

F32 = mybir.dt.float32
BF16 = mybir.dt.bfloat16
I32 = mybir.dt.int32
AF = mybir.ActivationFunctionType
ALU = mybir.AluOpType
AX = mybir.AxisListType

D = 1024
NT = 32
COLS = (512, 128, 128, 128, 128, 128, 128, 24, 512, 256, 256, 512, 512)
OFF = np.concatenate([[0], np.cumsum(COLS)]).astype(int)
D_IN = int(OFF[-1])
(O_Q, O_KC, O_VC, O_KS, O_VS, O_KW, O_VW, O_BR, O_GA, O_RQ, O_RK, O_RV, O_GR) = [int(v) for v in OFF[:-1]]
NEGB = -30000.0
GC = [float((1.0 - 2.0 ** (-5.0 - h)) ** 128) for h in range(4)]
EPS = 1e-6
SAME_ENGINE_SYNC = True
DO_SAMPLE = True


class TW:
    def __init__(self, t, name):
        self.t = t
        self.name = name
        self.w = None
        self.r = []

    def __getitem__(self, k):
        return self.t[k]


class TWView:
    def __init__(self, base, ap):
        self.base = base
        self.ap = ap
        self.name = base.name

    def __getitem__(self, k):
        return self.ap[k]

    @property
    def w(self):
        return self.base.w

    @w.setter
    def w(self, v):
        self.base.w = v

    @property
    def r(self):
        return self.base.r

    @r.setter
    def r(self, v):
        self.base.r = v


class Prog:
    def __init__(self, nc, es):
        self.nc = nc
        self.es = es
        self.scope = None
        self.tiles = {}
        self.ops = []
        self.eng = {"pe": nc.tensor, "act": nc.scalar, "dve": nc.vector, "pool": nc.gpsimd, "sp": nc.sync}
        self.cnt = {}
        self.sems = {}
        self.known = {e: {} for e in self.eng}
        self.emitted = 0
        self.all_waits = []

    def sem(self, name):
        return self.es.enter_context(self.nc.semaphore(name))

    def scope_push(self):
        self.scope = ExitStack()

    def scope_pop(self):
        self.scope.close()
        self.scope = None

    def sb(self, name, shape, dt):
        if name in self.tiles:
            return self.tiles[name]
        st = self.scope if self.scope is not None else self.es
        t = TW(st.enter_context(self.nc.sbuf_tensor(name, list(shape), dt)), name)
        if self.scope is None:
            self.tiles[name] = t
        return t

    def ps(self, name, shape, dt):
        return TW(self.es.enter_context(self.nc.psum_tensor(name, list(shape), dt)), name)

    def op(self, eng, fn, reads=(), writes=(), dma=None):
        idx = len(self.ops)
        deps = set()
        for t in reads:
            if t.w is not None:
                deps.add(t.w)
        for t in writes:
            if t.w is not None:
                deps.add(t.w)
            for e in t.r:
                deps.add(e)
        deps.discard(idx)
        self.ops.append(dict(eng=eng, fn=fn, deps=sorted(deps), dma=dma, sig=False))
        for t in reads:
            t.r.append(idx)
        for t in writes:
            t.w = idx
            t.r = []
        return idx

    def emit(self, final=True):
        ops = self.ops
        s0 = self.emitted
        seg = range(s0, len(ops))
        cnt, sems = self.cnt, self.sems
        for i in seg:
            o = ops[i]
            for d in o["deps"]:
                if d < s0:
                    continue
                p = ops[d]
                if p["dma"] is not None or p["eng"] != o["eng"] or (SAME_ENGINE_SYNC and o["eng"] != "pe"):
                    p["sig"] = True
        last = {}
        for i in seg:
            if ops[i]["dma"] is None:
                last[ops[i]["eng"]] = i
        for i in last.values():
            ops[i]["sig"] = True
        barrier = dict(cnt) if s0 > 0 else {}
        for i in seg:
            o = ops[i]
            key = ("dma", o["dma"]) if o["dma"] is not None else ("eng", o["eng"])
            if o["dma"] is not None:
                o["sig"] = True
            if o["sig"]:
                if key not in sems:
                    sems[key] = self.sem("s_%s_%s" % key)
                    cnt[key] = 0
                cnt[key] += 16 if o["dma"] is not None else 1
                o["ev"] = (key, cnt[key])
        did_barrier = set()
        for i in seg:
            o = ops[i]
            e = self.eng[o["eng"]]
            kn = self.known[o["eng"]]
            need = {}
            if o["eng"] not in did_barrier:
                did_barrier.add(o["eng"])
                for key, val in barrier.items():
                    need[key] = val
            for d in o["deps"]:
                if d < s0:
                    continue
                p = ops[d]
                if not p["sig"]:
                    continue
                if p["dma"] is None and p["eng"] == o["eng"] and (not SAME_ENGINE_SYNC or o["eng"] == "pe"):
                    continue
                pc = p["dma"] is not None and p["dma"].startswith("const")
                if pc and o["dma"] == p["dma"]:
                    continue
                key, val = p["ev"]
                if pc:
                    val = cnt[key]
                need[key] = max(need.get(key, 0), val)
            o["waits"] = []
            for key, val in need.items():
                if kn.get(key, 0) >= val:
                    continue
                e.wait_ge(sems[key], val)
                o["waits"].append((key, val))
                kn[key] = val
            ins = o["fn"]()
            if o["sig"]:
                key, val = o["ev"]
                ins.then_inc(sems[key], 16 if o["dma"] is not None else 1)
        self.emitted = len(ops)
        self.stats = dict(n_ops=len(ops), cnt={str(k): v for k, v in cnt.items()})
        if final:
            for key, s in sems.items():
                if key[0] == "dma":
                    self.nc.sync.wait_ge(s, cnt[key])


def build_program(sample=True):
    nc = bass.Bass("TRN2", target_bir_lowering=False)
    es = ExitStack()
    P = Prog(nc, es)

    def din(name, shape, dt=F32):
        return nc.dram_tensor(name, list(shape), dt, kind="ExternalInput").ap()

    def dout(name, shape, dt=F32):
        return nc.dram_tensor(name, list(shape), dt, kind="ExternalOutput").ap()

    xloc = din("xloc", [NT * 128, D])
    cvec = din("cvec", [5, D])
    w_ada = din("w_ada", [D, 3 * D])
    b_ada = din("b_ada", [1, 3 * D])
    w_in = din("w_in", [D, D_IN])
    w_out = din("w_out", [D, D])
    g_norm = din("g_norm", [1, D])
    g_q = din("g_q", [1, 64]); g_kc = din("g_kc", [1, 64]); g_ks = din("g_ks", [1, 64]); g_kw = din("g_kw", [1, 64])
    g_ret = din("g_ret", [1, 128])
    pe_ck = din("pe_ck", [32, 64]); pe_cv = din("pe_cv", [32, 64])
    w_ck1 = din("w_ck1", [32, 64, 64]); w_cv1 = din("w_cv1", [32, 64, 64])
    w_ck2 = din("w_ck2", [64, 64]); w_cv2 = din("w_cv2", [64, 64])
    c_ident = din("c_ident", [128, 128])
    c_expand = din("c_expand", [64, NT * 128])
    c_mask3 = din("c_mask3", [128, 3, 128])
    c_maskc = din("c_maskc", [128, 16, 2, 128])
    c_adds = din("c_adds", [128, 16, 64])
    c_rope = din("c_rope", [128, NT, 4, 32])
    c_wimp = din("c_wimp", [128, 2, 64])
    c_dmask = din("c_dmask", [128, 4, 128])
    c_rscal = din("c_rscal", [128, NT, 8])
    y_out = dout("y_out", [16 * 128, D])
    pcmp_out = dout("pcmp_out", [16 * 128, 256])
    pslc_out = dout("pslc_out", [16 * 128, 256])
    pwin_out = dout("pwin_out", [2 * 128, 256])
    pret_out = dout("pret_out", [64, 4, 128])
    DBGO = os.environ.get('KS_DBGOUT', '0') == '1'
    dbg_out = dout("dbg_out", [16 * 128, D]) if DBGO else None

    xs_in = din("xs_in", [4, D])
    ptc = din("ptc", [4, 128], I32)
    NPHYS = int(os.environ.get('KS_NPHYS', '5120'))
    cache_cmp = din("cache_cmp", [NPHYS * 128, 256])
    cache_slc = din("cache_slc", [NPHYS * 128, 256])
    dbg2_out = dout("dbg2_out", [128, 64]) if DBGO else None
    dbg3_out = dout("dbg3_out", [128, 2048]) if DBGO else None
    swin_in = din("swin_in", [4, 512, 256])
    sret_in = din("sret_in", [4, 4, 64, 128])
    c_rope_s = din("c_rope_s", [4, 4, 32])
    c_wfull = din("c_wfull", [128, 8, 256])
    c_top = din("c_top", [8, 2, 256])
    c_pcol = din("c_pcol", [128, 6])
    c_oh4 = din("c_oh4", [4, 4, 128])
    c_ohq = din("c_ohq", [64, 4, 4])
    ys_out = dout("ys_out", [4, D])
    scmp_out = dout("scmp_out", [4, 256])
    sslc_out = dout("sslc_out", [4, 256])
    swin_out = dout("swin_out", [4, 512, 256])
    sret_out = dout("sret_out", [4, 4, 64, 128])
    scr1 = nc.dram_tensor("scr1", [128, 1], F32, kind="Internal").ap()
    scr2 = nc.dram_tensor("scr2", [128, 1], F32, kind="Internal").ap()
    sb, ps = P.sb, P.ps
    V, A, T, G, S = "dve", "act", "pe", "pool", "sp"

    def dma(eng, out_t, out_ap, in_t, in_ap, grp, **kw):
        reads = [in_t] if in_t is not None else []
        writes = [out_t] if out_t is not None else []
        e = P.eng[eng]
        if not grp.startswith("const"):
            grp = (out_t or in_t).name
        return P.op(eng, lambda: e.dma_start(out=out_ap, in_=in_ap, **kw), reads, writes, dma=grp)

    def vop(name, reads, writes, eng=V, **kw):
        e = P.eng[eng]
        f = getattr(e, name)
        return P.op(eng, lambda: f(**kw), reads, writes)

    def act(out_t, out_ap, in_t, in_ap, func, extra_reads=(), extra_writes=(), **kw):
        return P.op(A, lambda: nc.scalar.activation(out=out_ap, in_=in_ap, func=func, **kw), [in_t] + list(extra_reads), [out_t] + list(extra_writes))

    def acopy(out_t, out_ap, in_t, in_ap):
        return P.op(A, lambda: nc.scalar.copy(out=out_ap, in_=in_ap), [in_t], [out_t])

    def mm(out_t, out_ap, l_t, l_ap, r_t, r_ap, start, stop, sgc=False):
        return P.op(T, lambda: nc.tensor.matmul(out_ap, lhsT=l_ap, rhs=r_ap, start=start, stop=stop, skip_group_check=sgc), [l_t, r_t], [out_t])

    def tr(out_t, out_ap, in_t, in_ap, id_t, id_ap):
        return P.op(T, lambda: nc.tensor.transpose(out=out_ap, in_=in_ap, identity=id_ap), [in_t, id_t], [out_t])

    sb("w_in_bf", [128, 8, D_IN], BF16)
    sb("w_out_bf", [128, 8, D], BF16)
    sb("ident", [128, 128], BF16)
    ident_f = sb("ident_f", [128, 128], F32)
    ones_b = sb("ones_b", [128, 128], BF16)
    for i in range(2):
        sb("w1blk%d" % i, [128, 32, 128], BF16)
        sb("w2blk%d" % i, [128, 128], BF16)
        sb("cbias%d" % i, [128, 1], F32)
    sb("modA", [128, 8, 5], F32)
    sb("modS", [128, 8, 5], F32)
    sb("gvec", [128, 64 * 4 + 128], F32)
    sb("modT", [128, 24, 5], F32)
    P.scope_push()
    w_in_bf = sb("w_in_bf", [128, 8, D_IN], BF16)
    w_out_bf = sb("w_out_bf", [128, 8, D], BF16)
    wst = [sb("wst%d" % i, [128, 8, 128], F32) for i in range(2)]
    ident = sb("ident", [128, 128], BF16)
    mask3 = sb("mask3", [128, 3, 128], BF16)
    maskc = sb("maskc", [128, 16, 2, 128], BF16)
    adds = sb("adds", [128, 16, 64], F32)
    wimp = sb("wimp", [128, 2, 64], BF16)
    dmask = sb("dmask", [128, 4, 128], F32)
    rscal = sb("rscal", [128, NT, 8], F32)
    ksT = [sb("ksT%d" % g, [128, NT * 128], BF16) for g in range(2)]
    vsa = [sb("vsa%d" % g, [128, NT, 65], BF16) for g in range(2)]
    kwT = [sb("kwT%d" % g, [64, 6, 128], BF16) for g in range(2)]
    vwa = [sb("vwa%d" % g, [128, 6, 65], BF16) for g in range(2)]
    cT = [sb("cT%d" % i, [128, 144], BF16) for i in range(2)]
    hTall = [sb("hTall%d" % i, [128, 256], BF16) for i in range(2)]
    kcT = [sb("kcT%d" % g, [64, 256], BF16) for g in range(2)]
    vca = [sb("vca%d" % g, [128, 2, 129], BF16) for g in range(2)]
    w1blk = [sb("w1blk%d" % i, [128, 32, 128], BF16) for i in range(2)]
    w2blk = [sb("w2blk%d" % i, [128, 128], BF16) for i in range(2)]
    cbias = [sb("cbias%d" % i, [128, 1], F32) for i in range(2)]
    S_f = sb("S_f", [64, 4, 128], F32)
    S_b = sb("S_b", [64, 4, 128], BF16)
    modA = sb("modA", [128, 8, 5], F32)
    modS = sb("modS", [128, 8, 5], F32)
    gate_bc = sb("gate_bc", [128, D], F32)
    gvec = sb("gvec", [128, 64 * 4 + 128], F32)
    gn_col = sb("gn_col", [128, 8], F32)

    ptr = ps("ptr", [128, 1024], BF16)
    pin = [ps("pin%d" % i, [128, 512], F32) for i in range(2)]
    pst = [ps("pst%d" % i, [128, 512], F32) for i in range(2)]
    pacc = [ps("pacc%d" % i, [128, 512], F32) for i in range(2)]
    paccw = ps("paccw", [128, 512], F32)

    CG = "const"
    for g in range(2):
        vop("memset", [], [vsa[g]], eng=G, ap=vsa[g][:, :, 64:65], constant=1.0)
        vop("memset", [], [vwa[g]], eng=G, ap=vwa[g][:, :, 64:65], constant=1.0)
        vop("memset", [], [vca[g]], eng=G, ap=vca[g][:], constant=0.0)
    for i in range(2):
        vop("memset", [], [hTall[i]], eng=G, ap=hTall[i][:], constant=0.0)
        vop("memset", [], [cT[i]], eng=G, ap=cT[i][:], constant=0.0)
        vop("memset", [], [w1blk[i]], eng=G, ap=w1blk[i][:], constant=0.0)
        vop("memset", [], [w2blk[i]], eng=G, ap=w2blk[i][:], constant=0.0)
    vop("memset", [], [ones_b], eng=G, ap=ones_b[:], constant=1.0)
    vop("memset", [], [S_f], eng=G, ap=S_f[:], constant=0.0)
    vop("memset", [], [S_b], eng=G, ap=S_b[:], constant=0.0)
    dma(G, ident, ident[:], None, c_ident, CG)
    dma(S, ident_f, ident_f[:], None, c_ident, CG)
    dma(G, mask3, mask3[:], None, c_mask3, CG)
    dma(S, adds, adds[:], None, c_adds, CG)
    dma(S, dmask, dmask[:], None, c_dmask, CG)
    dma(S, rscal, rscal[:], None, c_rscal, CG)
    dma(G, maskc, maskc[:], None, c_maskc, CG)
    dma(G, wimp, wimp[:], None, c_wimp, CG)
    for g in range(2):
        dma(G, ksT[g], ksT[g][64:128, :], None, c_expand, CG)
    dma(S, gvec, gvec[:, 0:64], None, g_q.to_broadcast([128, 64]), CG)
    dma(S, gvec, gvec[:, 64:128], None, g_kc.to_broadcast([128, 64]), CG)
    dma(S, gvec, gvec[:, 128:192], None, g_ks.to_broadcast([128, 64]), CG)
    dma(S, gvec, gvec[:, 192:256], None, g_kw.to_broadcast([128, 64]), CG)
    dma(S, gvec, gvec[:, 256:384], None, g_ret.to_broadcast([128, 128]), CG)
    dma(S, gn_col, gn_col[:], None, g_norm.rearrange("o (k p) -> p (o k)", p=128), CG, allow_slow_non_contiguous=True)
    peTb = [sb("peTb%d" % i, [128, 32], BF16) for i in range(2)]
    for i, (w1, w2, pe) in enumerate(((w_ck1, w_ck2, pe_ck), (w_cv1, w_cv2, pe_cv))):
        src = w1.rearrange("l d f -> d l f")
        dma(G, w1blk[i], w1blk[i][0:64, :, 0:64], None, src, CG)
        dma(G, w1blk[i], w1blk[i][64:128, :, 64:128], None, src, CG)
        dma(G, w2blk[i], w2blk[i][0:64, 0:64], None, w2, CG)
        dma(G, w2blk[i], w2blk[i][64:128, 64:128], None, w2, CG)
        srcp = pe.rearrange("l d -> d l")
        dma(G, peTb[i], peTb[i][0:64, :], None, srcp, CG, allow_slow_non_contiguous=True)
        dma(G, peTb[i], peTb[i][64:128, :], None, srcp, CG, allow_slow_non_contiguous=True)
    cT_f = sb("cT_f", [128, 8, 5], F32)
    cT_b = sb("cT_b", [128, 8, 5], BF16)
    xs = sb("xs", [128, D], BF16)
    hT = sb("hT", [128, 8, 128], BF16)
    crep = hT
    for r in range(5):
        dma(S, cT_f, cT_f[:, :, r], None, cvec[r:r + 1, :].rearrange("o (k p) -> p (o k)", p=128), CG, allow_slow_non_contiguous=True)
    bad = sb("bad", [128, 24], F32)
    dma(S, bad, bad[:], None, b_ada.rearrange("o (j p) -> p (o j)", p=128), CG, allow_slow_non_contiguous=True)
    yst = [sb("yst0", [128, D], F32)]
    badg = yst[0]
    dma(S, badg, badg[:], None, b_ada[:, 2 * D:3 * D].to_broadcast([128, D]), CG)
    STAGE = int(os.environ.get('KS_STAGE', '9'))
    RUN_NT = int(os.environ.get('KS_NT', str(NT)))
    if STAGE < 2:
        P.emit(); es.close(); return nc
    for g in range(2):
        vop("memset", [], [vca[g]], eng=G, ap=vca[g][:, :, 64:65], constant=1.0)
        vop("tensor_copy", [wimp], [vca[g]], eng=G, out=vca[g][:, :, 65:129], in_=wimp[:])
    vop("tensor_scalar", [gvec], [gvec], out=gvec[:, 64:256], in0=gvec[:, 64:256], scalar1=8.0, scalar2=None, op0=ALU.mult)
    vop("tensor_scalar", [gvec], [gvec], out=gvec[:, 256:384], in0=gvec[:, 256:384], scalar1=float(np.sqrt(128.0)), scalar2=None, op0=ALU.mult)
    for i in range(2):
        for l in range(32):
            mm(pin[0], pin[0][:, 0:1], w1blk[i], w1blk[i][:, l, :], peTb[i], peTb[i][:, l:l + 1], l == 0, l == 31)
        vop("tensor_copy", [pin[0]], [cbias[i]], out=cbias[i][:], in_=pin[0][:, 0:1])

    def load_cast(src, ncols, dst):
        k = 0
        for c0 in range(0, ncols, 128):
            cw = min(128, ncols - c0)
            st = wst[k % 2]
            k += 1
            dma(S if k % 2 else A, st, st[:, :, 0:cw], None, src[:, c0:c0 + cw].rearrange("(k p) c -> p k c", p=128), "x")
            vop("tensor_copy", [st], [dst], eng=(V if k % 2 else G), out=dst[:, :, c0:c0 + cw], in_=st[:, :, 0:cw])

    act(cT_b, cT_b[:], cT_f, cT_f[:], AF.Silu)
    for kt in range(8):
        vop("tensor_copy", [cT_b], [crep], out=crep[:, kt, :], in_=cT_b[:, kt, 0:1].to_broadcast([128, 128]))
    modT = sb("modT", [128, 24, 5], F32)
    wab = [sb("wab0", [128, 8, 128], BF16), TWView(xs, xs[:].rearrange("p (k c) -> p k c", k=8))]
    k = 0
    for c0 in range(0, 3 * D, 128):
        st = wst[k % 2]
        wb = wab[k % 2]
        k += 1
        dma(S if k % 2 else A, st, st[:], None, w_ada[:, c0:c0 + 128].rearrange("(k p) c -> p k c", p=128), "x")
        vop("tensor_copy", [st], [wb], out=wb[:], in_=st[:])
        j = c0 // 128
        po = pin[j % 2]
        for kt in range(8):
            mm(po, po[:, 0:5], wb, wb[:, kt, :], cT_b, cT_b[:, kt, :], kt == 0, kt == 7)
        vop("tensor_scalar", [po, bad], [modT], out=modT[:, j, :], in0=po[:, 0:5], scalar1=bad[:, j:j + 1], scalar2=None, op0=ALU.add)
        if c0 >= 2 * D:
            po = pst[j % 2]
            for kt in range(8):
                mm(po, po[:, 0:128], crep, crep[:, kt, :], wb, wb[:, kt, :], kt == 0, kt == 7)
            vop("tensor_tensor", [po, badg], [gate_bc], out=gate_bc[:, c0 - 2 * D:c0 - 2 * D + 128], in0=po[:, 0:128], in1=badg[:, c0 - 2 * D:c0 - 2 * D + 128], op=ALU.add)
    vop("tensor_scalar", [modT], [modA], out=modA[:], in0=modT[:, 8:16, :], scalar1=1.0, scalar2=None, op0=ALU.add)
    vop("tensor_tensor", [modA, gn_col], [modA], out=modA[:], in0=modA[:], in1=gn_col[:].unsqueeze(2).to_broadcast([128, 8, 5]), op=ALU.mult)
    vop("tensor_copy", [modT], [modS], out=modS[:], in_=modT[:, 0:8, :])

    load_cast(w_in, D_IN, w_in_bf)
    load_cast(w_out, D, w_out_bf)

    xt = [sb("xt0", [128, D], F32)] * 2
    rope = [sb("rope%d" % i, [128, 4, 32], F32) for i in range(2)]
    ss = sb("ss", [128, 1], F32)
    rstd = sb("rstd", [128, 1], F32)
    sq = sb("sq", [128, 512], F32)
    ssq = sb("ssq", [128, 16], F32)
    rq8 = sb("rq8", [128, 16], F32)
    qn = sb("qn", [128, 512], BF16)
    qaug = [sb("qaug%d" % g, [128, 4, 128], BF16) for g in range(2)]
    kvf = sb("kvf", [128, 512], F32)
    kvb = sb("kvb", [128, 512], BF16)
    kwf = sb("kwf", [128, 280], F32)
    kwb = sb("kwb", [128, 128], BF16)
    gates = sb("gates", [128, 24], F32)
    ga_s = sb("ga_s", [128, 512], BF16)
    gr_s = sb("gr_s", [128, 512], BF16)
    rot = sb("rot", [128, 2, 4, 64], F32)
    tmp1 = sb("tmp1", [128, 2, 4, 32], F32)
    tmp2 = sb("tmp2", [128, 2, 4, 32], F32)
    rqb = sb("rqb", [128, 3, 4, 64], BF16)
    ktl = sb("ktl", [128, 4, 64], BF16)
    rT = sb("rT", [64, 3, 4, 128], BF16)
    vb = sb("vb", [128, 4, 128], BF16)
    atm = sb("atm", [128, 4, 128], BF16)
    oret = sb("oret", [128, 4, 128], F32)
    et = [sb("et%d" % i, [128, 4, 128], BF16) for i in range(2)]
    obr = [sb("obr%d" % i, [128, 4, 65], F32) for i in range(3)]
    aimp = sb("aimp", [128, 4, 64], F32)
    rden = sb("rden", [128, 3, 4], F32)
    coef = sb("coef", [128, 3, 4], F32)
    score = sb("score", [128, 64], F32)
    swork = sb("swork", [128, 64], F32)
    mx8 = sb("mx8", [128, 16], F32)
    thr = sb("thr", [128, 1], F32)
    selb = sb("selb", [128, 128], BF16)
    onsa = sb("onsa", [128, 8, 64], F32)
    otmp = sb("otmp", [128, 4, 64], F32)
    ymix = xs
    yT = hT
    kcn = sb("kcn", [128, 128], F32)
    kcnb = sb("kcnb", [128, 128], BF16)
    vop("memset", [], [selb], eng=G, ap=selb[:], constant=0.0)

    def rms_heads(src_t, src_ap, nh, hd, gain_ap, out_t, out_ap, eps_sum, np_=128, scr=None):
        sq_, ssq_, rq_ = scr if scr is not None else (sq, ssq, rq8)
        act(sq_, sq_[0:np_, 0:nh * hd], src_t, src_ap, AF.Square)
        vop("tensor_reduce", [sq_], [ssq_], out=ssq_[0:np_, 0:nh], in_=sq_[0:np_, 0:nh * hd].rearrange("p (h d) -> p h d", h=nh), axis=AX.X, op=ALU.add)
        vop("tensor_scalar", [ssq_], [rq_], out=rq_[0:np_, 0:nh], in0=ssq_[0:np_, 0:nh], scalar1=eps_sum, scalar2=None, op0=ALU.add)
        act(rq_, rq_[0:np_, 0:nh], rq_, rq_[0:np_, 0:nh], AF.Sqrt)
        vop("reciprocal", [rq_], [rq_], out=rq_[0:np_, 0:nh], in_=rq_[0:np_, 0:nh])
        vop("tensor_tensor", [src_t, rq_], [sq_], out=sq_[0:np_, 0:nh * hd].rearrange("p (h d) -> p h d", h=nh),
            in0=src_ap.rearrange("p (h d) -> p h d", h=nh), in1=rq_[0:np_, 0:nh].unsqueeze(2).to_broadcast([np_, nh, hd]), op=ALU.mult)
        vop("tensor_tensor", [sq_, gvec], [out_t], out=out_ap.rearrange("p (h d) -> p h d", h=nh),
            in0=sq_[0:np_, 0:nh * hd].rearrange("p (h d) -> p h d", h=nh), in1=gain_ap.unsqueeze(1).to_broadcast([np_, nh, hd]), op=ALU.mult)

    def inproj(c0, cw, po):
        for kt in range(8):
            mm(po, po[:, 0:cw], hT, hT[:, kt, :], w_in_bf, w_in_bf[:, kt, c0:c0 + cw], kt == 0, kt == 7)

    n_et = [0]

    def attn_tile(g, po, l_t, l_ap, r_ap, masks, v_t, v_ap, nv, acc, first, last, accw=None, vw_ap=None):
        nmm = 1 + len(masks)
        mm(po, po[:], l_t, l_ap, qaug[g], r_ap, True, nmm == 1)
        for mi, (m_t, m_ap) in enumerate(masks):
            for h in range(4):
                mm(po, po[:, h * 128:(h + 1) * 128], ident, ident[:], m_t, m_ap, False, (mi == len(masks) - 1) and h == 3, sgc=True)
        e = et[n_et[0] % 2]
        n_et[0] += 1
        act(e, e[:].rearrange("p h q -> p (h q)"), po, po[:], AF.Exp)
        for h in range(4):
            mm(acc, acc[:, h * 128:h * 128 + nv], e, e[:, h, :], v_t, v_ap, first and h == 0, last and h == 3, sgc=True)
            if accw is not None:
                mm(accw, accw[:, h * 64:(h + 1) * 64], e, e[:, h, :], v_t, vw_ap, first and h == 0, last and h == 3, sgc=True)

    n_st = [0]

    if STAGE < 3:
        P.emit(); es.close(); return nc
    SUB = int(os.environ.get('KS_SUB', '99'))

    class _Stop(Exception):
        pass

    def ck(k):
        if SUB < k:
            raise _Stop()

    def _tile(L):
            own = (L % 2 == 1)
            j = L // 2
            x_t = xt[L % 2]
            rp = rope[L % 2]
            dma(S, x_t, x_t[:], None, xloc[L * 128:(L + 1) * 128, :], "x%d" % (L % 2))
            dma(A, rp, rp[:], None, c_rope[:, L, :, :], "x%d" % (L % 2))
            vop("memset", [], [ss], eng=G, ap=ss[:], constant=0.0)
            act(xs, xs[:], x_t, x_t[:], AF.Square, accum_out=ss[:], extra_writes=[ss])
            vop("tensor_scalar", [ss], [rstd], out=rstd[:], in0=ss[:], scalar1=1.0 / D, scalar2=EPS, op0=ALU.mult, op1=ALU.add)
            act(rstd, rstd[:], rstd, rstd[:], AF.Sqrt)
            vop("reciprocal", [rstd], [rstd], out=rstd[:], in_=rstd[:])
            P.op(A, lambda x_t=x_t: nc.scalar.mul(out=xs[:], in_=x_t[:], mul=rstd[:, 0:1]), [x_t, rstd], [xs])
            for kt in range(8):
                tr(ptr, ptr[:, kt * 128:(kt + 1) * 128], xs, xs[:, kt * 128:(kt + 1) * 128], ident, ident[:])
            for kt in range(8):
                vop("tensor_scalar", [ptr, modA, modS], [hT], out=hT[:, kt, :], in0=ptr[:, kt * 128:(kt + 1) * 128],
                    scalar1=modA[:, kt, 0:1], scalar2=modS[:, kt, 0:1], op0=ALU.mult, op1=ALU.add)

            ck(1)
            po = pin[0]
            inproj(O_KC, 512, po)
            DBG = int(os.environ.get('KS_DBG', '3'))
            if DBG & 1:
                vop("tensor_copy", [po], [kvf], out=kvf[:], in_=po[:])
            if DBG & 2:
                if os.environ.get('KS_ACTV', '0') == '1':
                    act(kvb, kvb[:], po, po[:], AF.Identity)
                elif os.environ.get('KS_ACTV', '0') == '2':
                    act(kvb, kvb[:], po, po[:], AF.Silu)
                else:
                    vop("tensor_copy", [kvf], [kvb], eng=G, out=kvb[:], in_=kvf[:])
            ck(2)
            for i in range(2):
                tr(ptr, ptr[:, i * 128:(i + 1) * 128], kvb, kvb[:, i * 128:(i + 1) * 128], ident, ident[:])
            for i in range(2):
                vop("tensor_copy", [ptr], [cT[i]], out=cT[i][:, 16:144], in_=ptr[:, i * 128:(i + 1) * 128])
            ck(3)
            rms_heads(kvf, kvf[:, 256:384], 2, 64, gvec[:, 128:192], kvf, kvf[:, 256:384], 64 * EPS)
            vop("tensor_copy", [kvf], [kvb], out=kvb[:, 256:384], in_=kvf[:, 256:384])
            for g in range(2):
                tr(ptr, ptr[0:64, (2 + g) * 128:(3 + g) * 128], kvb, kvb[:, 256 + g * 64:256 + (g + 1) * 64], ident, ident[:])
            for g in range(2):
                vop("tensor_copy", [ptr], [ksT[g]], out=ksT[g][0:64, L * 128:(L + 1) * 128], in_=ptr[0:64, (2 + g) * 128:(3 + g) * 128])
                vop("tensor_copy", [kvb], [vsa[g]], eng=G, out=vsa[g][:, L, 0:64], in_=kvb[:, 384 + g * 64:384 + (g + 1) * 64])
            if own:
                dma(S, None, pcmp_out[j * 128:(j + 1) * 128, :], kvf, kvf[:, 0:256], "x")
                dma(S, None, pslc_out[j * 128:(j + 1) * 128, :], kvf, kvf[:, 256:512], "x")
            ck(4)
            po = pin[1]
            cw = 280 if own else 256
            inproj(O_KW, cw, po)
            vop("tensor_copy", [po], [kwf], out=kwf[:, 0:cw], in_=po[:, 0:cw])
            rms_heads(kwf, kwf[:, 0:128], 2, 64, gvec[:, 192:256], kwf, kwf[:, 0:128], 64 * EPS)
            vop("tensor_copy", [kwf], [kwb], out=kwb[:], in_=kwf[:, 0:128])
            for g in range(2):
                tr(ptr, ptr[0:64, (4 + g) * 128:(5 + g) * 128], kwb, kwb[:, g * 64:(g + 1) * 64], ident, ident[:])
            for g in range(2):
                vop("tensor_copy", [ptr], [kwT[g]], out=kwT[g][:, L % 6, :], in_=ptr[0:64, (4 + g) * 128:(5 + g) * 128])
                vop("tensor_copy", [kwf], [vwa[g]], eng=G, out=vwa[g][:, L % 6, 0:64], in_=kwf[:, 128 + g * 64:128 + (g + 1) * 64])
            if L in (29, 31):
                dma(S, None, pwin_out[((L - 29) // 2) * 128:((L - 29) // 2 + 1) * 128, :], kwf, kwf[:, 0:256], "x")
            if own:
                act(gates, gates[:], kwf, kwf[:, 256:280], AF.Sigmoid)

            ck(5)
            nb0 = 8 * L - 1
            c_lo = 1 if L == 0 else 0
            for i in range(2):
                po = pin[i]
                for l in range(32):
                    mm(po, po[:, 0:8 - c_lo], w1blk[i], w1blk[i][:, l, :], cT[i], cT[i][:, 16 * c_lo + l:16 * c_lo + l + 16 * (7 - c_lo) + 1:16], l == 0, l == 31)
                act(hTall[i], hTall[i][:, nb0 + c_lo:nb0 + 8], po, po[:, 0:8 - c_lo], AF.Silu, extra_reads=[cbias[i]], bias=cbias[i][:, 0:1])
                vop("tensor_copy", [cT[i]], [cT[i]], out=cT[i][:, 0:16], in_=cT[i][:, 128:144])

            ck(6)
            po = pin[0]
            rqk = po
            if own:
                inproj(O_RQ, 512, po)
                qk0 = 0
                rqo = 0
            else:
                inproj(O_RK, 256, po)
                qk0 = 1
                rqo = -256
            po = pin[1]
            inproj(O_RV, 512, po)
            vop("tensor_copy", [po], [vb], out=vb[:].rearrange("p h d -> p (h d)"), in_=po[:])
            ck(7)
            for s in range(qk0, 2):
                xv = rqk[:, s * 256 + rqo:(s + 1) * 256 + rqo].rearrange("p (h d) -> p h d", h=4)
                cs = rp[:, 2 * s, :].unsqueeze(1).to_broadcast([128, 4, 32])
                sn = rp[:, 2 * s + 1, :].unsqueeze(1).to_broadcast([128, 4, 32])
                vop("tensor_tensor", [rqk, rp], [tmp1], out=tmp1[:, s], in0=xv[:, :, 0:32], in1=cs, op=ALU.mult)
                vop("tensor_tensor", [rqk, rp], [tmp2], out=tmp2[:, s], in0=xv[:, :, 32:64], in1=sn, op=ALU.mult)
                vop("tensor_tensor", [tmp1, tmp2], [rot], out=rot[:, s, :, 0:32], in0=tmp1[:, s], in1=tmp2[:, s], op=ALU.subtract)
                vop("tensor_tensor", [rqk, rp], [tmp1], out=tmp1[:, s], in0=xv[:, :, 0:32], in1=sn, op=ALU.mult)
                vop("tensor_tensor", [rqk, rp], [tmp2], out=tmp2[:, s], in0=xv[:, :, 32:64], in1=cs, op=ALU.mult)
                vop("tensor_tensor", [tmp1, tmp2], [rot], out=rot[:, s, :, 32:64], in0=tmp1[:, s], in1=tmp2[:, s], op=ALU.add)
            vop("tensor_tensor", [rot, rscal], [ktl], out=ktl[:], in0=rot[:, 1], in1=rscal[:, L, 0:4].unsqueeze(2).to_broadcast([128, 4, 64]), op=ALU.mult)
            if own:
                vop("tensor_copy", [rot], [rqb], out=rqb[:, 0:2], in_=rot[:])
                vop("tensor_tensor", [rot, rscal], [rqb], out=rqb[:, 2], in0=rot[:, 0], in1=rscal[:, L, 4:8].unsqueeze(2).to_broadcast([128, 4, 64]), op=ALU.mult)
                for s in range(2):
                    for h in range(4):
                        c = s * 4 + h
                        tr(ptr, ptr[0:64, c * 128:(c + 1) * 128], rqb, rqb[:, s, h, :], ident, ident[:])
                vop("tensor_copy", [ptr], [rT], out=rT[:, 0:2].rearrange("p s h q -> p (s h q)"), in_=ptr[0:64, :])
                for h in range(4):
                    tr(ptr, ptr[0:64, h * 128:(h + 1) * 128], rqb, rqb[:, 2, h, :], ident, ident[:])
                vop("tensor_copy", [ptr], [rT], out=rT[:, 2].rearrange("p h q -> p (h q)"), in_=ptr[0:64, 0:512])
                po = pst[0]
                for h in range(4):
                    mm(po, po[:, h * 128:(h + 1) * 128], rT, rT[:, 1, h, :], rT, rT[:, 0, h, :], True, True)
                vop("tensor_tensor", [po, dmask], [atm], out=atm[:].rearrange("p h q -> p (h q)"), in0=po[:], in1=dmask[:].rearrange("p h q -> p (h q)"), op=ALU.mult)
                po = pst[1]
                for h in range(4):
                    mm(po, po[:, h * 128:(h + 1) * 128], atm, atm[:, h, :], vb, vb[:, h, :], True, False)
                    mm(po, po[:, h * 128:(h + 1) * 128], rT, rT[:, 2, h, :], S_b, S_b[:, h, :], False, True)
                vop("tensor_copy", [po], [oret], out=oret[:].rearrange("p h d -> p (h d)"), in_=po[:])
            ck(8)
            po = pacc[0]
            for h in range(4):
                mm(po, po[0:64, h * 128:(h + 1) * 128], ktl, ktl[:, h, :], vb, vb[:, h, :], True, True)
            for h in range(4):
                vop("tensor_scalar", [S_f], [S_f], out=S_f[:, h, :], in0=S_f[:, h, :], scalar1=GC[h], scalar2=None, op0=ALU.mult)
            vop("tensor_tensor", [S_f, po], [S_f], out=S_f[:].rearrange("p h d -> p (h d)"), in0=S_f[:].rearrange("p h d -> p (h d)"), in1=po[0:64, :], op=ALU.add)
            vop("tensor_copy", [S_f], [S_b], out=S_b[:], in_=S_f[:])
            if L == NT - 1:
                dma(S, None, pret_out, S_f, S_f[:], "oret")
            if not own:
                return

            po = pin[0]
            inproj(O_Q, 512, po)
            rms_heads(po, po[:], 8, 64, gvec[:, 0:64], qn, qn[:], 64 * EPS)
            for hh in range(8):
                tr(ptr, ptr[0:64, hh * 128:(hh + 1) * 128], qn, qn[:, hh * 64:(hh + 1) * 64], ident, ident[:])
            for g in range(2):
                vop("tensor_copy", [ptr], [qaug[g]], out=qaug[g][0:64, :, :].rearrange("p h q -> p (h q)"), in_=ptr[0:64, g * 512:(g + 1) * 512])
            po = pin[1]
            inproj(O_GA, 512, po)
            act(ga_s, ga_s[:], po, po[:], AF.Silu)
            po = pin[0]
            inproj(O_GR, 512, po)
            act(gr_s, gr_s[:], po, po[:], AF.Silu)

            nmax = 8 * L + 6
            ntiles = nmax // 128 + 1
            for nt in range(ntiles):
                po = pin[nt % 2]
                mm(po, po[:, 0:128], hTall[0], hTall[0][:, nt * 128:(nt + 1) * 128], w2blk[0], w2blk[0][:], True, True)
                mm(po, po[:, 128:256], hTall[1], hTall[1][:, nt * 128:(nt + 1) * 128], w2blk[1], w2blk[1][:], True, True)
                vop("tensor_copy", [po], [kcn], out=kcn[:], in_=po[:, 0:128])
                rms_heads(kcn, kcn[:], 2, 64, gvec[:, 64:128], kcnb, kcnb[:], 64 * EPS)
                for g in range(2):
                    tr(ptr, ptr[0:64, g * 128:(g + 1) * 128], kcnb, kcnb[:, g * 64:(g + 1) * 64], ident, ident[:])
                    vop("tensor_copy", [ptr], [kcT[g]], out=kcT[g][:, nt * 128:(nt + 1) * 128], in_=ptr[0:64, g * 128:(g + 1) * 128])
                    vop("tensor_copy", [po], [vca[g]], out=vca[g][:, nt, 0:64], in_=po[:, 128 + g * 64:128 + (g + 1) * 64])

            for g in range(2):
                acc = pacc[n_st[0] % 2]; n_st[0] += 1
                for nt in range(ntiles):
                    po = pst[nt % 2]
                    attn_tile(g, po, kcT[g], kcT[g][:, nt * 128:(nt + 1) * 128], qaug[g][0:64, :, :].rearrange("p h q -> p (h q)"),
                              [(maskc, maskc[:, j, nt, :])], vca[g], vca[g][:, nt, 0:65], 65, acc, nt == 0, nt == ntiles - 1,
                              accw=paccw, vw_ap=vca[g][:, nt, 65:129])
                ob = obr[0]
                vop("tensor_copy", [acc], [ob], out=ob[:], in_=acc[:].rearrange("p (h d) -> p h d", h=4)[:, :, 0:65])
                vop("tensor_scalar", [ob], [rden], out=rden[:, 0, :], in0=ob[:, :, 64], scalar1=1e-30, scalar2=None, op0=ALU.max)
                vop("reciprocal", [rden], [rden], out=rden[:, 0, :], in_=rden[:, 0, :])
                vop("tensor_tensor", [paccw, rden], [aimp], out=aimp[:], in0=paccw[:, 0:256].rearrange("p (h s) -> p h s", h=4),
                    in1=rden[:, 0, :].unsqueeze(2).to_broadcast([128, 4, 64]), op=ALU.mult)
                vop("tensor_reduce", [aimp], [score], out=score[:], in_=aimp[:].rearrange("p h s -> p s h"), axis=AX.X, op=ALU.add)
                vop("tensor_tensor", [score, adds], [score], out=score[:], in0=score[:], in1=adds[:, j, :], op=ALU.add)
                vop("max", [score], [mx8], out=mx8[:, 0:8], in_=score[:])
                vop("match_replace", [mx8, score], [swork], out=swork[:], in_to_replace=mx8[:, 0:8], in_values=score[:], imm_value=-3e30)
                vop("max", [swork], [mx8], out=mx8[:, 8:16], in_=swork[:])
                vop("tensor_scalar", [mx8], [thr], out=thr[:], in0=mx8[:, 15:16], scalar1=-1e29, scalar2=None, op0=ALU.max)
                vop("tensor_scalar", [score, thr], [swork], out=swork[:], in0=score[:], scalar1=thr[:, 0:1], scalar2=None, op0=ALU.is_lt)
                vop("tensor_scalar", [swork], [selb], out=selb[:, 64:128], in0=swork[:], scalar1=NEGB, scalar2=None, op0=ALU.mult)
                tr(ptr, ptr[:, 0:128], selb, selb[:], ident, ident[:])
                vop("tensor_copy", [ptr], [qaug[g]], out=qaug[g][64:128, :, :], in_=ptr[64:128, 0:128].unsqueeze(1).to_broadcast([64, 4, 128]))
                acc = pacc[n_st[0] % 2]; n_st[0] += 1
                for kt in range(L + 1):
                    po = pst[kt % 2]
                    masks = [(mask3, mask3[:, 0, :])] if kt == L else []
                    attn_tile(g, po, ksT[g], ksT[g][:, kt * 128:(kt + 1) * 128], qaug[g][:].rearrange("p h q -> p (h q)"),
                              masks, vsa[g], vsa[g][:, kt, :], 65, acc, kt == 0, kt == L)
                ob = obr[1]
                vop("tensor_copy", [acc], [ob], out=ob[:], in_=acc[:].rearrange("p (h d) -> p h d", h=4)[:, :, 0:65])
                acc = pacc[n_st[0] % 2]; n_st[0] += 1
                k0 = max(0, L - 4)
                for kt in range(k0, L + 1):
                    po = pst[kt % 2]
                    masks = []
                    if kt == L - 4:
                        masks.append((mask3, mask3[:, 1, :]))
                    if kt == L:
                        masks.append((mask3, mask3[:, 0, :]))
                    if kt == 0:
                        masks.append((mask3, mask3[:, 2, :]))
                    attn_tile(g, po, kwT[g], kwT[g][:, kt % 6, :], qaug[g][0:64, :, :].rearrange("p h q -> p (h q)"),
                              masks, vwa[g], vwa[g][:, kt % 6, :], 65, acc, kt == k0, kt == L)
                ob = obr[2]
                vop("tensor_copy", [acc], [ob], out=ob[:], in_=acc[:].rearrange("p (h d) -> p h d", h=4)[:, :, 0:65])
                for b_ in (1, 2):
                    vop("reciprocal", [obr[b_]], [rden], out=rden[:, b_, :], in_=obr[b_][:, :, 64])
                gv = gates[:, g * 12:(g + 1) * 12].rearrange("p (h t) -> p t h", h=4)
                vop("tensor_tensor", [rden, gates], [coef], out=coef[:], in0=rden[:], in1=gv, op=ALU.mult)
                og = onsa[:, g * 4:(g + 1) * 4, :]
                vop("tensor_tensor", [obr[0], coef], [onsa], out=og, in0=obr[0][:, :, 0:64], in1=coef[:, 0, :].unsqueeze(2).to_broadcast([128, 4, 64]), op=ALU.mult)
                for b_ in (1, 2):
                    vop("tensor_tensor", [obr[b_], coef], [otmp], out=otmp[:], in0=obr[b_][:, :, 0:64], in1=coef[:, b_, :].unsqueeze(2).to_broadcast([128, 4, 64]), op=ALU.mult)
                    vop("tensor_tensor", [onsa, otmp], [onsa], out=og, in0=og, in1=otmp[:], op=ALU.add)
            vop("tensor_tensor", [onsa, ga_s], [ymix], out=ymix[:, 0:512], in0=onsa[:].rearrange("p h d -> p (h d)"), in1=ga_s[:], op=ALU.mult)
            rms_heads(oret, oret[:].rearrange("p h d -> p (h d)"), 4, 128, gvec[:, 256:384], oret, oret[:].rearrange("p h d -> p (h d)"), 128 * EPS)
            vop("tensor_tensor", [oret, gr_s], [ymix], out=ymix[:, 512:1024], in0=oret[:].rearrange("p h d -> p (h d)"), in1=gr_s[:], op=ALU.mult)
            if os.environ.get('KS_DBGOUT', '0') == '1':
                dma(G, None, dbg_out[j * 128:(j + 1) * 128, :], ymix, ymix[:], "x")
            for kt in range(8):
                tr(ptr, ptr[:, kt * 128:(kt + 1) * 128], ymix, ymix[:, kt * 128:(kt + 1) * 128], ident, ident[:])
            vop("tensor_copy", [ptr], [yT], out=yT[:].rearrange("p k t -> p (k t)"), in_=ptr[:])
            ys = yst[0]
            for n in range(2):
                po = pin[n]
                for kt in range(8):
                    mm(po, po[:], yT, yT[:, kt, :], w_out_bf, w_out_bf[:, kt, n * 512:(n + 1) * 512], kt == 0, kt == 7)
                vop("tensor_tensor", [po, gate_bc], [ys], out=ys[:, n * 512:(n + 1) * 512], in0=po[:], in1=gate_bc[:, n * 512:(n + 1) * 512], op=ALU.mult)
                vop("tensor_tensor", [ys, x_t], [ys], out=ys[:, n * 512:(n + 1) * 512], in0=ys[:, n * 512:(n + 1) * 512], in1=x_t[:, n * 512:(n + 1) * 512], op=ALU.add)
            dma(S, None, y_out[j * 128:(j + 1) * 128, :], ys, ys[:], "oy%d" % (j % 2))


    try:
        for L in range(RUN_NT):
            _tile(L)
    except _Stop:
        pass

    P.emit(final=False)
    P.scope_pop()
    if sample and os.environ.get('KS_NOSAMPLE', '0') != '1':
        _sample_phase(locals())
    P.emit(final=True)
    _NC_CACHE['stats'] = P.stats
    _NC_CACHE['ops'] = [(o['eng'], o.get('waits', []), o.get('ev') if o['sig'] else None, o['dma']) for o in P.ops]
    es.close()
    return nc


def _sample_phase(E):
    nc, P = E["nc"], E["P"]
    sb, dma, vop, act, mm, tr, rms_heads = E["sb"], E["dma"], E["vop"], E["act"], E["mm"], E["tr"], E["rms_heads"]
    V, A, T, G, S = "dve", "act", "pe", "pool", "sp"
    w_in_bf, w_out_bf, ident, ident_f, ones_b = E["w_in_bf"], E["w_out_bf"], E["ident"], E["ident_f"], E["ones_b"]
    w1blk, w2blk, cbias, modA, modS, gvec, modT = E["w1blk"], E["w2blk"], E["cbias"], E["modA"], E["modS"], E["gvec"], E["modT"]
    ptr, pin, pst, pacc, sacc = E["ptr"], E["pin"], E["pst"], E["pacc"], E["paccw"]
    CG2 = "const2"
    f4 = lambda t: t[0:4]
    xs4 = sb("xs4", [4, D], F32); xs4b = sb("xs4b", [4, D], BF16)
    ss4 = sb("ss4", [4, 1], F32); rs4 = sb("rs4", [4, 1], F32)
    hTf = sb("hTf", [128, 8, 4], F32); hTs = sb("hTs", [128, 8, 4], BF16)
    sq4 = sb("sq4", [4, 512], F32); ssq4 = sb("ssq4", [4, 16], F32); rq4 = sb("rq4", [4, 16], F32)
    scr4 = (sq4, ssq4, rq4)
    qn_s = sb("qn_s", [4, 512], BF16)
    kvf_s = sb("kvf_s", [4, 512], F32); kwf_s = sb("kwf_s", [4, 280], F32)
    gates_s = sb("gates_s", [4, 24], BF16)
    ga4 = sb("ga4", [4, 512], BF16); gr4 = sb("gr4", [4, 512], BF16)
    rope_s = sb("rope_s", [4, 4, 32], F32)
    rot_s = sb("rot_s", [4, 2, 4, 64], F32); t1 = sb("t1s", [4, 2, 4, 32], F32); t2 = sb("t2s", [4, 2, 4, 32], F32)
    rot_b = sb("rot_b", [4, 4, 64], BF16)
    v_s = sb("v_s", [4, 4, 128], F32)
    qblk = sb("qblk", [128, 4, 8], BF16)
    gaT = sb("gaT", [128, 4, 4], F32)
    gbc = sb("gbc", [128, 4, 24], F32)
    oh4 = sb("oh4", [4, 4, 128], BF16); oh4f = sb("oh4f", [4, 4, 128], F32)
    ohq = sb("ohq", [64, 4, 4], F32)
    pcol = sb("pcol", [128, 6], F32)
    wfull = sb("wfull", [128, 8, 256], BF16)
    topc = sb("topc", [8, 2, 256], F32)
    Sst = sb("Sst", [64, 4, 4, 128], F32); Sbf = sb("Sbf", [64, 4, 4, 128], BF16)
    kT_s = sb("kT_s", [64, 4, 4], F32); qT2 = sb("qT2", [64, 4, 4], F32); qTm = sb("qTm", [64, 4, 4, 4], BF16)
    qk = sb("qk", [4, 4], F32); prod = sb("prod", [4, 4, 64], F32)
    ors = sb("ors", [4, 4, 128], F32); yr4 = sb("yr4", [4, 512], BF16)
    ymT = sb("ymT", [128, 8, 4], BF16)
    ys4 = sb("ys4", [4, D], F32)
    xT = [sb("xTc%d" % i, [128, 16, 1 + 31 * 8], BF16) for i in range(2)]
    hS = [sb("hS%d" % i, [128, 1024], BF16) for i in range(2)]
    pgt = [sb("pgt%d" % i, [128, 256], F32) for i in range(8)]
    pgb = [sb("pgb%d" % i, [128, 256], BF16) for i in range(2)]
    ptb = sb("ptb", [128, 128], I32); ptf = sb("ptf", [128, 128], F32); idxb = sb("idxb", [128, 128], I32)
    kcn_s = sb("kcn_s", [128, 128], F32); kcb_s = sb("kcb_s", [128, 128], BF16); kcT_s = sb("kcT_s", [128, 128], BF16)
    sqs = sb("sqs", [128, 128], F32); ssqs = sb("ssqs", [128, 4], F32); rqs = sb("rqs", [128, 4], F32)
    Vd = sb("Vd", [128, 2, 2, 64], BF16)
    E8 = sb("E8", [128, 8], BF16)
    wt = sb("wt", [128, 4, 256], F32); wtb = sb("wtb", [128, 4, 256], BF16)
    res = sb("res", [128, 256], F32)
    rdn = sb("rdn", [128, 3, 32], F32); cf = sb("cf", [128, 3, 32], F32); comb = sb("comb", [128, 32], F32); ctmp = sb("ctmp", [128, 32], F32)
    impT = sb("impT", [128, 2, 8], F32); atn = sb("atn", [128, 2, 32], F32)
    sc8 = sb("sc8", [8, 256], F32); sw8 = sb("sw8", [8, 256], F32); mx = sb("mx16", [8, 16], F32); th8 = sb("th8", [8, 1], F32)
    sv = sb("sv", [8, 16], F32)
    s128 = sb("s128", [128, 1], F32); h128 = sb("h128", [128, 1], F32); d128 = sb("d128", [128, 1], F32); i128 = sb("i128", [128, 1], I32)
    pg128 = sb("pg128", [128, 1], I32); pf128 = sb("pf128", [128, 1], F32)
    rbb = sb("rbb", [128, 8, 8], F32); gidx = sb("gidx", [128, 8, 8], I32)
    T_scr1 = TW(None, "scr1"); T_scr2 = TW(None, "scr2")
    scr1, scr2 = E["scr1"], E["scr2"]
    cache_rows = [E["cache_cmp"], E["cache_slc"]]

    dma(S, xs4, xs4[:], None, E["xs_in"], CG2)
    dma(S, rope_s, rope_s[:], None, E["c_rope_s"], CG2)
    dma(S, oh4f, oh4f[:], None, E["c_oh4"], CG2)
    dma(G, oh4, oh4[:], None, E["c_oh4"], CG2)
    dma(S, ohq, ohq[:], None, E["c_ohq"], CG2)
    dma(S, pcol, pcol[:], None, E["c_pcol"], CG2)
    dma(G, wfull, wfull[:], None, E["c_wfull"], CG2)
    dma(S, topc, topc[:], None, E["c_top"], CG2)
    for b in range(4):
        dma(S, Sst, Sst[:, b], None, E["sret_in"][b].rearrange("h k v -> k h v"), CG2)
    for b in range(4):
        P.op(S, (lambda b=b: nc.sync.dma_start(out=E["swin_out"][b, 0:511, :], in_=E["swin_in"][b, 1:512, :])), [], [], dma="d2d")

    gate_s = sb("gate_s", [4, D], F32)
    for n in range(2):
        po = pacc[n]
        for k4 in range(4):
            kt = n * 4 + k4
            P.op(T, (lambda po=po, kt=kt, k4=k4: nc.tensor.transpose(out=po[0:4, k4 * 128:(k4 + 1) * 128], in_=modT[:, 16 + kt, 1:5], identity=ident_f[:])), [modT, ident_f], [po])
        vop("tensor_copy", [po], [gate_s], out=gate_s[:, n * 512:(n + 1) * 512], in_=po[0:4, :])
    vop("memset", [], [ss4], eng=G, ap=ss4[:], constant=0.0)
    act(xs4b, xs4b[:], xs4, xs4[:], AF.Square, accum_out=ss4[:], extra_writes=[ss4])
    vop("tensor_scalar", [ss4], [rs4], out=rs4[:], in0=ss4[:], scalar1=1.0 / D, scalar2=EPS, op0=ALU.mult, op1=ALU.add)
    act(rs4, rs4[:], rs4, rs4[:], AF.Sqrt)
    vop("reciprocal", [rs4], [rs4], out=rs4[:], in_=rs4[:])
    P.op(A, lambda: nc.scalar.mul(out=xs4b[:], in_=xs4[:], mul=rs4[:, 0:1]), [xs4, rs4], [xs4b])
    for kt in range(8):
        tr(ptr, ptr[:, kt * 4:(kt + 1) * 4], xs4b, xs4b[:, kt * 128:(kt + 1) * 128], ident, ident[0:4, 0:4])
    vop("tensor_tensor", [ptr, modA], [hTf], out=hTf[:], in0=ptr[:, 0:32].rearrange("p (k t) -> p k t", k=8), in1=modA[:, :, 1:5], op=ALU.mult)
    vop("tensor_tensor", [hTf, modS], [hTs], out=hTs[:], in0=hTf[:], in1=modS[:, :, 1:5], op=ALU.add)

    def inproj(c0, cw, po):
        for kt in range(8):
            mm(po, po[0:4, 0:cw], hTs, hTs[:, kt, :], w_in_bf, w_in_bf[:, kt, c0:c0 + cw], kt == 0, kt == 7)

    po = pin[0]; inproj(O_Q, 512, po)
    rms_heads(po, po[0:4, :], 8, 64, gvec[0:4, 0:64], qn_s, qn_s[:], 64 * EPS, np_=4, scr=scr4)
    po = pin[1]; inproj(O_KC, 512, po)
    vop("tensor_copy", [po], [kvf_s], out=kvf_s[:], in_=po[0:4, :])
    rms_heads(kvf_s, kvf_s[:, 256:384], 2, 64, gvec[0:4, 128:192], kvf_s, kvf_s[:, 256:384], 64 * EPS, np_=4, scr=scr4)
    dma(S, None, E["scmp_out"], kvf_s, kvf_s[:, 0:256], "x")
    dma(S, None, E["sslc_out"], kvf_s, kvf_s[:, 256:512], "x")
    po = pin[0]; inproj(O_KW, 280, po)
    vop("tensor_copy", [po], [kwf_s], out=kwf_s[:], in_=po[0:4, 0:280])
    rms_heads(kwf_s, kwf_s[:, 0:128], 2, 64, gvec[0:4, 192:256], kwf_s, kwf_s[:, 0:128], 64 * EPS, np_=4, scr=scr4)
    act(gates_s, gates_s[:], kwf_s, kwf_s[:, 256:280], AF.Sigmoid)
    for b in range(4):
        dma(S, None, E["swin_out"][b, 511:512, :], kwf_s, kwf_s[b:b + 1, 0:256], "x")
    po = pin[1]; inproj(O_GA, 512, po)
    act(ga4, ga4[:], po, po[0:4, :], AF.Silu)
    po = pin[0]; inproj(O_GR, 512, po)
    act(gr4, gr4[:], po, po[0:4, :], AF.Silu)
    po = pin[1]; inproj(O_RV, 512, po)
    vop("tensor_copy", [po], [v_s], out=v_s[:].rearrange("p h d -> p (h d)"), in_=po[0:4, :])
    po = pin[0]; inproj(O_RQ, 512, po)
    for s_ in range(2):
        xv = po[0:4, s_ * 256:(s_ + 1) * 256].rearrange("p (h d) -> p h d", h=4)
        cs = rope_s[:, 2 * s_, :].unsqueeze(1).to_broadcast([4, 4, 32])
        sn = rope_s[:, 2 * s_ + 1, :].unsqueeze(1).to_broadcast([4, 4, 32])
        vop("tensor_tensor", [po, rope_s], [t1], out=t1[:, s_], in0=xv[:, :, 0:32], in1=cs, op=ALU.mult)
        vop("tensor_tensor", [po, rope_s], [t2], out=t2[:, s_], in0=xv[:, :, 32:64], in1=sn, op=ALU.mult)
        vop("tensor_tensor", [t1, t2], [rot_s], out=rot_s[:, s_, :, 0:32], in0=t1[:, s_], in1=t2[:, s_], op=ALU.subtract)
        vop("tensor_tensor", [po, rope_s], [t1], out=t1[:, s_], in0=xv[:, :, 0:32], in1=sn, op=ALU.mult)
        vop("tensor_tensor", [po, rope_s], [t2], out=t2[:, s_], in0=xv[:, :, 32:64], in1=cs, op=ALU.mult)
        vop("tensor_tensor", [t1, t2], [rot_s], out=rot_s[:, s_, :, 32:64], in0=t1[:, s_], in1=t2[:, s_], op=ALU.add)

    vop("memset", [], [qblk], eng=G, ap=qblk[:], constant=0.0)
    for i in range(4):
        for g in range(2):
            c0 = g * 256 + i * 64
            tr(ptr, ptr[g * 64:(g + 1) * 64, 64 + i * 4:64 + (i + 1) * 4], qn_s, qn_s[:, c0:c0 + 64], ident, ident[0:4, 0:4])
    qv = ptr[:, 64:80].rearrange("p (i b) -> p b i", i=4)
    vop("tensor_copy", [ptr], [qblk], out=qblk[0:64, :, 0:4], in_=qv[0:64])
    vop("tensor_copy", [ptr], [qblk], out=qblk[64:128, :, 4:8], in_=qv[64:128])
    for kt in range(4):
        tr(ptr, ptr[:, 96 + kt * 4:96 + (kt + 1) * 4], ga4, ga4[:, kt * 128:(kt + 1) * 128], ident, ident[0:4, 0:4])
    vop("tensor_copy", [ptr], [gaT], out=gaT[:].rearrange("p k b -> p (k b)"), in_=ptr[:, 96:112])
    po = pacc[0]
    for b in range(4):
        mm(po, po[:, b * 24:(b + 1) * 24], oh4, oh4[:, b, :], gates_s, gates_s[:], True, True)
    vop("tensor_copy", [po], [gbc], out=gbc[:].rearrange("p b t -> p (b t)"), in_=po[:, 0:96])

    vop("tensor_copy", [Sst], [Sbf], out=Sbf[:], in_=Sst[:])
    DBG3 = os.environ.get('KS_DBGOUT', '0') == '1'
    if DBG3:
        dma(S, None, E["dbg3_out"][0:64, 0:512], Sst, Sst[:, 0].rearrange("p h v -> p (h v)"), "x")
        dma(S, None, E["dbg3_out"][0:4, 512:1024], rot_s, rot_s[:].rearrange("p s h d -> p (s h d)"), "x")
        dma(S, None, E["dbg3_out"][0:4, 1024:1536], v_s, v_s[:].rearrange("p h d -> p (h d)"), "x")
    rot_b2 = sb("rot_b2", [4, 2, 4, 64], BF16)
    vop("tensor_copy", [rot_s], [rot_b2], out=rot_b2[:], in_=rot_s[:])
    for h in range(4):
        tr(ptr, ptr[0:64, 160 + h * 4:160 + (h + 1) * 4], rot_b2, rot_b2[:, 1, h, :], ident, ident[0:4, 0:4])
        tr(ptr, ptr[0:64, 176 + h * 4:176 + (h + 1) * 4], rot_b2, rot_b2[:, 0, h, :], ident, ident[0:4, 0:4])
    vop("tensor_copy", [ptr], [kT_s], out=kT_s[:].rearrange("p h b -> p (h b)"), in_=ptr[0:64, 160:176])
    vop("tensor_copy", [ptr], [qT2], out=qT2[:].rearrange("p h b -> p (h b)"), in_=ptr[0:64, 176:192])
    vop("tensor_tensor", [qT2, ohq], [qTm], out=qTm[:], in0=qT2[:].unsqueeze(1).to_broadcast([64, 4, 4, 4]),
        in1=ohq[:].unsqueeze(2).to_broadcast([64, 4, 4, 4]), op=ALU.mult)
    po = pst[0]
    first = True
    for h in range(4):
        for b in range(4):
            mm(po, po[0:4, h * 128:(h + 1) * 128], qTm, qTm[:, b, h, :], Sbf, Sbf[:, b, h, :], first, (h == 3 and b == 3), sgc=True)
            first = False
    if DBG3:
        dma(S, None, E["dbg3_out"][0:64, 1536:1552], kT_s, kT_s[:].rearrange("p h b -> p (h b)"), "x")
        dma(S, None, E["dbg3_out"][0:64, 1552:1568], qT2, qT2[:].rearrange("p h b -> p (h b)"), "x")
    vop("tensor_tensor", [rot_s], [prod], out=prod[:], in0=rot_s[:, 0], in1=rot_s[:, 1], op=ALU.mult)
    vop("tensor_reduce", [prod], [qk], out=qk[:], in_=prod[:], axis=AX.X, op=ALU.add)
    vop("tensor_tensor", [v_s, qk], [ors], out=ors[:], in0=v_s[:], in1=qk[:].unsqueeze(2).to_broadcast([4, 4, 128]), op=ALU.mult)
    for h in range(4):
        gam = float(1.0 - 2.0 ** (-5.0 - h))
        vop("scalar_tensor_tensor", [po, ors], [ors], out=ors[:, h, :], in0=po[0:4, h * 128:(h + 1) * 128], scalar=gam, in1=ors[:, h, :], op0=ALU.mult, op1=ALU.add)
    for b in range(4):
        pv = pst[1]
        P.op(T, (lambda b=b, pv=pv: nc.tensor.matmul(pv[0:64, :], lhsT=oh4f[:, b, 0:64], rhs=v_s[:].rearrange("p h d -> p (h d)"), start=True, stop=True)), [oh4f, v_s], [pv])
        for h in range(4):
            gam = float(1.0 - 2.0 ** (-5.0 - h))
            vop("tensor_scalar", [Sst], [Sst], out=Sst[:, b, h, :], in0=Sst[:, b, h, :], scalar1=gam, scalar2=None, op0=ALU.mult)
            vop("scalar_tensor_tensor", [pv, kT_s, Sst], [Sst], out=Sst[:, b, h, :], in0=pv[0:64, h * 128:(h + 1) * 128], scalar=kT_s[:, h, b:b + 1], in1=Sst[:, b, h, :], op0=ALU.mult, op1=ALU.add)
    for b in range(4):
        dma(S, None, E["sret_out"][b].rearrange("h k v -> k h v"), Sst, Sst[:, b], "x")
    rms_heads(ors, ors[:].rearrange("p h d -> p (h d)"), 4, 128, gvec[0:4, 256:384], ors, ors[:].rearrange("p h d -> p (h d)"), 128 * EPS, np_=4, scr=scr4)
    vop("tensor_tensor", [ors, gr4], [yr4], out=yr4[:], in0=ors[:].rearrange("p h d -> p (h d)"), in1=gr4[:], op=ALU.mult)
    for kt in range(4):
        tr(ptr, ptr[:, 128 + kt * 4:128 + (kt + 1) * 4], yr4, yr4[:, kt * 128:(kt + 1) * 128], ident, ident[0:4, 0:4])
    vop("tensor_copy", [ptr], [ymT], out=ymT[:, 4:8, :].rearrange("p k b -> p (k b)"), in_=ptr[:, 128:144])

    sfirst = [True]

    def smm(col, n, l_t, l_ap, r_t, r_ap):
        mm(sacc, sacc[:, col:col + n], l_t, l_ap, r_t, r_ap, sfirst[0], False, sgc=True)
        sfirst[0] = False

    OC = lambda br: br * 64

    def key_tile(br, b, kT_t, kT_ap, vsrc_t, vsrc_ap, cols, bias_ap, po):
        mm(po, po[:, 0:8], kT_t, kT_ap, qblk, qblk[:, b, :], True, True)
        if bias_ap is None:
            act(E8, E8[:], po, po[:, 0:8], AF.Exp)
        else:
            act(E8, E8[:], po, po[:, 0:8], AF.Exp, extra_reads=[pcol], bias=bias_ap)
        vop("tensor_copy", [vsrc_t], [Vd], out=Vd[:], in_=vsrc_ap.rearrange("p (g d) -> p g d", g=2).unsqueeze(2).to_broadcast([128, 2, 2, 64]))
        for g in cols:
            smm(OC(br) + b * 8 + g * 4, 4, Vd, Vd[:, g].rearrange("p r d -> p (r d)"), E8, E8[:, g * 4:(g + 1) * 4])
            smm(OC(br) + 32 + b * 8 + g * 4, 4, ones_b, ones_b[:], E8, E8[:, g * 4:(g + 1) * 4])

    for b in range(4):
        dma(S, wt, wt[:], None, E["swin_in"][b].rearrange("(t p) c -> p t c", p=128), "x")
        dma(S, wt, wt[0:1, 0, :], kwf_s, kwf_s[b:b + 1, 0:256], "x")
        for t in range(4):
            pp = pst[t % 2]
            P.op(T, (lambda pp=pp, t=t: nc.tensor.transpose(out=pp[:, 0:128], in_=wt[:, t, 0:128], identity=ident_f[:])), [wt, ident_f], [pp])
            vop("tensor_copy", [pp], [kcT_s], out=kcT_s[:], in_=pp[:, 0:128])
            key_tile(2, b, kcT_s, kcT_s[:], wt, wt[:, t, 128:256], (0, 1), None, pin[t % 2])

    CH = [31, 31, 31, 31, 4]
    npg = [0]
    idx_all = sb("idx_all", [128, 4, 128], I32)
    for b in range(4):
        dma(S, ptb, ptb[:], None, E["ptc"][b:b + 1, :].to_broadcast([128, 128]), "x")
        vop("tensor_copy", [ptb], [ptf], out=ptf[:], in_=ptb[:])
        vop("tensor_scalar", [ptf], [ptf], out=ptf[:], in0=ptf[:], scalar1=128.0, scalar2=None, op0=ALU.mult)
        vop("tensor_tensor", [ptf, pcol], [ptf], out=ptf[:], in0=ptf[:], in1=pcol[:, 0:1].to_broadcast([128, 128]), op=ALU.add)
        vop("tensor_copy", [ptf], [idx_all], out=idx_all[:, b, :], in_=ptf[:])
    PF = 6
    pages = [(b, j) for b in range(4) for j in range(128)]
    issued = []

    def issue_next():
        k = len(issued)
        if k >= len(pages):
            return
        b_, j_ = pages[k]
        pt_ = pgt[k % 8]
        P.op(G, (lambda pt_=pt_, b_=b_, j_=j_: nc.gpsimd.indirect_dma_start(out=pt_[:], out_offset=None, in_=cache_rows[0],
             in_offset=bass.IndirectOffsetOnAxis(ap=idx_all[:, b_, j_:j_ + 1], axis=0))), [idx_all], [pt_], dma=pt_.name)
        issued.append(pt_)

    for _ in range(PF):
        issue_next()
    kpage = 0
    for b in range(4):
        for i in range(2):
            vop("memset", [], [xT[i]], eng=G, ap=xT[i][:, :, 0:1], constant=0.0)
        j0 = 0
        for c, npages in enumerate(CH):
            for jj in range(npages):
                pt_ = issued[kpage]
                issue_next()
                pp = pst[kpage % 2]
                kpage += 1
                for i in range(2):
                    P.op(T, (lambda pp=pp, pt_=pt_, i=i: nc.tensor.transpose(out=pp[:, i * 128:(i + 1) * 128], in_=pt_[:, i * 128:(i + 1) * 128], identity=ident_f[:])), [pt_, ident_f], [pp])
                for i in range(2):
                    vop("tensor_copy", [pp], [xT[i]], out=xT[i][:, :, 1 + jj * 8:1 + (jj + 1) * 8], in_=pp[:, i * 128:(i + 1) * 128].rearrange("p (m r) -> p r m", r=16))
            R0 = j0 * 128
            nblk = npages * 8
            nb0 = R0 // 16 - 1
            c_lo = 1 if c == 0 else 0
            for i in range(2):
                po = pin[i]
                nn = nblk - c_lo
                for l in range(32):
                    a_, r_ = l // 16, l % 16
                    mm(po, po[:, 0:nn], w1blk[i], w1blk[i][:, l, :], xT[i], xT[i][:, r_, c_lo + a_:c_lo + a_ + nn], l == 0, l == 31)
                act(hS[i], hS[i][:, nb0 + c_lo:nb0 + nblk], po, po[:, 0:nn], AF.Silu, extra_reads=[cbias[i]], bias=cbias[i][:, 0:1])
                vop("tensor_copy", [xT[i]], [xT[i]], out=xT[i][:, :, 0:1], in_=xT[i][:, :, npages * 8:npages * 8 + 1])
            j0 += npages
        for i in range(2):
            vop("memset", [], [hS[i]], eng=G, ap=hS[i][:, 1023:1024], constant=0.0)
        for nt in range(8):
            po = pin[nt % 2]
            mm(po, po[:, 0:128], hS[0], hS[0][:, nt * 128:(nt + 1) * 128], w2blk[0], w2blk[0][:], True, True)
            mm(po, po[:, 128:256], hS[1], hS[1][:, nt * 128:(nt + 1) * 128], w2blk[1], w2blk[1][:], True, True)
            vop("tensor_copy", [po], [kcn_s], out=kcn_s[:], in_=po[:, 0:128])
            rms_heads(kcn_s, kcn_s[:], 2, 64, gvec[:, 64:128], kcb_s, kcb_s[:], 64 * EPS, np_=128, scr=(sqs, ssqs, rqs))
            tr(ptr, ptr[:, 768:896], kcb_s, kcb_s[:], ident, ident[:])
            vop("tensor_copy", [ptr], [kcT_s], out=kcT_s[:], in_=ptr[:, 768:896])
            ps2 = pst[nt % 2]
            mm(ps2, ps2[:, 0:8], kcT_s, kcT_s[:], qblk, qblk[:, b, :], True, True)
            if nt == 7:
                act(E8, E8[:], ps2, ps2[:, 0:8], AF.Exp, extra_reads=[pcol], bias=pcol[:, 4:5])
            else:
                act(E8, E8[:], ps2, ps2[:, 0:8], AF.Exp)
            vop("tensor_copy", [po], [Vd], out=Vd[:], in_=po[:, 128:256].rearrange("p (g d) -> p g d", g=2).unsqueeze(2).to_broadcast([128, 2, 2, 64]))
            for g in range(2):
                smm(OC(0) + b * 8 + g * 4, 4, Vd, Vd[:, g].rearrange("p r d -> p (r d)"), E8, E8[:, g * 4:(g + 1) * 4])
            smm(OC(0) + 32 + b * 8, 8, ones_b, ones_b[:], E8, E8[:])
            for st in range(2):
                smm(192 + st * 32 + b * 8, 8, wfull, wfull[:, nt, st * 128:(st + 1) * 128], E8, E8[:])

    vop("tensor_copy", [sacc], [res], out=res[:], in_=sacc[:, 0:256])
    vop("reciprocal", [res], [rdn], out=rdn[:, 0, :], in_=res[:, 32:64])
    vop("tensor_tensor", [res, rdn], [atn], out=atn[:], in0=res[:, 192:256].rearrange("p (s c) -> p s c", s=2),
        in1=rdn[:, 0, :].unsqueeze(1).to_broadcast([128, 2, 32]), op=ALU.mult)
    vop("tensor_reduce", [atn], [impT], out=impT[:].rearrange("p s c -> p (s c)"), in_=atn[:].rearrange("p s (c i) -> p (s c) i", i=4), axis=AX.X, op=ALU.add)
    po = pacc[0]
    for st in range(2):
        P.op(T, (lambda st=st, po=po: nc.tensor.transpose(out=po[0:8, st * 128:(st + 1) * 128], in_=impT[:, st, :], identity=ident_f[:])), [impT, ident_f], [po])
    vop("tensor_tensor", [po, topc], [sc8], out=sc8[:], in0=po[0:8, 0:256], in1=topc[:, 0, :], op=ALU.add)
    vop("max", [sc8], [mx], out=mx[:, 0:8], in_=sc8[:])
    vop("match_replace", [mx, sc8], [sw8], out=sw8[:], in_to_replace=mx[:, 0:8], in_values=sc8[:], imm_value=-3e30)
    vop("max", [sw8], [mx], out=mx[:, 8:16], in_=sw8[:])
    vop("tensor_scalar", [sc8, mx], [sw8], out=sw8[:], in0=sc8[:], scalar1=mx[:, 14:15], scalar2=None, op0=ALU.is_ge)
    vop("tensor_tensor", [sw8, topc], [sw8], out=sw8[:], in0=sw8[:], in1=topc[:, 1, :], op=ALU.mult)
    vop("max", [sw8], [sv], out=sv[:, 0:8], in_=sw8[:])
    vop("match_replace", [sv, sw8], [sc8], out=sc8[:], in_to_replace=sv[:, 0:8], in_values=sw8[:], imm_value=0.0)
    vop("max", [sc8], [sv], out=sv[:, 8:16], in_=sc8[:])
    vop("tensor_scalar", [sv], [sv], out=sv[:], in0=sv[:], scalar1=-1.0, scalar2=0.0, op0=ALU.add, op1=ALU.max)
    P.op(S, lambda: nc.sync.dma_start(out=scr1.rearrange("(a k) o -> a (k o)", a=8), in_=sv[:]), [sv], [T_scr1], dma="scr1")
    P.op(S, lambda: nc.sync.dma_start(out=s128[:], in_=scr1), [T_scr1], [s128], dma="s128")
    vop("tensor_scalar", [s128], [d128], out=d128[:], in0=s128[:], scalar1=0.5, scalar2=-0.25, op0=ALU.mult, op1=ALU.add)
    vop("tensor_scalar", [d128], [d128], out=d128[:], in0=d128[:], scalar1=8388608.0, scalar2=None, op0=ALU.add)
    vop("tensor_scalar", [d128], [d128], out=d128[:], in0=d128[:], scalar1=-8388608.0, scalar2=None, op0=ALU.add)
    vop("tensor_scalar", [d128], [h128], out=h128[:], in0=d128[:], scalar1=-2.0, scalar2=None, op0=ALU.mult)
    vop("tensor_tensor", [s128, h128], [h128], out=h128[:], in0=s128[:], in1=h128[:], op=ALU.add)
    vop("tensor_tensor", [d128, pcol], [d128], out=d128[:], in0=d128[:], in1=pcol[:, 2:3], op=ALU.add)
    vop("tensor_copy", [d128], [i128], out=i128[:], in_=d128[:])
    P.op(G, lambda: nc.gpsimd.indirect_dma_start(out=pg128[:], out_offset=None, in_=E["ptc"].rearrange("b (j o) -> (b j) o", o=1),
         in_offset=bass.IndirectOffsetOnAxis(ap=i128[:, 0:1], axis=0)), [i128], [pg128], dma="pg128")
    vop("tensor_copy", [pg128], [pf128], out=pf128[:], in_=pg128[:])
    vop("tensor_scalar", [pf128], [pf128], out=pf128[:], in0=pf128[:], scalar1=128.0, scalar2=None, op0=ALU.mult)
    vop("tensor_scalar", [h128], [h128], out=h128[:], in0=h128[:], scalar1=64.0, scalar2=None, op0=ALU.mult)
    vop("tensor_tensor", [pf128, h128], [pf128], out=pf128[:], in0=pf128[:], in1=h128[:], op=ALU.add)
    P.op(S, lambda: nc.sync.dma_start(out=scr2, in_=pf128[:]), [pf128], [T_scr2], dma="scr2")
    s2v = scr2.rearrange("(a t h) o -> h a (t o)", a=8, t=8, h=2)
    for hh in range(2):
        P.op(S, (lambda hh=hh: nc.sync.dma_start(out=rbb[hh * 64:(hh + 1) * 64], in_=s2v[hh:hh + 1].to_broadcast([64, 8, 8]), allow_slow_non_contiguous=True)), [T_scr2], [rbb], dma="rbb%d" % hh)
    vop("tensor_tensor", [rbb, pcol], [rbb], out=rbb[:], in0=rbb[:], in1=pcol[:, 1:2].unsqueeze(2).to_broadcast([128, 8, 8]), op=ALU.add)
    vop("tensor_copy", [rbb], [gidx], out=gidx[:], in_=rbb[:])

    tiles = [(b, g, t) for b in range(4) for g in range(2) for t in range(8)]
    sissued = []

    def sissue_next():
        k = len(sissued)
        if k >= len(tiles):
            return
        b_, g_, t_ = tiles[k]
        pt_ = pgt[k % 8]
        P.op(G, (lambda pt_=pt_, bg=b_ * 2 + g_, t_=t_: nc.gpsimd.indirect_dma_start(out=pt_[:], out_offset=None, in_=cache_rows[1],
             in_offset=bass.IndirectOffsetOnAxis(ap=gidx[:, bg, t_:t_ + 1], axis=0))), [gidx], [pt_], dma=pt_.name)
        if t_ == 7:
            dma(S, pt_, pt_[64:65, :], kvf_s, kvf_s[b_:b_ + 1, 256:512], "x")
        sissued.append(pt_)

    for _ in range(PF):
        sissue_next()
    for k, (b, g, t) in enumerate(tiles):
        pt_ = sissued[k]
        sissue_next()
        pp = pst[k % 2]
        P.op(T, (lambda pp=pp, pt_=pt_: nc.tensor.transpose(out=pp[:, 0:128], in_=pt_[:, 0:128], identity=ident_f[:])), [pt_, ident_f], [pp])
        vop("tensor_copy", [pp], [kcT_s], out=kcT_s[:], in_=pp[:, 0:128])
        key_tile(1, b, kcT_s, kcT_s[:], pt_, pt_[:, 128:256], (g,), (pcol[:, 3:4] if t == 7 else None), pin[k % 2])

    vop("tensor_copy", [sacc], [res], out=res[:, 0:192], in_=sacc[:, 0:192])
    for br in range(3):
        vop("reciprocal", [res], [rdn], out=rdn[:, br, :], in_=res[:, br * 64 + 32:br * 64 + 64])
    vop("tensor_tensor", [rdn, gbc], [cf], out=cf[:].rearrange("p r (b c) -> p r b c", b=4), in0=rdn[:].rearrange("p r (b c) -> p r b c", b=4),
        in1=gbc[:].rearrange("p b (c r) -> p r b c", r=3), op=ALU.mult)
    vop("tensor_tensor", [res, cf], [comb], out=comb[:], in0=res[:, 0:32], in1=cf[:, 0, :], op=ALU.mult)
    for br in (1, 2):
        vop("tensor_tensor", [res, cf], [ctmp], out=ctmp[:], in0=res[:, br * 64:br * 64 + 32], in1=cf[:, br, :], op=ALU.mult)
        vop("tensor_tensor", [comb, ctmp], [comb], out=comb[:], in0=comb[:], in1=ctmp[:], op=ALU.add)
    cv = comb[:].rearrange("p (b g i r) -> p r g i b", b=4, g=2, i=2, r=2)
    for hf in range(2):
        ps_ = slice(hf * 64, (hf + 1) * 64)
        vop("tensor_tensor", [comb, gaT], [ymT], out=ymT[ps_, 0:4, :].rearrange("p (g i) b -> p g i b", g=2), in0=cv[ps_, hf],
            in1=gaT[ps_, :, :].rearrange("p (g i) b -> p g i b", g=2), op=ALU.mult)
    if os.environ.get('KS_DBGOUT', '0') == '1':
        dma(G, None, E["dbg2_out"][:, 0:32], ymT, ymT[:].rearrange("p k b -> p (k b)"), "x")
        dma(S, None, E["dbg2_out"][:, 32:64], comb, comb[:], "x")
    for n in range(2):
        po = pin[n]
        for kt in range(8):
            mm(po, po[0:4, :], ymT, ymT[:, kt, :], w_out_bf, w_out_bf[:, kt, n * 512:(n + 1) * 512], kt == 0, kt == 7)
        vop("tensor_tensor", [po, gate_s], [ys4], out=ys4[:, n * 512:(n + 1) * 512], in0=po[0:4, :], in1=gate_s[:, n * 512:(n + 1) * 512], op=ALU.mult)
        vop("tensor_tensor", [ys4, xs4], [ys4], out=ys4[:, n * 512:(n + 1) * 512], in0=ys4[:, n * 512:(n + 1) * 512], in1=xs4[:, n * 512:(n + 1) * 512], op=ALU.add)
    dma(S, None, E["ys_out"], ys4, ys4[:], "x")


def _consts(p):
    sh = 1 - p
    c = {}
    c["c_ident"] = np.eye(128, dtype=np.float32)
    n = np.arange(NT * 128)
    c["c_expand"] = (n[None, :] // 64 == np.arange(64)[:, None]).astype(np.float32)
    kk = np.arange(128)[:, None]
    qq = np.arange(128)[None, :]
    m3 = np.zeros((128, 3, 128), np.float32)
    m3[:, 0, :] = np.where(kk <= qq, 0.0, NEGB)
    m3[:, 1, :] = np.where(kk > qq, 0.0, NEGB)
    m3[:, 2, :] = NEGB if p == 0 else 0.0
    c["c_mask3"] = m3
    mc = np.zeros((128, 16, 2, 128), np.float32)
    for j in range(16):
        L = 2 * j + 1
        t = L * 128 + np.arange(128)[None, :]
        for nt in range(2):
            nn = nt * 128 + np.arange(128)[:, None]
            ok = (16 * nn + 31 <= t) & (nn >= 8 * sh)
            mc[:, j, nt, :] = np.where(ok, 0.0, NEGB)
    c["c_maskc"] = mc
    ad = np.zeros((128, 16, 64), np.float32)
    for j in range(16):
        L = 2 * j + 1
        t = L * 128 + np.arange(128)[:, None]
        s = np.arange(64)[None, :]
        jt = t // 64
        valid = (s * 64 <= t) & (s >= 2 * sh)
        forced = (s == 2 * sh) | (s == jt) | (s == jt - 1)
        ad[:, j, :] = np.where(valid, np.where(forced, 1e4, 0.0), -1e30)
    c["c_adds"] = ad
    half = 32
    freqs = (10000.0 ** (-np.arange(half, dtype=np.float32) / half)).astype(np.float32)
    rp = np.zeros((128, NT, 4, 32), np.float32)
    rs = np.zeros((128, NT, 8), np.float32)
    lg = np.log(1.0 - 2.0 ** (-5.0 - np.arange(4, dtype=np.float32))).astype(np.float32)
    i = np.arange(128, dtype=np.float32)
    for L in range(NT):
        gt = L - sh
        if gt < 0:
            continue
        pos = (gt * 128 + np.arange(128)).astype(np.float32)
        ang = pos[:, None] * freqs[None, :]
        rp[:, L, 0] = np.cos(ang); rp[:, L, 1] = np.sin(ang)
        rp[:, L, 2] = np.cos(ang) / 8.0; rp[:, L, 3] = np.sin(ang) / 8.0
        rs[:, L, 0:4] = np.exp(lg[None, :] * (127.0 - i[:, None]))
        rs[:, L, 4:8] = np.exp(lg[None, :] * (i[:, None] + 1.0))
    c["c_rope"] = rp
    c["c_rscal"] = rs
    w = np.zeros((255 + 1, 64), np.float32)
    for s in range(64):
        for nn, wt in ((4 * s - 1, 1), (4 * s, 2), (4 * s + 1, 2), (4 * s + 2, 2), (4 * s + 3, 1)):
            if 0 <= nn < 255:
                w[nn, s] += wt
    wl = np.zeros((256, 64), np.float32)
    for nn in range(255):
        for s in range(64):
            if w[nn, s] and nn + 8 * sh < 256 and s + 2 * sh < 64:
                wl[nn + 8 * sh, s + 2 * sh] = w[nn, s]
    c["c_wimp"] = wl.reshape(2, 128, 64).transpose(1, 0, 2).copy()
    jj = np.arange(128)[:, None]; ii = np.arange(128)[None, :]
    dm = np.zeros((128, 4, 128), np.float32)
    for h in range(4):
        dm[:, h, :] = np.where(ii >= jj, np.exp(lg[h] * np.maximum(ii - jj, 0)), 0.0)
    c["c_dmask"] = dm
    return {k: np.ascontiguousarray(v, dtype=np.float32) for k, v in c.items()}


def _sconsts():
    c = {}
    half = 32
    freqs = (10000.0 ** (-np.arange(half, dtype=np.float32) / half)).astype(np.float32)
    ang = np.float32(16384.0) * freqs
    r = np.stack([np.cos(ang), np.sin(ang), np.cos(ang) / 8.0, np.sin(ang) / 8.0]).astype(np.float32)
    c["c_rope_s"] = np.broadcast_to(r[None], (4, 4, 32)).copy()
    w = np.zeros((1024, 256), np.float32)
    for s_ in range(256):
        for nn, wt in ((4 * s_ - 1, 1), (4 * s_, 2), (4 * s_ + 1, 2), (4 * s_ + 2, 2), (4 * s_ + 3, 1)):
            if 0 <= nn < 1023:
                w[nn, s_] += wt
    c["c_wfull"] = w.reshape(8, 128, 256).transpose(1, 0, 2).copy()
    top = np.zeros((8, 2, 256), np.float32)
    top[:, 0, 0] = 1e4
    top[:, 0, 255] = 1e4
    top[:, 1, :] = np.arange(256, dtype=np.float32)[None, :] + 1.0
    c["c_top"] = top
    p = np.arange(128)
    pc = np.zeros((128, 6), np.float32)
    pc[:, 0] = p
    pc[:, 1] = p % 64
    pc[:, 2] = (p // 32) * 128
    pc[:, 3] = np.where(p <= 64, 0.0, NEGB)
    pc[:, 4] = np.where(p == 127, NEGB, 0.0)
    c["c_pcol"] = pc
    oh = np.zeros((4, 4, 128), np.float32)
    for b in range(4):
        oh[b, b, :] = 1.0
    c["c_oh4"] = oh
    c["c_ohq"] = np.broadcast_to(np.eye(4, dtype=np.float32)[None], (64, 4, 4)).copy()
    return {k: np.ascontiguousarray(v, dtype=np.float32) for k, v in c.items()}


_NC_CACHE = {}


def kernel(x_prompt, x_sample, c_prompt, c_sample, cache_cmp, cache_slc, state_win, state_ret, page_table,
           g_norm, w_ada, b_ada, w_in, g_q, g_kc, g_ks, g_kw, pe_ck, w_ck1, w_ck2, pe_cv, w_cv1, w_cv2,
           g_ret, w_out):
    f = lambda a: np.ascontiguousarray(np.asarray(a), dtype=np.float32)
    x_prompt = f(x_prompt)
    if "nc" not in _NC_CACHE:
        _NC_CACHE["nc"] = build_program()
    nc = _NC_CACHE["nc"]
    shared = dict(w_ada=f(w_ada)[0], b_ada=f(b_ada), w_in=f(w_in)[0], w_out=f(w_out)[0], g_norm=f(g_norm),
                  g_q=f(g_q), g_kc=f(g_kc), g_ks=f(g_ks), g_kw=f(g_kw), g_ret=f(g_ret),
                  pe_ck=f(pe_ck)[0], pe_cv=f(pe_cv)[0], w_ck1=f(w_ck1)[0], w_cv1=f(w_cv1)[0],
                  w_ck2=f(w_ck2)[0], w_cv2=f(w_cv2)[0])
    consts = [_consts(0), _consts(1)]
    sconst = _sconsts()
    cc = f(cache_cmp).reshape(-1, 256)
    cs_ = f(cache_slc).reshape(-1, 256)
    in_maps = []
    for c in range(8):
        b, p = c // 2, c % 2
        if p == 0:
            xl = np.concatenate([np.zeros((128, D), np.float32), x_prompt[b, :31 * 128]], axis=0)
        else:
            xl = x_prompt[b]
        cv = np.concatenate([f(c_prompt)[b:b + 1], f(c_sample)[4 * c:4 * c + 4]], axis=0)
        m = dict(shared)
        m.update(consts[p])
        m["xloc"] = np.ascontiguousarray(xl)
        m["cvec"] = np.ascontiguousarray(cv)
        m.update(sconst)
        m["xs_in"] = np.ascontiguousarray(f(x_sample)[4 * c:4 * c + 4, 0])
        m["ptc"] = np.ascontiguousarray(np.asarray(page_table)[4 * c:4 * c + 4].astype(np.int32))
        m["cache_cmp"] = cc
        m["cache_slc"] = cs_
        m["swin_in"] = np.ascontiguousarray(f(state_win)[0, 4 * c:4 * c + 4].reshape(4, 512, 256))
        m["sret_in"] = np.ascontiguousarray(f(state_ret)[0, 4 * c:4 * c + 4])
        in_maps.append(m)
    res = run_bass_kernel_spmd(nc, in_maps, core_ids=list(range(8)))
    R = res.results
    y_prompt = np.zeros((4, 4096, D), np.float32)
    p_cmp = np.zeros((1, 4, 4096, 256), np.float32)
    p_slc = np.zeros((1, 4, 4096, 256), np.float32)
    p_win = np.zeros((1, 4, 512, 256), np.float32)
    p_ret = np.zeros((1, 4, 4, 64, 128), np.float32)
    for c in range(8):
        b, p = c // 2, c % 2
        yo = R[c]["y_out"].reshape(16, 128, D)
        pc = R[c]["pcmp_out"].reshape(16, 128, 256)
        pl = R[c]["pslc_out"].reshape(16, 128, 256)
        pw = R[c]["pwin_out"].reshape(2, 128, 256)
        for j in range(16):
            gt = 2 * j + p
            y_prompt[b, gt * 128:(gt + 1) * 128] = yo[j]
            p_cmp[0, b, gt * 128:(gt + 1) * 128] = pc[j]
            p_slc[0, b, gt * 128:(gt + 1) * 128] = pl[j]
        for k in range(2):
            gt = 28 + 2 * k + p
            p_win[0, b, (gt - 28) * 128:(gt - 27) * 128] = pw[k]
        if p == 1:
            p_ret[0, b] = R[c]["pret_out"].transpose(1, 0, 2)
    y_sample = np.concatenate([R[c]["ys_out"] for c in range(8)], axis=0).reshape(32, 1, D)
    s_cmp = np.concatenate([R[c]["scmp_out"] for c in range(8)], axis=0).reshape(1, 32, 1, 2, 2, 64)
    s_slc = np.concatenate([R[c]["sslc_out"] for c in range(8)], axis=0).reshape(1, 32, 1, 2, 2, 64)
    s_win = np.concatenate([R[c]["swin_out"] for c in range(8)], axis=0).reshape(1, 32, 512, 2, 2, 64)
    s_ret = np.concatenate([R[c]["sret_out"] for c in range(8)], axis=0).reshape(1, 32, 4, 64, 128)
    outs = (y_prompt, y_sample, p_cmp.reshape(1, 4, 4096, 2, 2, 64), p_slc.reshape(1, 4, 4096, 2, 2, 64),
            p_win.reshape(1, 4, 512, 2, 2, 64), p_ret, s_cmp, s_slc, s_win, s_ret)
    return outs
```

```python
import os
import numpy as np
import ml_dtypes
from contextlib import ExitStack
import concourse.bass as bass
import concourse.mybir as mybir
from concourse.bass_utils import run_bass_kernel_spmd

F32 = mybir.dt.float32
BF16 = mybir.dt.bfloat16
I32 = mybir.dt.int32
AF = mybir.ActivationFunctionType
ALU = mybir.AluOpType
AX = mybir.AxisListType

D = 1024
NT = 32
COLS = (512, 128, 128, 128, 128, 128, 128, 24, 512, 256, 256, 512, 512)
OFF = np.concatenate([[0], np.cumsum(COLS)]).astype(int)
D_IN = int(OFF[-1])
(O_Q, O_KC, O_VC, O_KS, O_VS, O_KW, O_VW, O_BR, O_GA, O_RQ, O_RK, O_RV, O_GR) = [int(v) for v in OFF[:-1]]
NEGB = -30000.0
GC = [float((1.0 - 2.0 ** (-5.0 - h)) ** 128) for h in range(4)]
EPS = 1e-6
SAME_ENGINE_SYNC = True
DO_SAMPLE = True


class TW:
    def __init__(self, t, name):
        self.t = t
        self.name = name
        self.w = None
        self.r = []

    def __getitem__(self, k):
        return self.t[k]


class TWView:
    def __init__(self, base, ap):
        self.base = base
        self.ap = ap
        self.name = base.name

    def __getitem__(self, k):
        return self.ap[k]

    @property
    def w(self):
        return self.base.w

    @w.setter
    def w(self, v):
        self.base.w = v

    @property
    def r(self):
        return self.base.r

    @r.setter
    def r(self, v):
        self.base.r = v


class Prog:
    def __init__(self, nc, es):
        self.nc = nc
        self.es = es
        self.scope = None
        self.tiles = {}
        self.ops = []
        self.eng = {"pe": nc.tensor, "act": nc.scalar, "dve": nc.vector, "pool": nc.gpsimd, "sp": nc.sync}
        self.cnt = {}
        self.sems = {}
        self.known = {e: {} for e in self.eng}
        self.emitted = 0
        self.all_waits = []

    def sem(self, name):
        return self.es.enter_context(self.nc.semaphore(name))

    def scope_push(self):
        self.scope = ExitStack()

    def scope_pop(self):
        self.scope.close()
        self.scope = None

    def sb(self, name, shape, dt):
        if name in self.tiles:
            return self.tiles[name]
        st = self.scope if self.scope is not None else self.es
        t = TW(st.enter_context(self.nc.sbuf_tensor(name, list(shape), dt)), name)
        if self.scope is None:
            self.tiles[name] = t
        return t

    def ps(self, name, shape, dt):
        return TW(self.es.enter_context(self.nc.psum_tensor(name, list(shape), dt)), name)

    def op(self, eng, fn, reads=(), writes=(), dma=None):
        idx = len(self.ops)
        deps = set()
        for t in reads:
            if t.w is not None:
                deps.add(t.w)
        for t in writes:
            if t.w is not None:
                deps.add(t.w)
            for e in t.r:
                deps.add(e)
        deps.discard(idx)
        self.ops.append(dict(eng=eng, fn=fn, deps=sorted(deps), dma=dma, sig=False))
        for t in reads:
            t.r.append(idx)
        for t in writes:
            t.w = idx
            t.r = []
        return idx

    def emit(self, final=True):
        ops = self.ops
        s0 = self.emitted
        seg = range(s0, len(ops))
        cnt, sems = self.cnt, self.sems
        for i in seg:
            o = ops[i]
            for d in o["deps"]:
                if d < s0:
                    continue
                p = ops[d]
                if p["dma"] is not None or p["eng"] != o["eng"] or (SAME_ENGINE_SYNC and o["eng"] != "pe"):
                    p["sig"] = True
        last = {}
        for i in seg:
            if ops[i]["dma"] is None:
                last[ops[i]["eng"]] = i
        for i in last.values():
            ops[i]["sig"] = True
        barrier = dict(cnt) if s0 > 0 else {}
        for i in seg:
            o = ops[i]
            key = ("dma", o["dma"]) if o["dma"] is not None else ("eng", o["eng"])
            if o["dma"] is not None:
                o["sig"] = True
            if o["sig"]:
                if key not in sems:
                    sems[key] = self.sem("s_%s_%s" % key)
                    cnt[key] = 0
                cnt[key] += 16 if o["dma"] is not None else 1
                o["ev"] = (key, cnt[key])
        did_barrier = set()
        for i in seg:
            o = ops[i]
            e = self.eng[o["eng"]]
            kn = self.known[o["eng"]]
            need = {}
            if o["eng"] not in did_barrier:
                did_barrier.add(o["eng"])
                for key, val in barrier.items():
                    need[key] = val
            for d in o["deps"]:
                if d < s0:
                    continue
                p = ops[d]
                if not p["sig"]:
                    continue
                if p["dma"] is None and p["eng"] == o["eng"] and (not SAME_ENGINE_SYNC or o["eng"] == "pe"):
                    continue
                pc = p["dma"] is not None and p["dma"].startswith("const")
                if pc and o["dma"] == p["dma"]:
                    continue
                key, val = p["ev"]
                if pc:
                    val = cnt[key]
                need[key] = max(need.get(key, 0), val)
            o["waits"] = []
            for key, val in need.items():
                if kn.get(key, 0) >= val:
                    continue
                e.wait_ge(sems[key], val)
                o["waits"].append((key, val))
                kn[key] = val
            ins = o["fn"]()
            if o["sig"]:
                key, val = o["ev"]
                ins.then_inc(sems[key], 16 if o["dma"] is not None else 1)
        self.emitted = len(ops)
        self.stats = dict(n_ops=len(ops), cnt={str(k): v for k, v in cnt.items()})
        if final:
            for key, s in sems.items():
                if key[0] == "dma":
                    self.nc.sync.wait_ge(s, cnt[key])


def build_program(sample=True):
    nc = bass.Bass("TRN2", target_bir_lowering=False)
    es = ExitStack()
    P = Prog(nc, es)

    def din(name, shape, dt=F32):
        return nc.dram_tensor(name, list(shape), dt, kind="ExternalInput").ap()

    def dout(name, shape, dt=F32):
        return nc.dram_tensor(name, list(shape), dt, kind="ExternalOutput").ap()

    xloc = din("xloc", [NT * 128, D])
    cvec = din("cvec", [5, D])
    w_ada = din("w_ada", [D, 3 * D])
    b_ada = din("b_ada", [1, 3 * D])
    w_in = din("w_in", [D, D_IN])
    w_out = din("w_out", [D, D])
    g_norm = din("g_norm", [1, D])
    g_q = din("g_q", [1, 64]); g_kc = din("g_kc", [1, 64]); g_ks = din("g_ks", [1, 64]); g_kw = din("g_kw", [1, 64])
    g_ret = din("g_ret", [1, 128])
    pe_ck = din("pe_ck", [32, 64]); pe_cv = din("pe_cv", [32, 64])
    w_ck1 = din("w_ck1", [32, 64, 64]); w_cv1 = din("w_cv1", [32, 64, 64])
    w_ck2 = din("w_ck2", [64, 64]); w_cv2 = din("w_cv2", [64, 64])
    c_ident = din("c_ident", [128, 128])
    c_expand = din("c_expand", [64, NT * 128])
    c_mask3 = din("c_mask3", [128, 3, 128])
    c_maskc = din("c_maskc", [128, 16, 2, 128])
    c_adds = din("c_adds", [128, 16, 64])
    c_rope = din("c_rope", [128, NT, 4, 32])
    c_wimp = din("c_wimp", [128, 2, 64])
    c_dmask = din("c_dmask", [128, 4, 128])
    c_rscal = din("c_rscal", [128, NT, 8])
    y_out = dout("y_out", [16 * 128, D])
    pcmp_out = dout("pcmp_out", [16 * 128, 256])
    pslc_out = dout("pslc_out", [16 * 128, 256])
    pwin_out = dout("pwin_out", [2 * 128, 256])
    pret_out = dout("pret_out", [64, 4, 128])
    DBGO = os.environ.get('KS_DBGOUT', '0') == '1'
    dbg_out = dout("dbg_out", [16 * 128, D]) if DBGO else None

    xs_in = din("xs_in", [4, D])
    ptc = din("ptc", [4, 128], I32)
    NPHYS = int(os.environ.get('KS_NPHYS', '5120'))
    cache_cmp = din("cache_cmp", [NPHYS * 128, 256])
    cache_slc = din("cache_slc", [NPHYS * 128, 256])
    dbg2_out = dout("dbg2_out", [128, 64]) if DBGO else None
    dbg3_out = dout("dbg3_out", [128, 2048]) if DBGO else None
    swin_in = din("swin_in", [4, 512, 256])
    sret_in = din("sret_in", [4, 4, 64, 128])
    c_rope_s = din("c_rope_s", [4, 4, 32])
    c_wfull = din("c_wfull", [128, 8, 256])
    c_top = din("c_top", [8, 2, 256])
    c_pcol = din("c_pcol", [128, 6])
    c_oh4 = din("c_oh4", [4, 4, 128])
    c_ohq = din("c_ohq", [64, 4, 4])
    ys_out = dout("ys_out", [4, D])
    scmp_out = dout("scmp_out", [4, 256])
    sslc_out = dout("sslc_out", [4, 256])
    swin_out = dout("swin_out", [4, 512, 256])
    sret_out = dout("sret_out", [4, 4, 64, 128])
    scr1 = nc.dram_tensor("scr1", [128, 1], F32, kind="Internal").ap()
    scr2 = nc.dram_tensor("scr2", [128, 1], F32, kind="Internal").ap()
    sb, ps = P.sb, P.ps
    V, A, T, G, S = "dve", "act", "pe", "pool", "sp"

    def dma(eng, out_t, out_ap, in_t, in_ap, grp, **kw):
        reads = [in_t] if in_t is not None else []
        writes = [out_t] if out_t is not None else []
        e = P.eng[eng]
        if not grp.startswith("const"):
            grp = (out_t or in_t).name
        return P.op(eng, lambda: e.dma_start(out=out_ap, in_=in_ap, **kw), reads, writes, dma=grp)

    def vop(name, reads, writes, eng=V, **kw):
        e = P.eng[eng]
        f = getattr(e, name)
        return P.op(eng, lambda: f(**kw), reads, writes)

    def act(out_t, out_ap, in_t, in_ap, func, extra_reads=(), extra_writes=(), **kw):
        return P.op(A, lambda: nc.scalar.activation(out=out_ap, in_=in_ap, func=func, **kw), [in_t] + list(extra_reads), [out_t] + list(extra_writes))

    def acopy(out_t, out_ap, in_t, in_ap):
        return P.op(A, lambda: nc.scalar.copy(out=out_ap, in_=in_ap), [in_t], [out_t])

    def mm(out_t, out_ap, l_t, l_ap, r_t, r_ap, start, stop, sgc=False):
        return P.op(T, lambda: nc.tensor.matmul(out_ap, lhsT=l_ap, rhs=r_ap, start=start, stop=stop, skip_group_check=sgc), [l_t, r_t], [out_t])

    def tr(out_t, out_ap, in_t, in_ap, id_t, id_ap):
        return P.op(T, lambda: nc.tensor.transpose(out=out_ap, in_=in_ap, identity=id_ap), [in_t, id_t], [out_t])

    sb("w_in_bf", [128, 8, D_IN], BF16)
    sb("w_out_bf", [128, 8, D], BF16)
    sb("ident", [128, 128], BF16)
    ident_f = sb("ident_f", [128, 128], F32)
    ones_b = sb("ones_b", [128, 128], BF16)
    for i in range(2):
        sb("w1blk%d" % i, [128, 32, 128], BF16)
        sb("w2blk%d" % i, [128, 128], BF16)
        sb("cbias%d" % i, [128, 1], F32)
    sb("modA", [128, 8, 5], F32)
    sb("modS", [128, 8, 5], F32)
    sb("gvec", [128, 64 * 4 + 128], F32)
    sb("modT", [128, 24, 5], F32)
    P.scope_push()
    w_in_bf = sb("w_in_bf", [128, 8, D_IN], BF16)
    w_out_bf = sb("w_out_bf", [128, 8, D], BF16)
    wst = [sb("wst%d" % i, [128, 8, 128], F32) for i in range(2)]
    ident = sb("ident", [128, 128], BF16)
    mask3 = sb("mask3", [128, 3, 128], BF16)
    maskc = sb("maskc", [128, 16, 2, 128], BF16)
    adds = sb("adds", [128, 16, 64], F32)
    wimp = sb("wimp", [128, 2, 64], BF16)
    dmask = sb("dmask", [128, 4, 128], F32)
    rscal = sb("rscal", [128, NT, 8], F32)
    ksT = [sb("ksT%d" % g, [128, NT * 128], BF16) for g in range(2)]
    vsa = [sb("vsa%d" % g, [128, NT, 65], BF16) for g in range(2)]
    kwT = [sb("kwT%d" % g, [64, 6, 128], BF16) for g in range(2)]
    vwa = [sb("vwa%d" % g, [128, 6, 65], BF16) for g in range(2)]
    cT = [sb("cT%d" % i, [128, 144], BF16) for i in range(2)]
    hTall = [sb("hTall%d" % i, [128, 256], BF16) for i in range(2)]
    kcT = [sb("kcT%d" % g, [64, 256], BF16) for g in range(2)]
    vca = [sb("vca%d" % g, [128, 2, 129], BF16) for g in range(2)]
    w1blk = [sb("w1blk%d" % i, [128, 32, 128], BF16) for i in range(2)]
    w2blk = [sb("w2blk%d" % i, [128, 128], BF16) for i in range(2)]
    cbias = [sb("cbias%d" % i, [128, 1], F32) for i in range(2)]
    S_f = sb("S_f", [64, 4, 128], F32)
    S_b = sb("S_b", [64, 4, 128], BF16)
    modA = sb("modA", [128, 8, 5], F32)
    modS = sb("modS", [128, 8, 5], F32)
    gate_bc = sb("gate_bc", [128, D], F32)
    gvec = sb("gvec", [128, 64 * 4 + 128], F32)
    gn_col = sb("gn_col", [128, 8], F32)

    ptr = ps("ptr", [128, 1024], BF16)
    pin = [ps("pin%d" % i, [128, 512], F32) for i in range(2)]
    pst = [ps("pst%d" % i, [128, 512], F32) for i in range(2)]
    pacc = [ps("pacc%d" % i, [128, 512], F32) for i in range(2)]
    paccw = ps("paccw", [128, 512], F32)

    CG = "const"
    for g in range(2):
        vop("memset", [], [vsa[g]], eng=G, ap=vsa[g][:, :, 64:65], constant=1.0)
        vop("memset", [], [vwa[g]], eng=G, ap=vwa[g][:, :, 64:65], constant=1.0)
        vop("memset", [], [vca[g]], eng=G, ap=vca[g][:], constant=0.0)
    for i in range(2):
        vop("memset", [], [hTall[i]], eng=G, ap=hTall[i][:], constant=0.0)
        vop("memset", [], [cT[i]], eng=G, ap=cT[i][:], constant=0.0)
        vop("memset", [], [w1blk[i]], eng=G, ap=w1blk[i][:], constant=0.0)
        vop("memset", [], [w2blk[i]], eng=G, ap=w2blk[i][:], constant=0.0)
    vop("memset", [], [ones_b], eng=G, ap=ones_b[:], constant=1.0)
    vop("memset", [], [S_f], eng=G, ap=S_f[:], constant=0.0)
    vop("memset", [], [S_b], eng=G, ap=S_b[:], constant=0.0)
    dma(G, ident, ident[:], None, c_ident, CG)
    dma(S, ident_f, ident_f[:], None, c_ident, CG)
    dma(G, mask3, mask3[:], None, c_mask3, CG)
    dma(S, adds, adds[:], None, c_adds, CG)
    dma(S, dmask, dmask[:], None, c_dmask, CG)
    dma(S, rscal, rscal[:], None, c_rscal, CG)
    dma(G, maskc, maskc[:], None, c_maskc, CG)
    dma(G, wimp, wimp[:], None, c_wimp, CG)
    for g in range(2):
        dma(G, ksT[g], ksT[g][64:128, :], None, c_expand, CG)
    dma(S, gvec, gvec[:, 0:64], None, g_q.to_broadcast([128, 64]), CG)
    dma(S, gvec, gvec[:, 64:128], None, g_kc.to_broadcast([128, 64]), CG)
    dma(S, gvec, gvec[:, 128:192], None, g_ks.to_broadcast([128, 64]), CG)
    dma(S, gvec, gvec[:, 192:256], None, g_kw.to_broadcast([128, 64]), CG)
    dma(S, gvec, gvec[:, 256:384], None, g_ret.to_broadcast([128, 128]), CG)
    dma(S, gn_col, gn_col[:], None, g_norm.rearrange("o (k p) -> p (o k)", p=128), CG, allow_slow_non_contiguous=True)
    peTb = [sb("peTb%d" % i, [128, 32], BF16) for i in range(2)]
    for i, (w1, w2, pe) in enumerate(((w_ck1, w_ck2, pe_ck), (w_cv1, w_cv2, pe_cv))):
        src = w1.rearrange("l d f -> d l f")
        dma(G, w1blk[i], w1blk[i][0:64, :, 0:64], None, src, CG)
        dma(G, w1blk[i], w1blk[i][64:128, :, 64:128], None, src, CG)
        dma(G, w2blk[i], w2blk[i][0:64, 0:64], None, w2, CG)
        dma(G, w2blk[i], w2blk[i][64:128, 64:128], None, w2, CG)
        srcp = pe.rearrange("l d -> d l")
        dma(G, peTb[i], peTb[i][0:64, :], None, srcp, CG, allow_slow_non_contiguous=True)
        dma(G, peTb[i], peTb[i][64:128, :], None, srcp, CG, allow_slow_non_contiguous=True)
    cT_f = sb("cT_f", [128, 8, 5], F32)
    cT_b = sb("cT_b", [128, 8, 5], BF16)
    xs = sb("xs", [128, D], BF16)
    hT = sb("hT", [128, 8, 128], BF16)
    crep = hT
    for r in range(5):
        dma(S, cT_f, cT_f[:, :, r], None, cvec[r:r + 1, :].rearrange("o (k p) -> p (o k)", p=128), CG, allow_slow_non_contiguous=True)
    bad = sb("bad", [128, 24], F32)
    dma(S, bad, bad[:], None, b_ada.rearrange("o (j p) -> p (o j)", p=128), CG, allow_slow_non_contiguous=True)
    yst = [sb("yst0", [128, D], F32)]
    badg = yst[0]
    dma(S, badg, badg[:], None, b_ada[:, 2 * D:3 * D].to_broadcast([128, D]), CG)
    STAGE = int(os.environ.get('KS_STAGE', '9'))
    RUN_NT = int(os.environ.get('KS_NT', str(NT)))
    if STAGE < 2:
        P.emit(); es.close(); return nc
    for g in range(2):
        vop("memset", [], [vca[g]], eng=G, ap=vca[g][:, :, 64:65], constant=1.0)
        vop("tensor_copy", [wimp], [vca[g]], eng=G, out=vca[g][:, :, 65:129], in_=wimp[:])
    vop("tensor_scalar", [gvec], [gvec], out=gvec[:, 64:256], in0=gvec[:, 64:256], scalar1=8.0, scalar2=None, op0=ALU.mult)
    vop("tensor_scalar", [gvec], [gvec], out=gvec[:, 256:384], in0=gvec[:, 256:384], scalar1=float(np.sqrt(128.0)), scalar2=None, op0=ALU.mult)
    for i in range(2):
        for l in range(32):
            mm(pin[0], pin[0][:, 0:1], w1blk[i], w1blk[i][:, l, :], peTb[i], peTb[i][:, l:l + 1], l == 0, l == 31)
        vop("tensor_copy", [pin[0]], [cbias[i]], out=cbias[i][:], in_=pin[0][:, 0:1])

    def load_cast(src, ncols, dst):
        k = 0
        for c0 in range(0, ncols, 128):
            cw = min(128, ncols - c0)
            st = wst[k % 2]
            k += 1
            dma(S if k % 2 else A, st, st[:, :, 0:cw], None, src[:, c0:c0 + cw].rearrange("(k p) c -> p k c", p=128), "x")
            vop("tensor_copy", [st], [dst], eng=(V if k % 2 else G), out=dst[:, :, c0:c0 + cw], in_=st[:, :, 0:cw])

    act(cT_b, cT_b[:], cT_f, cT_f[:], AF.Silu)
    for kt in range(8):
        vop("tensor_copy", [cT_b], [crep], out=crep[:, kt, :], in_=cT_b[:, kt, 0:1].to_broadcast([128, 128]))
    modT = sb("modT", [128, 24, 5], F32)
    wab = [sb("wab0", [128, 8, 128], BF16), TWView(xs, xs[:].rearrange("p (k c) -> p k c", k=8))]
    k = 0
    for c0 in range(0, 3 * D, 128):
        st = wst[k % 2]
        wb = wab[k % 2]
        k += 1
        dma(S if k % 2 else A, st, st[:], None, w_ada[:, c0:c0 + 128].rearrange("(k p) c -> p k c", p=128), "x")
        vop("tensor_copy", [st], [wb], out=wb[:], in_=st[:])
        j = c0 // 128
        po = pin[j % 2]
        for kt in range(8):
            mm(po, po[:, 0:5], wb, wb[:, kt, :], cT_b, cT_b[:, kt, :], kt == 0, kt == 7)
        vop("tensor_scalar", [po, bad], [modT], out=modT[:, j, :], in0=po[:, 0:5], scalar1=bad[:, j:j + 1], scalar2=None, op0=ALU.add)
        if c0 >= 2 * D:
            po = pst[j % 2]
            for kt in range(8):
                mm(po, po[:, 0:128], crep, crep[:, kt, :], wb, wb[:, kt, :], kt == 0, kt == 7)
            vop("tensor_tensor", [po, badg], [gate_bc], out=gate_bc[:, c0 - 2 * D:c0 - 2 * D + 128], in0=po[:, 0:128], in1=badg[:, c0 - 2 * D:c0 - 2 * D + 128], op=ALU.add)
    vop("tensor_scalar", [modT], [modA], out=modA[:], in0=modT[:, 8:16, :], scalar1=1.0, scalar2=None, op0=ALU.add)
    vop("tensor_tensor", [modA, gn_col], [modA], out=modA[:], in0=modA[:], in1=gn_col[:].unsqueeze(2).to_broadcast([128, 8, 5]), op=ALU.mult)
    vop("tensor_copy", [modT], [modS], out=modS[:], in_=modT[:, 0:8, :])

    load_cast(w_in, D_IN, w_in_bf)
    load_cast(w_out, D, w_out_bf)

    xt = [sb("xt0", [128, D], F32)] * 2
    rope = [sb("rope%d" % i, [128, 4, 32], F32) for i in range(2)]
    ss = sb("ss", [128, 1], F32)
    rstd = sb("rstd", [128, 1], F32)
    sq = sb("sq", [128, 512], F32)
    ssq = sb("ssq", [128, 16], F32)
    rq8 = sb("rq8", [128, 16], F32)
    qn = sb("qn", [128, 512], BF16)
    qaug = [sb("qaug%d" % g, [128, 4, 128], BF16) for g in range(2)]
    kvf = sb("kvf", [128, 512], F32)
    kvb = sb("kvb", [128, 512], BF16)
    kwf = sb("kwf", [128, 280], F32)
    kwb = sb("kwb", [128, 128], BF16)
    gates = sb("gates", [128, 24], F32)
    ga_s = sb("ga_s", [128, 512], BF16)
    gr_s = sb("gr_s", [128, 512], BF16)
    rot = sb("rot", [128, 2, 4, 64], F32)
    tmp1 = sb("tmp1", [128, 2, 4, 32], F32)
    tmp2 = sb("tmp2", [128, 2, 4, 32], F32)
    rqb = sb("rqb", [128, 3, 4, 64], BF16)
    ktl = sb("ktl", [128, 4, 64], BF16)
    rT = sb("rT", [64, 3, 4, 128], BF16)
    vb = sb("vb", [128, 4, 128], BF16)
    atm = sb("atm", [128, 4, 128], BF16)
    oret = sb("oret", [128, 4, 128], F32)
    et = [sb("et%d" % i, [128, 4, 128], BF16) for i in range(3)]
    obr = [sb("obr%d" % i, [128, 4, 65], F32) for i in range(3)]
    aimp = sb("aimp", [128, 4, 64], F32)
    rden = sb("rden", [128, 3, 4], F32)
    coef = sb("coef", [128, 3, 4], F32)
    score = sb("score", [128, 64], F32)
    swork = sb("swork", [128, 64], F32)
    mx8 = sb("mx8", [128, 16], F32)
    thr = sb("thr", [128, 1], F32)
    selb = sb("selb", [128, 128], BF16)
    onsa = sb("onsa", [128, 8, 64], F32)
    otmp = sb("otmp", [128, 4, 64], F32)
    ymix = xs
    yT = hT
    kcn = sb("kcn", [128, 128], F32)
    kcnb = sb("kcnb", [128, 128], BF16)
    vop("memset", [], [selb], eng=G, ap=selb[:], constant=0.0)

    def rms_heads(src_t, src_ap, nh, hd, gain_ap, out_t, out_ap, eps_sum, np_=128, scr=None):
        sq_, ssq_, rq_ = scr if scr is not None else (sq, ssq, rq8)
        act(sq_, sq_[0:np_, 0:nh * hd], src_t, src_ap, AF.Square)
        vop("tensor_reduce", [sq_], [ssq_], out=ssq_[0:np_, 0:nh], in_=sq_[0:np_, 0:nh * hd].rearrange("p (h d) -> p h d", h=nh), axis=AX.X, op=ALU.add)
        vop("tensor_scalar", [ssq_], [rq_], out=rq_[0:np_, 0:nh], in0=ssq_[0:np_, 0:nh], scalar1=eps_sum, scalar2=None, op0=ALU.add)
        act(rq_, rq_[0:np_, 0:nh], rq_, rq_[0:np_, 0:nh], AF.Sqrt)
        vop("reciprocal", [rq_], [rq_], out=rq_[0:np_, 0:nh], in_=rq_[0:np_, 0:nh])
        vop("tensor_tensor", [src_t, rq_], [sq_], out=sq_[0:np_, 0:nh * hd].rearrange("p (h d) -> p h d", h=nh),
            in0=src_ap.rearrange("p (h d) -> p h d", h=nh), in1=rq_[0:np_, 0:nh].unsqueeze(2).to_broadcast([np_, nh, hd]), op=ALU.mult)
        vop("tensor_tensor", [sq_, gvec], [out_t], out=out_ap.rearrange("p (h d) -> p h d", h=nh),
            in0=sq_[0:np_, 0:nh * hd].rearrange("p (h d) -> p h d", h=nh), in1=gain_ap.unsqueeze(1).to_broadcast([np_, nh, hd]), op=ALU.mult)

    def inproj(c0, cw, po):
        for kt in range(8):
            mm(po, po[:, 0:cw], hT, hT[:, kt, :], w_in_bf, w_in_bf[:, kt, c0:c0 + cw], kt == 0, kt == 7)

    n_et = [0]

    def attn_tile(g, po, l_t, l_ap, r_ap, masks, v_t, v_ap, nv, acc, first, last, accw=None, vw_ap=None):
        nmm = 1 + len(masks)
        mm(po, po[:], l_t, l_ap, qaug[g], r_ap, True, nmm == 1)
        for mi, (m_t, m_ap) in enumerate(masks):
            for h in range(4):
                mm(po, po[:, h * 128:(h + 1) * 128], ident, ident[:], m_t, m_ap, False, (mi == len(masks) - 1) and h == 3, sgc=True)
        e = et[n_et[0] % 3]
        n_et[0] += 1
        act(e, e[:].rearrange("p h q -> p (h q)"), po, po[:], AF.Exp)
        for h in range(4):
            mm(acc, acc[:, h * 128:h * 128 + nv], e, e[:, h, :], v_t, v_ap, first and h == 0, last and h == 3, sgc=True)
            if accw is not None:
                mm(accw, accw[:, h * 64:(h + 1) * 64], e, e[:, h, :], v_t, vw_ap, first and h == 0, last and h == 3, sgc=True)

    n_st = [0]

    if STAGE < 3:
        P.emit(); es.close(); return nc
    SUB = int(os.environ.get('KS_SUB', '99'))

    class _Stop(Exception):
        pass

    def ck(k):
        if SUB < k:
            raise _Stop()

    def _tile(L):
            own = (L % 2 == 1)
            j = L // 2
            x_t = xt[L % 2]
            rp = rope[L % 2]
            dma(S, x_t, x_t[:], None, xloc[L * 128:(L + 1) * 128, :], "x%d" % (L % 2))
            dma(A, rp, rp[:], None, c_rope[:, L, :, :], "x%d" % (L % 2))
            vop("memset", [], [ss], eng=G, ap=ss[:], constant=0.0)
            act(xs, xs[:], x_t, x_t[:], AF.Square, accum_out=ss[:], extra_writes=[ss])
            vop("tensor_scalar", [ss], [rstd], out=rstd[:], in0=ss[:], scalar1=1.0 / D, scalar2=EPS, op0=ALU.mult, op1=ALU.add)
            act(rstd, rstd[:], rstd, rstd[:], AF.Sqrt)
            vop("reciprocal", [rstd], [rstd], out=rstd[:], in_=rstd[:])
            P.op(A, lambda x_t=x_t: nc.scalar.mul(out=xs[:], in_=x_t[:], mul=rstd[:, 0:1]), [x_t, rstd], [xs])
            for kt in range(8):
                tr(ptr, ptr[:, kt * 128:(kt + 1) * 128], xs, xs[:, kt * 128:(kt + 1) * 128], ident, ident[:])
            for kt in range(8):
                vop("tensor_scalar", [ptr, modA, modS], [hT], out=hT[:, kt, :], in0=ptr[:, kt * 128:(kt + 1) * 128],
                    scalar1=modA[:, kt, 0:1], scalar2=modS[:, kt, 0:1], op0=ALU.mult, op1=ALU.add)

            ck(1)
            po = pin[0]
            inproj(O_KC, 512, po)
            DBG = int(os.environ.get('KS_DBG', '3'))
            if DBG & 1:
                vop("tensor_copy", [po], [kvf], out=kvf[:], in_=po[:])
            if DBG & 2:
                if os.environ.get('KS_ACTV', '0') == '1':
                    act(kvb, kvb[:], po, po[:], AF.Identity)
                elif os.environ.get('KS_ACTV', '0') == '2':
                    act(kvb, kvb[:], po, po[:], AF.Silu)
                else:
                    vop("tensor_copy", [kvf], [kvb], eng=G, out=kvb[:], in_=kvf[:])
            ck(2)
            for i in range(2):
                tr(ptr, ptr[:, i * 128:(i + 1) * 128], kvb, kvb[:, i * 128:(i + 1) * 128], ident, ident[:])
            for i in range(2):
                vop("tensor_copy", [ptr], [cT[i]], out=cT[i][:, 16:144], in_=ptr[:, i * 128:(i + 1) * 128])
            ck(3)
            rms_heads(kvf, kvf[:, 256:384], 2, 64, gvec[:, 128:192], kvf, kvf[:, 256:384], 64 * EPS)
            vop("tensor_copy", [kvf], [kvb], out=kvb[:, 256:384], in_=kvf[:, 256:384])
            for g in range(2):
                tr(ptr, ptr[0:64, (2 + g) * 128:(3 + g) * 128], kvb, kvb[:, 256 + g * 64:256 + (g + 1) * 64], ident, ident[:])
            for g in range(2):
                vop("tensor_copy", [ptr], [ksT[g]], out=ksT[g][0:64, L * 128:(L + 1) * 128], in_=ptr[0:64, (2 + g) * 128:(3 + g) * 128])
                vop("tensor_copy", [kvb], [vsa[g]], eng=G, out=vsa[g][:, L, 0:64], in_=kvb[:, 384 + g * 64:384 + (g + 1) * 64])
            if own:
                dma(S, None, pcmp_out[j * 128:(j + 1) * 128, :], kvf, kvf[:, 0:256], "x")
                dma(S, None, pslc_out[j * 128:(j + 1) * 128, :], kvf, kvf[:, 256:512], "x")
            ck(4)
            po = pin[1]
            cw = 280 if own else 256
            inproj(O_KW, cw, po)
            vop("tensor_copy", [po], [kwf], out=kwf[:, 0:cw], in_=po[:, 0:cw])
            rms_heads(kwf, kwf[:, 0:128], 2, 64, gvec[:, 192:256], kwf, kwf[:, 0:128], 64 * EPS)
            vop("tensor_copy", [kwf], [kwb], out=kwb[:], in_=kwf[:, 0:128])
            for g in range(2):
                tr(ptr, ptr[0:64, (4 + g) * 128:(5 + g) * 128], kwb, kwb[:, g * 64:(g + 1) * 64], ident, ident[:])
            for g in range(2):
                vop("tensor_copy", [ptr], [kwT[g]], out=kwT[g][:, L % 6, :], in_=ptr[0:64, (4 + g) * 128:(5 + g) * 128])
                vop("tensor_copy", [kwf], [vwa[g]], eng=G, out=vwa[g][:, L % 6, 0:64], in_=kwf[:, 128 + g * 64:128 + (g + 1) * 64])
            if L in (29, 31):
                dma(S, None, pwin_out[((L - 29) // 2) * 128:((L - 29) // 2 + 1) * 128, :], kwf, kwf[:, 0:256], "x")
            if own:
                act(gates, gates[:], kwf, kwf[:, 256:280], AF.Sigmoid)

            ck(5)
            nb0 = 8 * L - 1
            c_lo = 1 if L == 0 else 0
            for i in range(2):
                po = pin[i]
                for l in range(32):
                    mm(po, po[:, 0:8 - c_lo], w1blk[i], w1blk[i][:, l, :], cT[i], cT[i][:, 16 * c_lo + l:16 * c_lo + l + 16 * (7 - c_lo) + 1:16], l == 0, l == 31)
                act(hTall[i], hTall[i][:, nb0 + c_lo:nb0 + 8], po, po[:, 0:8 - c_lo], AF.Silu, extra_reads=[cbias[i]], bias=cbias[i][:, 0:1])
                vop("tensor_copy", [cT[i]], [cT[i]], out=cT[i][:, 0:16], in_=cT[i][:, 128:144])

            ck(6)
            po = pin[0]
            rqk = po
            if own:
                inproj(O_RQ, 512, po)
                qk0 = 0
                rqo = 0
            else:
                inproj(O_RK, 256, po)
                qk0 = 1
                rqo = -256
            po = pin[1]
            inproj(O_RV, 512, po)
            vop("tensor_copy", [po], [vb], out=vb[:].rearrange("p h d -> p (h d)"), in_=po[:])
            ck(7)
            for s in range(qk0, 2):
                xv = rqk[:, s * 256 + rqo:(s + 1) * 256 + rqo].rearrange("p (h d) -> p h d", h=4)
                cs = rp[:, 2 * s, :].unsqueeze(1).to_broadcast([128, 4, 32])
                sn = rp[:, 2 * s + 1, :].unsqueeze(1).to_broadcast([128, 4, 32])
                vop("tensor_tensor", [rqk, rp], [tmp1], out=tmp1[:, s], in0=xv[:, :, 0:32], in1=cs, op=ALU.mult)
                vop("tensor_tensor", [rqk, rp], [tmp2], out=tmp2[:, s], in0=xv[:, :, 32:64], in1=sn, op=ALU.mult)
                vop("tensor_tensor", [tmp1, tmp2], [rot], out=rot[:, s, :, 0:32], in0=tmp1[:, s], in1=tmp2[:, s], op=ALU.subtract)
                vop("tensor_tensor", [rqk, rp], [tmp1], out=tmp1[:, s], in0=xv[:, :, 0:32], in1=sn, op=ALU.mult)
                vop("tensor_tensor", [rqk, rp], [tmp2], out=tmp2[:, s], in0=xv[:, :, 32:64], in1=cs, op=ALU.mult)
                vop("tensor_tensor", [tmp1, tmp2], [rot], out=rot[:, s, :, 32:64], in0=tmp1[:, s], in1=tmp2[:, s], op=ALU.add)
            vop("tensor_tensor", [rot, rscal], [ktl], out=ktl[:], in0=rot[:, 1], in1=rscal[:, L, 0:4].unsqueeze(2).to_broadcast([128, 4, 64]), op=ALU.mult)
            if own:
                vop("tensor_copy", [rot], [rqb], out=rqb[:, 0:2], in_=rot[:])
                vop("tensor_tensor", [rot, rscal], [rqb], out=rqb[:, 2], in0=rot[:, 0], in1=rscal[:, L, 4:8].unsqueeze(2).to_broadcast([128, 4, 64]), op=ALU.mult)
                for s in range(2):
                    for h in range(4):
                        c = s * 4 + h
                        tr(ptr, ptr[0:64, c * 128:(c + 1) * 128], rqb, rqb[:, s, h, :], ident, ident[:])
                vop("tensor_copy", [ptr], [rT], out=rT[:, 0:2].rearrange("p s h q -> p (s h q)"), in_=ptr[0:64, :])
                for h in range(4):
                    tr(ptr, ptr[0:64, h * 128:(h + 1) * 128], rqb, rqb[:, 2, h, :], ident, ident[:])
                vop("tensor_copy", [ptr], [rT], out=rT[:, 2].rearrange("p h q -> p (h q)"), in_=ptr[0:64, 0:512])
                po = pst[0]
                for h in range(4):
                    mm(po, po[:, h * 128:(h + 1) * 128], rT, rT[:, 1, h, :], rT, rT[:, 0, h, :], True, True)
                vop("tensor_tensor", [po, dmask], [atm], out=atm[:].rearrange("p h q -> p (h q)"), in0=po[:], in1=dmask[:].rearrange("p h q -> p (h q)"), op=ALU.mult)
                po = pst[1]
                for h in range(4):
                    mm(po, po[:, h * 128:(h + 1) * 128], atm, atm[:, h, :], vb, vb[:, h, :], True, False)
                    mm(po, po[:, h * 128:(h + 1) * 128], rT, rT[:, 2, h, :], S_b, S_b[:, h, :], False, True)
                vop("tensor_copy", [po], [oret], out=oret[:].rearrange("p h d -> p (h d)"), in_=po[:])
            ck(8)
            po = pacc[0]
            for h in range(4):
                mm(po, po[0:64, h * 128:(h + 1) * 128], ktl, ktl[:, h, :], vb, vb[:, h, :], True, True)
            for h in range(4):
                vop("tensor_scalar", [S_f], [S_f], out=S_f[:, h, :], in0=S_f[:, h, :], scalar1=GC[h], scalar2=None, op0=ALU.mult)
            vop("tensor_tensor", [S_f, po], [S_f], out=S_f[:].rearrange("p h d -> p (h d)"), in0=S_f[:].rearrange("p h d -> p (h d)"), in1=po[0:64, :], op=ALU.add)
            vop("tensor_copy", [S_f], [S_b], out=S_b[:], in_=S_f[:])
            if L == NT - 1:
                dma(S, None, pret_out, S_f, S_f[:], "oret")
            if not own:
                return

            po = pin[0]
            inproj(O_Q, 512, po)
            rms_heads(po, po[:], 8, 64, gvec[:, 0:64], qn, qn[:], 64 * EPS)
            for hh in range(8):
                tr(ptr, ptr[0:64, hh * 128:(hh + 1) * 128], qn, qn[:, hh * 64:(hh + 1) * 64], ident, ident[:])
            for g in range(2):
                vop("tensor_copy", [ptr], [qaug[g]], out=qaug[g][0:64, :, :].rearrange("p h q -> p (h q)"), in_=ptr[0:64, g * 512:(g + 1) * 512])
            po = pin[1]
            inproj(O_GA, 512, po)
            act(ga_s, ga_s[:], po, po[:], AF.Silu)
            po = pin[0]
            inproj(O_GR, 512, po)
            act(gr_s, gr_s[:], po, po[:], AF.Silu)

            nmax = 8 * L + 6
            ntiles = nmax // 128 + 1
            for nt in range(ntiles):
                po = pin[nt % 2]
                mm(po, po[:, 0:128], hTall[0], hTall[0][:, nt * 128:(nt + 1) * 128], w2blk[0], w2blk[0][:], True, True)
                mm(po, po[:, 128:256], hTall[1], hTall[1][:, nt * 128:(nt + 1) * 128], w2blk[1], w2blk[1][:], True, True)
                vop("tensor_copy", [po], [kcn], out=kcn[:], in_=po[:, 0:128])
                rms_heads(kcn, kcn[:], 2, 64, gvec[:, 64:128], kcnb, kcnb[:], 64 * EPS)
                for g in range(2):
                    tr(ptr, ptr[0:64, g * 128:(g + 1) * 128], kcnb, kcnb[:, g * 64:(g + 1) * 64], ident, ident[:])
                    vop("tensor_copy", [ptr], [kcT[g]], out=kcT[g][:, nt * 128:(nt + 1) * 128], in_=ptr[0:64, g * 128:(g + 1) * 128])
                    vop("tensor_copy", [po], [vca[g]], out=vca[g][:, nt, 0:64], in_=po[:, 128 + g * 64:128 + (g + 1) * 64])

            for g in range(2):
                acc = pacc[n_st[0] % 2]; n_st[0] += 1
                for nt in range(ntiles):
                    po = pst[nt % 2]
                    attn_tile(g, po, kcT[g], kcT[g][:, nt * 128:(nt + 1) * 128], qaug[g][0:64, :, :].rearrange("p h q -> p (h q)"),
                              [(maskc, maskc[:, j, nt, :])], vca[g], vca[g][:, nt, 0:65], 65, acc, nt == 0, nt == ntiles - 1,
                              accw=paccw, vw_ap=vca[g][:, nt, 65:129])
                ob = obr[0]
                vop("tensor_copy", [acc], [ob], out=ob[:], in_=acc[:].rearrange("p (h d) -> p h d", h=4)[:, :, 0:65])
                vop("tensor_scalar", [ob], [rden], out=rden[:, 0, :], in0=ob[:, :, 64], scalar1=1e-30, scalar2=None, op0=ALU.max)
                vop("reciprocal", [rden], [rden], out=rden[:, 0, :], in_=rden[:, 0, :])
                vop("tensor_tensor", [paccw, rden], [aimp], out=aimp[:], in0=paccw[:, 0:256].rearrange("p (h s) -> p h s", h=4),
                    in1=rden[:, 0, :].unsqueeze(2).to_broadcast([128, 4, 64]), op=ALU.mult)
                vop("tensor_reduce", [aimp], [score], out=score[:], in_=aimp[:].rearrange("p h s -> p s h"), axis=AX.X, op=ALU.add)
                vop("tensor_tensor", [score, adds], [score], out=score[:], in0=score[:], in1=adds[:, j, :], op=ALU.add)
                vop("max", [score], [mx8], out=mx8[:, 0:8], in_=score[:])
                vop("match_replace", [mx8, score], [swork], out=swork[:], in_to_replace=mx8[:, 0:8], in_values=score[:], imm_value=-3e30)
                vop("max", [swork], [mx8], out=mx8[:, 8:16], in_=swork[:])
                vop("tensor_scalar", [mx8], [thr], out=thr[:], in0=mx8[:, 15:16], scalar1=-1e29, scalar2=None, op0=ALU.max)
                vop("tensor_scalar", [score, thr], [swork], out=swork[:], in0=score[:], scalar1=thr[:, 0:1], scalar2=None, op0=ALU.is_lt)
                vop("tensor_scalar", [swork], [selb], out=selb[:, 64:128], in0=swork[:], scalar1=NEGB, scalar2=None, op0=ALU.mult)
                tr(ptr, ptr[:, 0:128], selb, selb[:], ident, ident[:])
                vop("tensor_copy", [ptr], [qaug[g]], out=qaug[g][64:128, :, :], in_=ptr[64:128, 0:128].unsqueeze(1).to_broadcast([64, 4, 128]))
                acc = pacc[n_st[0] % 2]; n_st[0] += 1
                for kt in range(L + 1):
                    po = pst[kt % 2]
                    masks = [(mask3, mask3[:, 0, :])] if kt == L else []
                    attn_tile(g, po, ksT[g], ksT[g][:, kt * 128:(kt + 1) * 128], qaug[g][:].rearrange("p h q -> p (h q)"),
                              masks, vsa[g], vsa[g][:, kt, :], 65, acc, kt == 0, kt == L)
                ob = obr[1]
                vop("tensor_copy", [acc], [ob], out=ob[:], in_=acc[:].rearrange("p (h d) -> p h d", h=4)[:, :, 0:65])
                acc = pacc[n_st[0] % 2]; n_st[0] += 1
                k0 = max(0, L - 4)
                for kt in range(k0, L + 1):
                    po = pst[kt % 2]
                    masks = []
                    if kt == L - 4:
                        masks.append((mask3, mask3[:, 1, :]))
                    if kt == L:
                        masks.append((mask3, mask3[:, 0, :]))
                    if kt == 0:
                        masks.append((mask3, mask3[:, 2, :]))
                    attn_tile(g, po, kwT[g], kwT[g][:, kt % 6, :], qaug[g][0:64, :, :].rearrange("p h q -> p (h q)"),
                              masks, vwa[g], vwa[g][:, kt % 6, :], 65, acc, kt == k0, kt == L)
                ob = obr[2]
                vop("tensor_copy", [acc], [ob], out=ob[:], in_=acc[:].rearrange("p (h d) -> p h d", h=4)[:, :, 0:65])
                for b_ in (1, 2):
                    vop("reciprocal", [obr[b_]], [rden], out=rden[:, b_, :], in_=obr[b_][:, :, 64])
                gv = gates[:, g * 12:(g + 1) * 12].rearrange("p (h t) -> p t h", h=4)
                vop("tensor_tensor", [rden, gates], [coef], out=coef[:], in0=rden[:], in1=gv, op=ALU.mult)
                og = onsa[:, g * 4:(g + 1) * 4, :]
                vop("tensor_tensor", [obr[0], coef], [onsa], out=og, in0=obr[0][:, :, 0:64], in1=coef[:, 0, :].unsqueeze(2).to_broadcast([128, 4, 64]), op=ALU.mult)
                for b_ in (1, 2):
                    vop("tensor_tensor", [obr[b_], coef], [otmp], out=otmp[:], in0=obr[b_][:, :, 0:64], in1=coef[:, b_, :].unsqueeze(2).to_broadcast([128, 4, 64]), op=ALU.mult)
                    vop("tensor_tensor", [onsa, otmp], [onsa], out=og, in0=og, in1=otmp[:], op=ALU.add)
            vop("tensor_tensor", [onsa, ga_s], [ymix], out=ymix[:, 0:512], in0=onsa[:].rearrange("p h d -> p (h d)"), in1=ga_s[:], op=ALU.mult)
            rms_heads(oret, oret[:].rearrange("p h d -> p (h d)"), 4, 128, gvec[:, 256:384], oret, oret[:].rearrange("p h d -> p (h d)"), 128 * EPS)
            vop("tensor_tensor", [oret, gr_s], [ymix], out=ymix[:, 512:1024], in0=oret[:].rearrange("p h d -> p (h d)"), in1=gr_s[:], op=ALU.mult)
            if os.environ.get('KS_DBGOUT', '0') == '1':
                dma(G, None, dbg_out[j * 128:(j + 1) * 128, :], ymix, ymix[:], "x")
            for kt in range(8):
                tr(ptr, ptr[:, kt * 128:(kt + 1) * 128], ymix, ymix[:, kt * 128:(kt + 1) * 128], ident, ident[:])
            vop("tensor_copy", [ptr], [yT], out=yT[:].rearrange("p k t -> p (k t)"), in_=ptr[:])
            ys = yst[0]
            for n in range(2):
                po = pin[n]
                for kt in range(8):
                    mm(po, po[:], yT, yT[:, kt, :], w_out_bf, w_out_bf[:, kt, n * 512:(n + 1) * 512], kt == 0, kt == 7)
                vop("tensor_tensor", [po, gate_bc], [ys], out=ys[:, n * 512:(n + 1) * 512], in0=po[:], in1=gate_bc[:, n * 512:(n + 1) * 512], op=ALU.mult)
                vop("tensor_tensor", [ys, x_t], [ys], out=ys[:, n * 512:(n + 1) * 512], in0=ys[:, n * 512:(n + 1) * 512], in1=x_t[:, n * 512:(n + 1) * 512], op=ALU.add)
            dma(S, None, y_out[j * 128:(j + 1) * 128, :], ys, ys[:], "oy%d" % (j % 2))


    try:
        for L in range(RUN_NT):
            _tile(L)
    except _Stop:
        pass

    P.emit(final=False)
    P.scope_pop()
    if sample and os.environ.get('KS_NOSAMPLE', '0') != '1':
        _sample_phase(locals())
    P.emit(final=True)
    _NC_CACHE['stats'] = P.stats
    _NC_CACHE['ops'] = [(o['eng'], o.get('waits', []), o.get('ev') if o['sig'] else None, o['dma']) for o in P.ops]
    es.close()
    return nc


def _sample_phase(E):
    nc, P = E["nc"], E["P"]
    sb, dma, vop, act, mm, tr, rms_heads = E["sb"], E["dma"], E["vop"], E["act"], E["mm"], E["tr"], E["rms_heads"]
    V, A, T, G, S = "dve", "act", "pe", "pool", "sp"
    w_in_bf, w_out_bf, ident, ident_f, ones_b = E["w_in_bf"], E["w_out_bf"], E["ident"], E["ident_f"], E["ones_b"]
    w1blk, w2blk, cbias, modA, modS, gvec, modT = E["w1blk"], E["w2blk"], E["cbias"], E["modA"], E["modS"], E["gvec"], E["modT"]
    ptr, pin, pst, pacc, sacc = E["ptr"], E["pin"], E["pst"], E["pacc"], E["paccw"]
    CG2 = "const2"
    f4 = lambda t: t[0:4]
    xs4 = sb("xs4", [4, D], F32); xs4b = sb("xs4b", [4, D], BF16)
    ss4 = sb("ss4", [4, 1], F32); rs4 = sb("rs4", [4, 1], F32)
    hTf = sb("hTf", [128, 8, 4], F32); hTs = sb("hTs", [128, 8, 4], BF16)
    sq4 = sb("sq4", [4, 512], F32); ssq4 = sb("ssq4", [4, 16], F32); rq4 = sb("rq4", [4, 16], F32)
    scr4 = (sq4, ssq4, rq4)
    qn_s = sb("qn_s", [4, 512], BF16)
    kvf_s = sb("kvf_s", [4, 512], F32); kwf_s = sb("kwf_s", [4, 280], F32)
    gates_s = sb("gates_s", [4, 24], BF16)
    ga4 = sb("ga4", [4, 512], BF16); gr4 = sb("gr4", [4, 512], BF16)
    rope_s = sb("rope_s", [4, 4, 32], F32)
    rot_s = sb("rot_s", [4, 2, 4, 64], F32); t1 = sb("t1s", [4, 2, 4, 32], F32); t2 = sb("t2s", [4, 2, 4, 32], F32)
    rot_b = sb("rot_b", [4, 4, 64], BF16)
    v_s = sb("v_s", [4, 4, 128], F32)
    qblk = sb("qblk", [128, 4, 8], BF16)
    gaT = sb("gaT", [128, 4, 4], F32)
    gbc = sb("gbc", [128, 4, 24], F32)
    oh4 = sb("oh4", [4, 4, 128], BF16); oh4f = sb("oh4f", [4, 4, 128], F32)
    ohq = sb("ohq", [64, 4, 4], F32)
    pcol = sb("pcol", [128, 6], F32)
    wfull = sb("wfull", [128, 8, 256], BF16)
    topc = sb("topc", [8, 2, 256], F32)
    Sst = sb("Sst", [64, 4, 4, 128], F32); Sbf = sb("Sbf", [64, 4, 4, 128], BF16)
    kT_s = sb("kT_s", [64, 4, 4], F32); qT2 = sb("qT2", [64, 4, 4], F32); qTm = sb("qTm", [64, 4, 4, 4], BF16)
    qk = sb("qk", [4, 4], F32); prod = sb("prod", [4, 4, 64], F32)
    ors = sb("ors", [4, 4, 128], F32); yr4 = sb("yr4", [4, 512], BF16)
    ymT = sb("ymT", [128, 8, 4], BF16)
    ys4 = sb("ys4", [4, D], F32)
    xT = [sb("xTc%d" % i, [128, 16, 1 + 31 * 8], BF16) for i in range(2)]
    hS = [sb("hS%d" % i, [128, 1024], BF16) for i in range(2)]
    pgt = [sb("pgt%d" % i, [128, 256], F32) for i in range(8)]
    pgb = [sb("pgb%d" % i, [128, 256], BF16) for i in range(2)]
    ptb = sb("ptb", [128, 128], I32); ptf = sb("ptf", [128, 128], F32); idxb = sb("idxb", [128, 128], I32)
    kcn_s = sb("kcn_s", [128, 128], F32); kcb_s = sb("kcb_s", [128, 128], BF16); kcT_s = sb("kcT_s", [128, 128], BF16)
    sqs = sb("sqs", [128, 128], F32); ssqs = sb("ssqs", [128, 4], F32); rqs = sb("rqs", [128, 4], F32)
    Vd = sb("Vd", [128, 2, 2, 64], BF16)
    E8 = sb("E8", [128, 8], BF16)
    wt = sb("wt", [128, 4, 256], F32); wtb = sb("wtb", [128, 4, 256], BF16)
    res = sb("res", [128, 256], F32)
    rdn = sb("rdn", [128, 3, 32], F32); cf = sb("cf", [128, 3, 32], F32); comb = sb("comb", [128, 32], F32); ctmp = sb("ctmp", [128, 32], F32)
    impT = sb("impT", [128, 2, 8], F32); atn = sb("atn", [128, 2, 32], F32)
    sc8 = sb("sc8", [8, 256], F32); sw8 = sb("sw8", [8, 256], F32); mx = sb("mx16", [8, 16], F32); th8 = sb("th8", [8, 1], F32)
    sv = sb("sv", [8, 16], F32)
    s128 = sb("s128", [128, 1], F32); h128 = sb("h128", [128, 1], F32); d128 = sb("d128", [128, 1], F32); i128 = sb("i128", [128, 1], I32)
    pg128 = sb("pg128", [128, 1], I32); pf128 = sb("pf128", [128, 1], F32)
    rbb = sb("rbb", [128, 8, 8], F32); gidx = sb("gidx", [128, 8, 8], I32)
    T_scr1 = TW(None, "scr1"); T_scr2 = TW(None, "scr2")
    scr1, scr2 = E["scr1"], E["scr2"]
    cache_rows = [E["cache_cmp"], E["cache_slc"]]

    dma(S, xs4, xs4[:], None, E["xs_in"], CG2)
    dma(S, rope_s, rope_s[:], None, E["c_rope_s"], CG2)
    dma(S, oh4f, oh4f[:], None, E["c_oh4"], CG2)
    dma(G, oh4, oh4[:], None, E["c_oh4"], CG2)
    dma(S, ohq, ohq[:], None, E["c_ohq"], CG2)
    dma(S, pcol, pcol[:], None, E["c_pcol"], CG2)
    dma(G, wfull, wfull[:], None, E["c_wfull"], CG2)
    dma(S, topc, topc[:], None, E["c_top"], CG2)
    for b in range(4):
        dma(S, Sst, Sst[:, b], None, E["sret_in"][b].rearrange("h k v -> k h v"), CG2)
    for b in range(4):
        P.op(S, (lambda b=b: nc.sync.dma_start(out=E["swin_out"][b, 0:511, :], in_=E["swin_in"][b, 1:512, :])), [], [], dma="d2d")

    gate_s = sb("gate_s", [4, D], F32)
    for n in range(2):
        po = pacc[n]
        for k4 in range(4):
            kt = n * 4 + k4
            P.op(T, (lambda po=po, kt=kt, k4=k4: nc.tensor.transpose(out=po[0:4, k4 * 128:(k4 + 1) * 128], in_=modT[:, 16 + kt, 1:5], identity=ident_f[:])), [modT, ident_f], [po])
        vop("tensor_copy", [po], [gate_s], out=gate_s[:, n * 512:(n + 1) * 512], in_=po[0:4, :])
    vop("memset", [], [ss4], eng=G, ap=ss4[:], constant=0.0)
    act(xs4b, xs4b[:], xs4, xs4[:], AF.Square, accum_out=ss4[:], extra_writes=[ss4])
    vop("tensor_scalar", [ss4], [rs4], out=rs4[:], in0=ss4[:], scalar1=1.0 / D, scalar2=EPS, op0=ALU.mult, op1=ALU.add)
    act(rs4, rs4[:], rs4, rs4[:], AF.Sqrt)
    vop("reciprocal", [rs4], [rs4], out=rs4[:], in_=rs4[:])
    P.op(A, lambda: nc.scalar.mul(out=xs4b[:], in_=xs4[:], mul=rs4[:, 0:1]), [xs4, rs4], [xs4b])
    for kt in range(8):
        tr(ptr, ptr[:, kt * 4:(kt + 1) * 4], xs4b, xs4b[:, kt * 128:(kt + 1) * 128], ident, ident[0:4, 0:4])
    vop("tensor_tensor", [ptr, modA], [hTf], out=hTf[:], in0=ptr[:, 0:32].rearrange("p (k t) -> p k t", k=8), in1=modA[:, :, 1:5], op=ALU.mult)
    vop("tensor_tensor", [hTf, modS], [hTs], out=hTs[:], in0=hTf[:], in1=modS[:, :, 1:5], op=ALU.add)

    def inproj(c0, cw, po):
        for kt in range(8):
            mm(po, po[0:4, 0:cw], hTs, hTs[:, kt, :], w_in_bf, w_in_bf[:, kt, c0:c0 + cw], kt == 0, kt == 7)

    po = pin[0]; inproj(O_Q, 512, po)
    rms_heads(po, po[0:4, :], 8, 64, gvec[0:4, 0:64], qn_s, qn_s[:], 64 * EPS, np_=4, scr=scr4)
    po = pin[1]; inproj(O_KC, 512, po)
    vop("tensor_copy", [po], [kvf_s], out=kvf_s[:], in_=po[0:4, :])
    rms_heads(kvf_s, kvf_s[:, 256:384], 2, 64, gvec[0:4, 128:192], kvf_s, kvf_s[:, 256:384], 64 * EPS, np_=4, scr=scr4)
    dma(S, None, E["scmp_out"], kvf_s, kvf_s[:, 0:256], "x")
    dma(S, None, E["sslc_out"], kvf_s, kvf_s[:, 256:512], "x")
    po = pin[0]; inproj(O_KW, 280, po)
    vop("tensor_copy", [po], [kwf_s], out=kwf_s[:], in_=po[0:4, 0:280])
    rms_heads(kwf_s, kwf_s[:, 0:128], 2, 64, gvec[0:4, 192:256], kwf_s, kwf_s[:, 0:128], 64 * EPS, np_=4, scr=scr4)
    act(gates_s, gates_s[:], kwf_s, kwf_s[:, 256:280], AF.Sigmoid)
    for b in range(4):
        dma(S, None, E["swin_out"][b, 511:512, :], kwf_s, kwf_s[b:b + 1, 0:256], "x")
    po = pin[1]; inproj(O_GA, 512, po)
    act(ga4, ga4[:], po, po[0:4, :], AF.Silu)
    po = pin[0]; inproj(O_GR, 512, po)
    act(gr4, gr4[:], po, po[0:4, :], AF.Silu)
    po = pin[1]; inproj(O_RV, 512, po)
    vop("tensor_copy", [po], [v_s], out=v_s[:].rearrange("p h d -> p (h d)"), in_=po[0:4, :])
    po = pin[0]; inproj(O_RQ, 512, po)
    for s_ in range(2):
        xv = po[0:4, s_ * 256:(s_ + 1) * 256].rearrange("p (h d) -> p h d", h=4)
        cs = rope_s[:, 2 * s_, :].unsqueeze(1).to_broadcast([4, 4, 32])
        sn = rope_s[:, 2 * s_ + 1, :].unsqueeze(1).to_broadcast([4, 4, 32])
        vop("tensor_tensor", [po, rope_s], [t1], out=t1[:, s_], in0=xv[:, :, 0:32], in1=cs, op=ALU.mult)
        vop("tensor_tensor", [po, rope_s], [t2], out=t2[:, s_], in0=xv[:, :, 32:64], in1=sn, op=ALU.mult)
        vop("tensor_tensor", [t1, t2], [rot_s], out=rot_s[:, s_, :, 0:32], in0=t1[:, s_], in1=t2[:, s_], op=ALU.subtract)
        vop("tensor_tensor", [po, rope_s], [t1], out=t1[:, s_], in0=xv[:, :, 0:32], in1=sn, op=ALU.mult)
        vop("tensor_tensor", [po, rope_s], [t2], out=t2[:, s_], in0=xv[:, :, 32:64], in1=cs, op=ALU.mult)
        vop("tensor_tensor", [t1, t2], [rot_s], out=rot_s[:, s_, :, 32:64], in0=t1[:, s_], in1=t2[:, s_], op=ALU.add)

    vop("memset", [], [qblk], eng=G, ap=qblk[:], constant=0.0)
    for i in range(4):
        for g in range(2):
            c0 = g * 256 + i * 64
            tr(ptr, ptr[g * 64:(g + 1) * 64, 64 + i * 4:64 + (i + 1) * 4], qn_s, qn_s[:, c0:c0 + 64], ident, ident[0:4, 0:4])
    qv = ptr[:, 64:80].rearrange("p (i b) -> p b i", i=4)
    vop("tensor_copy", [ptr], [qblk], out=qblk[0:64, :, 0:4], in_=qv[0:64])
    vop("tensor_copy", [ptr], [qblk], out=qblk[64:128, :, 4:8], in_=qv[64:128])
    for kt in range(4):
        tr(ptr, ptr[:, 96 + kt * 4:96 + (kt + 1) * 4], ga4, ga4[:, kt * 128:(kt + 1) * 128], ident, ident[0:4, 0:4])
    vop("tensor_copy", [ptr], [gaT], out=gaT[:].rearrange("p k b -> p (k b)"), in_=ptr[:, 96:112])
    po = pacc[0]
    for b in range(4):
        mm(po, po[:, b * 24:(b + 1) * 24], oh4, oh4[:, b, :], gates_s, gates_s[:], True, True)
    vop("tensor_copy", [po], [gbc], out=gbc[:].rearrange("p b t -> p (b t)"), in_=po[:, 0:96])

    vop("tensor_copy", [Sst], [Sbf], out=Sbf[:], in_=Sst[:])
    DBG3 = os.environ.get('KS_DBGOUT', '0') == '1'
    if DBG3:
        dma(S, None, E["dbg3_out"][0:64, 0:512], Sst, Sst[:, 0].rearrange("p h v -> p (h v)"), "x")
        dma(S, None, E["dbg3_out"][0:4, 512:1024], rot_s, rot_s[:].rearrange("p s h d -> p (s h d)"), "x")
        dma(S, None, E["dbg3_out"][0:4, 1024:1536], v_s, v_s[:].rearrange("p h d -> p (h d)"), "x")
    rot_b2 = sb("rot_b2", [4, 2, 4, 64], BF16)
    vop("tensor_copy", [rot_s], [rot_b2], out=rot_b2[:], in_=rot_s[:])
    for h in range(4):
        tr(ptr, ptr[0:64, 160 + h * 4:160 + (h + 1) * 4], rot_b2, rot_b2[:, 1, h, :], ident, ident[0:4, 0:4])
        tr(ptr, ptr[0:64, 176 + h * 4:176 + (h + 1) * 4], rot_b2, rot_b2[:, 0, h, :], ident, ident[0:4, 0:4])
    vop("tensor_copy", [ptr], [kT_s], out=kT_s[:].rearrange("p h b -> p (h b)"), in_=ptr[0:64, 160:176])
    vop("tensor_copy", [ptr], [qT2], out=qT2[:].rearrange("p h b -> p (h b)"), in_=ptr[0:64, 176:192])
    vop("tensor_tensor", [qT2, ohq], [qTm], out=qTm[:], in0=qT2[:].unsqueeze(1).to_broadcast([64, 4, 4, 4]),
        in1=ohq[:].unsqueeze(2).to_broadcast([64, 4, 4, 4]), op=ALU.mult)
    po = pst[0]
    first = True
    for h in range(4):
        for b in range(4):
            mm(po, po[0:4, h * 128:(h + 1) * 128], qTm, qTm[:, b, h, :], Sbf, Sbf[:, b, h, :], first, (h == 3 and b == 3), sgc=True)
            first = False
    if DBG3:
        dma(S, None, E["dbg3_out"][0:64, 1536:1552], kT_s, kT_s[:].rearrange("p h b -> p (h b)"), "x")
        dma(S, None, E["dbg3_out"][0:64, 1552:1568], qT2, qT2[:].rearrange("p h b -> p (h b)"), "x")
    vop("tensor_tensor", [rot_s], [prod], out=prod[:], in0=rot_s[:, 0], in1=rot_s[:, 1], op=ALU.mult)
    vop("tensor_reduce", [prod], [qk], out=qk[:], in_=prod[:], axis=AX.X, op=ALU.add)
    vop("tensor_tensor", [v_s, qk], [ors], out=ors[:], in0=v_s[:], in1=qk[:].unsqueeze(2).to_broadcast([4, 4, 128]), op=ALU.mult)
    for h in range(4):
        gam = float(1.0 - 2.0 ** (-5.0 - h))
        vop("scalar_tensor_tensor", [po, ors], [ors], out=ors[:, h, :], in0=po[0:4, h * 128:(h + 1) * 128], scalar=gam, in1=ors[:, h, :], op0=ALU.mult, op1=ALU.add)
    for b in range(4):
        pv = pst[1]
        P.op(T, (lambda b=b, pv=pv: nc.tensor.matmul(pv[0:64, :], lhsT=oh4f[:, b, 0:64], rhs=v_s[:].rearrange("p h d -> p (h d)"), start=True, stop=True)), [oh4f, v_s], [pv])
        for h in range(4):
            gam = float(1.0 - 2.0 ** (-5.0 - h))
            vop("tensor_scalar", [Sst], [Sst], out=Sst[:, b, h, :], in0=Sst[:, b, h, :], scalar1=gam, scalar2=None, op0=ALU.mult)
            vop("scalar_tensor_tensor", [pv, kT_s, Sst], [Sst], out=Sst[:, b, h, :], in0=pv[0:64, h * 128:(h + 1) * 128], scalar=kT_s[:, h, b:b + 1], in1=Sst[:, b, h, :], op0=ALU.mult, op1=ALU.add)
    for b in range(4):
        dma(S, None, E["sret_out"][b].rearrange("h k v -> k h v"), Sst, Sst[:, b], "x")
    rms_heads(ors, ors[:].rearrange("p h d -> p (h d)"), 4, 128, gvec[0:4, 256:384], ors, ors[:].rearrange("p h d -> p (h d)"), 128 * EPS, np_=4, scr=scr4)
    vop("tensor_tensor", [ors, gr4], [yr4], out=yr4[:], in0=ors[:].rearrange("p h d -> p (h d)"), in1=gr4[:], op=ALU.mult)
    for kt in range(4):
        tr(ptr, ptr[:, 128 + kt * 4:128 + (kt + 1) * 4], yr4, yr4[:, kt * 128:(kt + 1) * 128], ident, ident[0:4, 0:4])
    vop("tensor_copy", [ptr], [ymT], out=ymT[:, 4:8, :].rearrange("p k b -> p (k b)"), in_=ptr[:, 128:144])

    sfirst = [True]

    def smm(col, n, l_t, l_ap, r_t, r_ap):
        mm(sacc, sacc[:, col:col + n], l_t, l_ap, r_t, r_ap, sfirst[0], False, sgc=True)
        sfirst[0] = False

    OC = lambda br: br * 64

    def key_tile(br, b, kT_t, kT_ap, vsrc_t, vsrc_ap, cols, bias_ap, po):
        mm(po, po[:, 0:8], kT_t, kT_ap, qblk, qblk[:, b, :], True, True)
        if bias_ap is None:
            act(E8, E8[:], po, po[:, 0:8], AF.Exp)
        else:
            act(E8, E8[:], po, po[:, 0:8], AF.Exp, extra_reads=[pcol], bias=bias_ap)
        vop("tensor_copy", [vsrc_t], [Vd], out=Vd[:], in_=vsrc_ap.rearrange("p (g d) -> p g d", g=2).unsqueeze(2).to_broadcast([128, 2, 2, 64]))
        for g in cols:
            smm(OC(br) + b * 8 + g * 4, 4, Vd, Vd[:, g].rearrange("p r d -> p (r d)"), E8, E8[:, g * 4:(g + 1) * 4])
            smm(OC(br) + 32 + b * 8 + g * 4, 4, ones_b, ones_b[:], E8, E8[:, g * 4:(g + 1) * 4])

    for b in range(4):
        dma(S, wt, wt[:], None, E["swin_in"][b].rearrange("(t p) c -> p t c", p=128), "x")
        dma(S, wt, wt[0:1, 0, :], kwf_s, kwf_s[b:b + 1, 0:256], "x")
        for t in range(4):
            pp = pst[t % 2]
            P.op(T, (lambda pp=pp, t=t: nc.tensor.transpose(out=pp[:, 0:128], in_=wt[:, t, 0:128], identity=ident_f[:])), [wt, ident_f], [pp])
            vop("tensor_copy", [pp], [kcT_s], out=kcT_s[:], in_=pp[:, 0:128])
            key_tile(2, b, kcT_s, kcT_s[:], wt, wt[:, t, 128:256], (0, 1), None, pin[t % 2])

    CH = [31, 31, 31, 31, 4]
    npg = [0]
    idx_all = sb("idx_all", [128, 4, 128], I32)
    for b in range(4):
        dma(S, ptb, ptb[:], None, E["ptc"][b:b + 1, :].to_broadcast([128, 128]), "x")
        vop("tensor_copy", [ptb], [ptf], out=ptf[:], in_=ptb[:])
        vop("tensor_scalar", [ptf], [ptf], out=ptf[:], in0=ptf[:], scalar1=128.0, scalar2=None, op0=ALU.mult)
        vop("tensor_tensor", [ptf, pcol], [ptf], out=ptf[:], in0=ptf[:], in1=pcol[:, 0:1].to_broadcast([128, 128]), op=ALU.add)
        vop("tensor_copy", [ptf], [idx_all], out=idx_all[:, b, :], in_=ptf[:])
    PF = 6
    pages = [(b, j) for b in range(4) for j in range(128)]
    issued = []

    def issue_next():
        k = len(issued)
        if k >= len(pages):
            return
        b_, j_ = pages[k]
        pt_ = pgt[k % 8]
        P.op(G, (lambda pt_=pt_, b_=b_, j_=j_: nc.gpsimd.indirect_dma_start(out=pt_[:], out_offset=None, in_=cache_rows[0],
             in_offset=bass.IndirectOffsetOnAxis(ap=idx_all[:, b_, j_:j_ + 1], axis=0))), [idx_all], [pt_], dma=pt_.name)
        issued.append(pt_)

    for _ in range(PF):
        issue_next()
    kpage = 0
    for b in range(4):
        for i in range(2):
            vop("memset", [], [xT[i]], eng=G, ap=xT[i][:, :, 0:1], constant=0.0)
        j0 = 0
        for c, npages in enumerate(CH):
            for jj in range(npages):
                pt_ = issued[kpage]
                issue_next()
                pp = pst[kpage % 2]
                kpage += 1
                for i in range(2):
                    P.op(T, (lambda pp=pp, pt_=pt_, i=i: nc.tensor.transpose(out=pp[:, i * 128:(i + 1) * 128], in_=pt_[:, i * 128:(i + 1) * 128], identity=ident_f[:])), [pt_, ident_f], [pp])
                for i in range(2):
                    vop("tensor_copy", [pp], [xT[i]], out=xT[i][:, :, 1 + jj * 8:1 + (jj + 1) * 8], in_=pp[:, i * 128:(i + 1) * 128].rearrange("p (m r) -> p r m", r=16))
            R0 = j0 * 128
            nblk = npages * 8
            nb0 = R0 // 16 - 1
            c_lo = 1 if c == 0 else 0
            for i in range(2):
                po = pin[i]
                nn = nblk - c_lo
                for l in range(32):
                    a_, r_ = l // 16, l % 16
                    mm(po, po[:, 0:nn], w1blk[i], w1blk[i][:, l, :], xT[i], xT[i][:, r_, c_lo + a_:c_lo + a_ + nn], l == 0, l == 31)
                act(hS[i], hS[i][:, nb0 + c_lo:nb0 + nblk], po, po[:, 0:nn], AF.Silu, extra_reads=[cbias[i]], bias=cbias[i][:, 0:1])
                vop("tensor_copy", [xT[i]], [xT[i]], out=xT[i][:, :, 0:1], in_=xT[i][:, :, npages * 8:npages * 8 + 1])
            j0 += npages
        for i in range(2):
            vop("memset", [], [hS[i]], eng=G, ap=hS[i][:, 1023:1024], constant=0.0)
        for nt in range(8):
            po = pin[nt % 2]
            mm(po, po[:, 0:128], hS[0], hS[0][:, nt * 128:(nt + 1) * 128], w2blk[0], w2blk[0][:], True, True)
            mm(po, po[:, 128:256], hS[1], hS[1][:, nt * 128:(nt + 1) * 128], w2blk[1], w2blk[1][:], True, True)
            vop("tensor_copy", [po], [kcn_s], out=kcn_s[:], in_=po[:, 0:128])
            rms_heads(kcn_s, kcn_s[:], 2, 64, gvec[:, 64:128], kcb_s, kcb_s[:], 64 * EPS, np_=128, scr=(sqs, ssqs, rqs))
            tr(ptr, ptr[:, 768:896], kcb_s, kcb_s[:], ident, ident[:])
            vop("tensor_copy", [ptr], [kcT_s], out=kcT_s[:], in_=ptr[:, 768:896])
            ps2 = pst[nt % 2]
            mm(ps2, ps2[:, 0:8], kcT_s, kcT_s[:], qblk, qblk[:, b, :], True, True)
            if nt == 7:
                act(E8, E8[:], ps2, ps2[:, 0:8], AF.Exp, extra_reads=[pcol], bias=pcol[:, 4:5])
            else:
                act(E8, E8[:], ps2, ps2[:, 0:8], AF.Exp)
            vop("tensor_copy", [po], [Vd], out=Vd[:], in_=po[:, 128:256].rearrange("p (g d) -> p g d", g=2).unsqueeze(2).to_broadcast([128, 2, 2, 64]))
            for g in range(2):
                smm(OC(0) + b * 8 + g * 4, 4, Vd, Vd[:, g].rearrange("p r d -> p (r d)"), E8, E8[:, g * 4:(g + 1) * 4])
            smm(OC(0) + 32 + b * 8, 8, ones_b, ones_b[:], E8, E8[:])
            for st in range(2):
                smm(192 + st * 32 + b * 8, 8, wfull, wfull[:, nt, st * 128:(st + 1) * 128], E8, E8[:])

    vop("tensor_copy", [sacc], [res], out=res[:], in_=sacc[:, 0:256])
    vop("reciprocal", [res], [rdn], out=rdn[:, 0, :], in_=res[:, 32:64])
    vop("tensor_tensor", [res, rdn], [atn], out=atn[:], in0=res[:, 192:256].rearrange("p (s c) -> p s c", s=2),
        in1=rdn[:, 0, :].unsqueeze(1).to_broadcast([128, 2, 32]), op=ALU.mult)
    vop("tensor_reduce", [atn], [impT], out=impT[:].rearrange("p s c -> p (s c)"), in_=atn[:].rearrange("p s (c i) -> p (s c) i", i=4), axis=AX.X, op=ALU.add)
    po = pacc[0]
    for st in range(2):
        P.op(T, (lambda st=st, po=po: nc.tensor.transpose(out=po[0:8, st * 128:(st + 1) * 128], in_=impT[:, st, :], identity=ident_f[:])), [impT, ident_f], [po])
    vop("tensor_tensor", [po, topc], [sc8], out=sc8[:], in0=po[0:8, 0:256], in1=topc[:, 0, :], op=ALU.add)
    vop("max", [sc8], [mx], out=mx[:, 0:8], in_=sc8[:])
    vop("match_replace", [mx, sc8], [sw8], out=sw8[:], in_to_replace=mx[:, 0:8], in_values=sc8[:], imm_value=-3e30)
    vop("max", [sw8], [mx], out=mx[:, 8:16], in_=sw8[:])
    vop("tensor_scalar", [sc8, mx], [sw8], out=sw8[:], in0=sc8[:], scalar1=mx[:, 14:15], scalar2=None, op0=ALU.is_ge)
    vop("tensor_tensor", [sw8, topc], [sw8], out=sw8[:], in0=sw8[:], in1=topc[:, 1, :], op=ALU.mult)
    vop("max", [sw8], [sv], out=sv[:, 0:8], in_=sw8[:])
    vop("match_replace", [sv, sw8], [sc8], out=sc8[:], in_to_replace=sv[:, 0:8], in_values=sw8[:], imm_value=0.0)
    vop("max", [sc8], [sv], out=sv[:, 8:16], in_=sc8[:])
    vop("tensor_scalar", [sv], [sv], out=sv[:], in0=sv[:], scalar1=-1.0, scalar2=0.0, op0=ALU.add, op1=ALU.max)
    P.op(S, lambda: nc.sync.dma_start(out=scr1.rearrange("(a k) o -> a (k o)", a=8), in_=sv[:]), [sv], [T_scr1], dma="scr1")
    P.op(S, lambda: nc.sync.dma_start(out=s128[:], in_=scr1), [T_scr1], [s128], dma="s128")
    vop("tensor_scalar", [s128], [d128], out=d128[:], in0=s128[:], scalar1=0.5, scalar2=-0.25, op0=ALU.mult, op1=ALU.add)
    vop("tensor_scalar", [d128], [d128], out=d128[:], in0=d128[:], scalar1=8388608.0, scalar2=None, op0=ALU.add)
    vop("tensor_scalar", [d128], [d128], out=d128[:], in0=d128[:], scalar1=-8388608.0, scalar2=None, op0=ALU.add)
    vop("tensor_scalar", [d128], [h128], out=h128[:], in0=d128[:], scalar1=-2.0, scalar2=None, op0=ALU.mult)
    vop("tensor_tensor", [s128, h128], [h128], out=h128[:], in0=s128[:], in1=h128[:], op=ALU.add)
    vop("tensor_tensor", [d128, pcol], [d128], out=d128[:], in0=d128[:], in1=pcol[:, 2:3], op=ALU.add)
    vop("tensor_copy", [d128], [i128], out=i128[:], in_=d128[:])
    P.op(G, lambda: nc.gpsimd.indirect_dma_start(out=pg128[:], out_offset=None, in_=E["ptc"].rearrange("b (j o) -> (b j) o", o=1),
         in_offset=bass.IndirectOffsetOnAxis(ap=i128[:, 0:1], axis=0)), [i128], [pg128], dma="pg128")
    vop("tensor_copy", [pg128], [pf128], out=pf128[:], in_=pg128[:])
    vop("tensor_scalar", [pf128], [pf128], out=pf128[:], in0=pf128[:], scalar1=128.0, scalar2=None, op0=ALU.mult)
    vop("tensor_scalar", [h128], [h128], out=h128[:], in0=h128[:], scalar1=64.0, scalar2=None, op0=ALU.mult)
    vop("tensor_tensor", [pf128, h128], [pf128], out=pf128[:], in0=pf128[:], in1=h128[:], op=ALU.add)
    P.op(S, lambda: nc.sync.dma_start(out=scr2, in_=pf128[:]), [pf128], [T_scr2], dma="scr2")
    s2v = scr2.rearrange("(a t h) o -> h a (t o)", a=8, t=8, h=2)
    for hh in range(2):
        P.op(S, (lambda hh=hh: nc.sync.dma_start(out=rbb[hh * 64:(hh + 1) * 64], in_=s2v[hh:hh + 1].to_broadcast([64, 8, 8]), allow_slow_non_contiguous=True)), [T_scr2], [rbb], dma="rbb%d" % hh)
    vop("tensor_tensor", [rbb, pcol], [rbb], out=rbb[:], in0=rbb[:], in1=pcol[:, 1:2].unsqueeze(2).to_broadcast([128, 8, 8]), op=ALU.add)
    vop("tensor_copy", [rbb], [gidx], out=gidx[:], in_=rbb[:])

    tiles = [(b, g, t) for b in range(4) for g in range(2) for t in range(8)]
    sissued = []

    def sissue_next():
        k = len(sissued)
        if k >= len(tiles):
            return
        b_, g_, t_ = tiles[k]
        pt_ = pgt[k % 8]
        P.op(G, (lambda pt_=pt_, bg=b_ * 2 + g_, t_=t_: nc.gpsimd.indirect_dma_start(out=pt_[:], out_offset=None, in_=cache_rows[1],
             in_offset=bass.IndirectOffsetOnAxis(ap=gidx[:, bg, t_:t_ + 1], axis=0))), [gidx], [pt_], dma=pt_.name)
        if t_ == 7:
            dma(S, pt_, pt_[64:65, :], kvf_s, kvf_s[b_:b_ + 1, 256:512], "x")
        sissued.append(pt_)

    for _ in range(PF):
        sissue_next()
    for k, (b, g, t) in enumerate(tiles):
        pt_ = sissued[k]
        sissue_next()
        pp = pst[k % 2]
        P.op(T, (lambda pp=pp, pt_=pt_: nc.tensor.transpose(out=pp[:, 0:128], in_=pt_[:, 0:128], identity=ident_f[:])), [pt_, ident_f], [pp])
        vop("tensor_copy", [pp], [kcT_s], out=kcT_s[:], in_=pp[:, 0:128])
        key_tile(1, b, kcT_s, kcT_s[:], pt_, pt_[:, 128:256], (g,), (pcol[:, 3:4] if t == 7 else None), pin[k % 2])

    vop("tensor_copy", [sacc], [res], out=res[:, 0:192], in_=sacc[:, 0:192])
    for br in range(3):
        vop("reciprocal", [res], [rdn], out=rdn[:, br, :], in_=res[:, br * 64 + 32:br * 64 + 64])
    vop("tensor_tensor", [rdn, gbc], [cf], out=cf[:].rearrange("p r (b c) -> p r b c", b=4), in0=rdn[:].rearrange("p r (b c) -> p r b c", b=4),
        in1=gbc[:].rearrange("p b (c r) -> p r b c", r=3), op=ALU.mult)
    vop("tensor_tensor", [res, cf], [comb], out=comb[:], in0=res[:, 0:32], in1=cf[:, 0, :], op=ALU.mult)
    for br in (1, 2):
        vop("tensor_tensor", [res, cf], [ctmp], out=ctmp[:], in0=res[:, br * 64:br * 64 + 32], in1=cf[:, br, :], op=ALU.mult)
        vop("tensor_tensor", [comb, ctmp], [comb], out=comb[:], in0=comb[:], in1=ctmp[:], op=ALU.add)
    cv = comb[:].rearrange("p (b g i r) -> p r g i b", b=4, g=2, i=2, r=2)
    for hf in range(2):
        ps_ = slice(hf * 64, (hf + 1) * 64)
        vop("tensor_tensor", [comb, gaT], [ymT], out=ymT[ps_, 0:4, :].rearrange("p (g i) b -> p g i b", g=2), in0=cv[ps_, hf],
            in1=gaT[ps_, :, :].rearrange("p (g i) b -> p g i b", g=2), op=ALU.mult)
    if os.environ.get('KS_DBGOUT', '0') == '1':
        dma(G, None, E["dbg2_out"][:, 0:32], ymT, ymT[:].rearrange("p k b -> p (k b)"), "x")
        dma(S, None, E["dbg2_out"][:, 32:64], comb, comb[:], "x")
    for n in range(2):
        po = pin[n]
        for kt in range(8):
            mm(po, po[0:4, :], ymT, ymT[:, kt, :], w_out_bf, w_out_bf[:, kt, n * 512:(n + 1) * 512], kt == 0, kt == 7)
        vop("tensor_tensor", [po, gate_s], [ys4], out=ys4[:, n * 512:(n + 1) * 512], in0=po[0:4, :], in1=gate_s[:, n * 512:(n + 1) * 512], op=ALU.mult)
        vop("tensor_tensor", [ys4, xs4], [ys4], out=ys4[:, n * 512:(n + 1) * 512], in0=ys4[:, n * 512:(n + 1) * 512], in1=xs4[:, n * 512:(n + 1) * 512], op=ALU.add)
    dma(S, None, E["ys_out"], ys4, ys4[:], "x")


def _consts(p):
    sh = 1 - p
    c = {}
    c["c_ident"] = np.eye(128, dtype=np.float32)
    n = np.arange(NT * 128)
    c["c_expand"] = (n[None, :] // 64 == np.arange(64)[:, None]).astype(np.float32)
    kk = np.arange(128)[:, None]
    qq = np.arange(128)[None, :]
    m3 = np.zeros((128, 3, 128), np.float32)
    m3[:, 0, :] = np.where(kk <= qq, 0.0, NEGB)
    m3[:, 1, :] = np.where(kk > qq, 0.0, NEGB)
    m3[:, 2, :] = NEGB if p == 0 else 0.0
    c["c_mask3"] = m3
    mc = np.zeros((128, 16, 2, 128), np.float32)
    for j in range(16):
        L = 2 * j + 1
        t = L * 128 + np.arange(128)[None, :]
        for nt in range(2):
            nn = nt * 128 + np.arange(128)[:, None]
            ok = (16 * nn + 31 <= t) & (nn >= 8 * sh)
            mc[:, j, nt, :] = np.where(ok, 0.0, NEGB)
    c["c_maskc"] = mc
    ad = np.zeros((128, 16, 64), np.float32)
    for j in range(16):
        L = 2 * j + 1
        t = L * 128 + np.arange(128)[:, None]
        s = np.arange(64)[None, :]
        jt = t // 64
        valid = (s * 64 <= t) & (s >= 2 * sh)
        forced = (s == 2 * sh) | (s == jt) | (s == jt - 1)
        ad[:, j, :] = np.where(valid, np.where(forced, 1e4, 0.0), -1e30)
    c["c_adds"] = ad
    half = 32
    freqs = (10000.0 ** (-np.arange(half, dtype=np.float32) / half)).astype(np.float32)
    rp = np.zeros((128, NT, 4, 32), np.float32)
    rs = np.zeros((128, NT, 8), np.float32)
    lg = np.log(1.0 - 2.0 ** (-5.0 - np.arange(4, dtype=np.float32))).astype(np.float32)
    i = np.arange(128, dtype=np.float32)
    for L in range(NT):
        gt = L - sh
        if gt < 0:
            continue
        pos = (gt * 128 + np.arange(128)).astype(np.float32)
        ang = pos[:, None] * freqs[None, :]
        rp[:, L, 0] = np.cos(ang); rp[:, L, 1] = np.sin(ang)
        rp[:, L, 2] = np.cos(ang) / 8.0; rp[:, L, 3] = np.sin(ang) / 8.0
        rs[:, L, 0:4] = np.exp(lg[None, :] * (127.0 - i[:, None]))
        rs[:, L, 4:8] = np.exp(lg[None, :] * (i[:, None] + 1.0))
    c["c_rope"] = rp
    c["c_rscal"] = rs
    w = np.zeros((255 + 1, 64), np.float32)
    for s in range(64):
        for nn, wt in ((4 * s - 1, 1), (4 * s, 2), (4 * s + 1, 2), (4 * s + 2, 2), (4 * s + 3, 1)):
            if 0 <= nn < 255:
                w[nn, s] += wt
    wl = np.zeros((256, 64), np.float32)
    for nn in range(255):
        for s in range(64):
            if w[nn, s] and nn + 8 * sh < 256 and s + 2 * sh < 64:
                wl[nn + 8 * sh, s + 2 * sh] = w[nn, s]
    c["c_wimp"] = wl.reshape(2, 128, 64).transpose(1, 0, 2).copy()
    jj = np.arange(128)[:, None]; ii = np.arange(128)[None, :]
    dm = np.zeros((128, 4, 128), np.float32)
    for h in range(4):
        dm[:, h, :] = np.where(ii >= jj, np.exp(lg[h] * np.maximum(ii - jj, 0)), 0.0)
    c["c_dmask"] = dm
    return {k: np.ascontiguousarray(v, dtype=np.float32) for k, v in c.items()}


def _sconsts():
    c = {}
    half = 32
    freqs = (10000.0 ** (-np.arange(half, dtype=np.float32) / half)).astype(np.float32)
    ang = np.float32(16384.0) * freqs
    r = np.stack([np.cos(ang), np.sin(ang), np.cos(ang) / 8.0, np.sin(ang) / 8.0]).astype(np.float32)
    c["c_rope_s"] = np.broadcast_to(r[None], (4, 4, 32)).copy()
    w = np.zeros((1024, 256), np.float32)
    for s_ in range(256):
        for nn, wt in ((4 * s_ - 1, 1), (4 * s_, 2), (4 * s_ + 1, 2), (4 * s_ + 2, 2), (4 * s_ + 3, 1)):
            if 0 <= nn < 1023:
                w[nn, s_] += wt
    c["c_wfull"] = w.reshape(8, 128, 256).transpose(1, 0, 2).copy()
    top = np.zeros((8, 2, 256), np.float32)
    top[:, 0, 0] = 1e4
    top[:, 0, 255] = 1e4
    top[:, 1, :] = np.arange(256, dtype=np.float32)[None, :] + 1.0
    c["c_top"] = top
    p = np.arange(128)
    pc = np.zeros((128, 6), np.float32)
    pc[:, 0] = p
    pc[:, 1] = p % 64
    pc[:, 2] = (p // 32) * 128
    pc[:, 3] = np.where(p <= 64, 0.0, NEGB)
    pc[:, 4] = np.where(p == 127, NEGB, 0.0)
    c["c_pcol"] = pc
    oh = np.zeros((4, 4, 128), np.float32)
    for b in range(4):
        oh[b, b, :] = 1.0
    c["c_oh4"] = oh
    c["c_ohq"] = np.broadcast_to(np.eye(4, dtype=np.float32)[None], (64, 4, 4)).copy()
    return {k: np.ascontiguousarray(v, dtype=np.float32) for k, v in c.items()}


_NC_CACHE = {}


def kernel(x_prompt, x_sample, c_prompt, c_sample, cache_cmp, cache_slc, state_win, state_ret, page_table,
           g_norm, w_ada, b_ada, w_in, g_q, g_kc, g_ks, g_kw, pe_ck, w_ck1, w_ck2, pe_cv, w_cv1, w_cv2,
           g_ret, w_out):
    f = lambda a: np.ascontiguousarray(np.asarray(a), dtype=np.float32)
    x_prompt = f(x_prompt)
    if "nc" not in _NC_CACHE:
        _NC_CACHE["nc"] = build_program()
    nc = _NC_CACHE["nc"]
    shared = dict(w_ada=f(w_ada)[0], b_ada=f(b_ada), w_in=f(w_in)[0], w_out=f(w_out)[0], g_norm=f(g_norm),
                  g_q=f(g_q), g_kc=f(g_kc), g_ks=f(g_ks), g_kw=f(g_kw), g_ret=f(g_ret),
                  pe_ck=f(pe_ck)[0], pe_cv=f(pe_cv)[0], w_ck1=f(w_ck1)[0], w_cv1=f(w_cv1)[0],
                  w_ck2=f(w_ck2)[0], w_cv2=f(w_cv2)[0])
    consts = [_consts(0), _consts(1)]
    sconst = _sconsts()
    cc = f(cache_cmp).reshape(-1, 256)
    cs_ = f(cache_slc).reshape(-1, 256)
    in_maps = []
    for c in range(8):
        b, p = c // 2, c % 2
        if p == 0:
            xl = np.concatenate([np.zeros((128, D), np.float32), x_prompt[b, :31 * 128]], axis=0)
        else:
            xl = x_prompt[b]
        cv = np.concatenate([f(c_prompt)[b:b + 1], f(c_sample)[4 * c:4 * c + 4]], axis=0)
        m = dict(shared)
        m.update(consts[p])
        m["xloc"] = np.ascontiguousarray(xl)
        m["cvec"] = np.ascontiguousarray(cv)
        m.update(sconst)
        m["xs_in"] = np.ascontiguousarray(f(x_sample)[4 * c:4 * c + 4, 0])
        m["ptc"] = np.ascontiguousarray(np.asarray(page_table)[4 * c:4 * c + 4].astype(np.int32))
        m["cache_cmp"] = cc
        m["cache_slc"] = cs_
        m["swin_in"] = np.ascontiguousarray(f(state_win)[0, 4 * c:4 * c + 4].reshape(4, 512, 256))
        m["sret_in"] = np.ascontiguousarray(f(state_ret)[0, 4 * c:4 * c + 4])
        in_maps.append(m)
    res = run_bass_kernel_spmd(nc, in_maps, core_ids=list(range(8)))
    R = res.results
    y_prompt = np.zeros((4, 4096, D), np.float32)
    p_cmp = np.zeros((1, 4, 4096, 256), np.float32)
    p_slc = np.zeros((1, 4, 4096, 256), np.float32)
    p_win = np.zeros((1, 4, 512, 256), np.float32)
    p_ret = np.zeros((1, 4, 4, 64, 128), np.float32)
    for c in range(8):
        b, p = c // 2, c % 2
        yo = R[c]["y_out"].reshape(16, 128, D)
        pc = R[c]["pcmp_out"].reshape(16, 128, 256)
        pl = R[c]["pslc_out"].reshape(16, 128, 256)
        pw = R[c]["pwin_out"].reshape(2, 128, 256)
        for j in range(16):
            gt = 2 * j + p
            y_prompt[b, gt * 128:(gt + 1) * 128] = yo[j]
            p_cmp[0, b, gt * 128:(gt + 1) * 128] = pc[j]
            p_slc[0, b, gt * 128:(gt + 1) * 128] = pl[j]
        for k in range(2):
            gt = 28 + 2 * k + p
            p_win[0, b, (gt - 28) * 128:(gt - 27) * 128] = pw[k]
        if p == 1:
            p_ret[0, b] = R[c]["pret_out"].transpose(1, 0, 2)
    y_sample = np.concatenate([R[c]["ys_out"] for c in range(8)], axis=0).reshape(32, 1, D)
    s_cmp = np.concatenate([R[c]["scmp_out"] for c in range(8)], axis=0).reshape(1, 32, 1, 2, 2, 64)
    s_slc = np.concatenate([R[c]["sslc_out"] for c in range(8)], axis=0).reshape(1, 32, 1, 2, 2, 64)
    s_win = np.concatenate([R[c]["swin_out"] for c in range(8)], axis=0).reshape(1, 32, 512, 2, 2, 64)
    s_ret = np.concatenate([R[c]["sret_out"] for c in range(8)], axis=0).reshape(1, 32, 4, 64, 128)
    outs = (y_prompt, y_sample, p_cmp.reshape(1, 4, 4096, 2, 2, 64), p_slc.reshape(1, 4, 4096, 2, 2, 64),
            p_win.reshape(1, 4, 512, 2, 2, 64), p_ret, s_cmp, s_slc, s_win, s_ret)
    return outs
```
